# Optimizing a Trainium2 kernel written in Bass

```python
import math
import jax, jax.numpy as jnp
from jax import lax
import numpy as np

D_MODEL = 1024
BATCH = 16
SEQ = 2048
DEPTH = 4

MEM_LEN = 256
ROPE_THETA = 500000.0
Q_BLOCK = 128
MLA_HEADS = 8
MLA_NOPE = 64
MLA_ROPE = 32
MLA_V = 64
MLA_Q_LORA = 384
MLA_KV_LORA = 256
DIFF_HEADS = 8
DIFF_D = 64
DIFF_ROT = DIFF_D // 4
MEM_HEADS = 4
MEM_HD = 128
N_BRANCH = 3
D_FF = 4 * D_MODEL
DEEPNORM_ALPHA = (2 * DEPTH) ** 0.25
DEEPNORM_BETA = (8 * DEPTH) ** -0.25
LN_EPS = 1e-5

MLA_OUT = MLA_HEADS * MLA_V
DIFF_QK = DIFF_HEADS * 2 * DIFF_D
DIFF_OUT = DIFF_HEADS * 2 * DIFF_D
MEM_OUT = MEM_HEADS * MEM_HD
BRANCH_W = MLA_OUT + DIFF_OUT + MEM_OUT
IN_SIZES = (MLA_Q_LORA, MLA_KV_LORA, MLA_ROPE, DIFF_QK, DIFF_QK, DIFF_OUT, MEM_OUT, N_BRANCH * D_MODEL)
N_IN = sum(IN_SIZES)

kernel_name = 'hybrid_mla_diffattn_memory_encoder'


def _split_points():
    pts, acc = [], 0
    for s in IN_SIZES[:-1]:
        acc += s
        pts.append(acc)
    return pts


def _rms(x, g, eps=1e-6):
    xf = x.astype(jnp.float32)
    y = xf * lax.rsqrt(jnp.mean(xf * xf, axis=-1, keepdims=True) + eps)
    return (y * g.astype(jnp.float32)).astype(x.dtype)


def _layernorm(x, g, b):
    xf = x.astype(jnp.float32)
    mu = jnp.mean(xf, axis=-1, keepdims=True)
    var = jnp.mean(jnp.square(xf - mu), axis=-1, keepdims=True)
    y = (xf - mu) * lax.rsqrt(var + LN_EPS) * g.astype(jnp.float32) + b.astype(jnp.float32)
    return y.astype(x.dtype)


def _rope_tables(positions, rot_dim, dtype):
    inv = ROPE_THETA ** (-jnp.arange(0, rot_dim, 2, dtype=jnp.float32) / rot_dim)
    ang = positions.astype(jnp.float32)[..., None] * inv
    return jnp.cos(ang).astype(dtype), jnp.sin(ang).astype(dtype)


def _apply_rope(x, cos, sin):
    half = cos.shape[-1]
    shape = cos.shape[:2] + (1,) * (x.ndim - 3) + (half,)
    c, s = cos.reshape(shape), sin.reshape(shape)
    x1, x2, rest = x[..., :half], x[..., half:2 * half], x[..., 2 * half:]
    return jnp.concatenate([x1 * c - x2 * s, x2 * c + x1 * s, rest], axis=-1)


def _sweep_query_blocks(fn, q):
    b, s = q.shape[0], q.shape[1]
    nb = s // Q_BLOCK
    qb = jnp.moveaxis(q.reshape((b, nb, Q_BLOCK) + q.shape[2:]), 1, 0)
    out = jnp.moveaxis(lax.map(fn, qb), 0, 1)
    return out.reshape((b, s) + out.shape[3:])


def _mla_branch(c_q, c_kv, k_pe, q_norm, kv_norm, w_uq, w_ukv, cos, sin):
    b, s, _ = c_q.shape
    q = (_rms(c_q, q_norm) @ w_uq).reshape(b, s, MLA_HEADS, MLA_NOPE + MLA_ROPE)
    q = jnp.concatenate([q[..., :MLA_NOPE], _apply_rope(q[..., MLA_NOPE:], cos, sin)], axis=-1)
    kv = (_rms(c_kv, kv_norm) @ w_ukv).reshape(b, s, MLA_HEADS, MLA_NOPE + MLA_V)
    k_nope, v = kv[..., :MLA_NOPE], kv[..., MLA_NOPE:]
    k_pe = _apply_rope(k_pe, cos, sin)
    scale = (MLA_NOPE + MLA_ROPE) ** -0.5

    def attend(qblk):
        qn, qp = qblk[..., :MLA_NOPE], qblk[..., MLA_NOPE:]
        sc = jnp.einsum('bqhn,bkhn->bhqk', qn, k_nope) + jnp.einsum('bqhr,bkr->bhqk', qp, k_pe)
        p = jax.nn.softmax(sc.astype(jnp.float32) * scale, axis=-1).astype(v.dtype)
        return jnp.einsum('bhqk,bkhv->bqhv', p, v)

    return _sweep_query_blocks(attend, q).reshape(b, s, MLA_OUT)


def _diff_branch(q, k, v, lam, subln, lambda_init, cos, sin):
    b, s, _ = q.shape
    q = _apply_rope(q.reshape(b, s, DIFF_HEADS, 2, DIFF_D), cos, sin)
    k = _apply_rope(k.reshape(b, s, DIFF_HEADS, 2, DIFF_D), cos, sin)
    v = v.reshape(b, s, DIFF_HEADS, 2 * DIFF_D)
    lf = lam.astype(jnp.float32)
    lambda_full = jnp.exp(jnp.sum(lf[0] * lf[1])) - jnp.exp(jnp.sum(lf[2] * lf[3])) + lambda_init
    scale = DIFF_D ** -0.5

    def attend(qblk):
        sc = jnp.einsum('bqhcd,bkhcd->bhcqk', qblk, k).astype(jnp.float32) * scale
        p = jax.nn.softmax(sc, axis=-1)
        w = (p[:, :, 0] - lambda_full * p[:, :, 1]).astype(v.dtype)
        return jnp.einsum('bhqk,bkhe->bqhe', w, v)

    o = _sweep_query_blocks(attend, q)
    o = _rms(o, subln, 1e-5) * (1.0 - lambda_init)
    return o.reshape(b, s, DIFF_OUT)


def _mem_branch(q, mem, w_kv):
    b, s, _ = q.shape
    q = q.reshape(b, s, MEM_HEADS, MEM_HD)
    kv = (mem @ w_kv).reshape(b, mem.shape[1], 2, MEM_HEADS, MEM_HD)
    k, v = kv[:, :, 0], kv[:, :, 1]
    sc = jnp.einsum('bqhd,bmhd->bhqm', q, k).astype(jnp.float32) * (MEM_HD ** -0.5)
    p = jax.nn.softmax(sc, axis=-1).astype(v.dtype)
    return jnp.einsum('bhqm,bmhd->bqhd', p, v).reshape(b, s, MEM_OUT)


def setup_inputs(seed: int = 0) -> dict:
    key = jax.random.key(seed)
    ks = jax.random.split(key, 24)
    f32 = jnp.float32
    L, D = DEPTH, D_MODEL

    def w(k, shape, fan_in, gain=1.0):
        return jax.random.normal(k, shape, f32) * (gain * fan_in ** -0.5)

    def gain(k, shape):
        return 1.0 + 0.02 * jax.random.normal(k, shape, f32)

    def bias(k, shape):
        return 0.02 * jax.random.normal(k, shape, f32)

    x = jax.random.normal(ks[0], (BATCH, SEQ, D), f32)
    mem = jax.random.normal(ks[1], (BATCH, MEM_LEN, D), f32)
    offset = jax.random.randint(ks[2], (BATCH, 1), 0, SEQ, dtype=jnp.int32)
    positions = (jnp.arange(SEQ, dtype=jnp.int32)[None, :] + offset).astype(jnp.int32)
    return {
        'x': x,
        'mem': mem,
        'positions': positions,
        'w_in': w(ks[3], (L, D, N_IN), D),
        'b_gate': bias(ks[4], (L, N_BRANCH, D)),
        'mla_q_norm': gain(ks[5], (L, MLA_Q_LORA)),
        'mla_kv_norm': gain(ks[6], (L, MLA_KV_LORA)),
        'mla_w_uq': w(ks[7], (L, MLA_Q_LORA, MLA_HEADS * (MLA_NOPE + MLA_ROPE)), MLA_Q_LORA),
        'mla_w_ukv': w(ks[8], (L, MLA_KV_LORA, MLA_HEADS * (MLA_NOPE + MLA_V)), MLA_KV_LORA),
        'diff_lambda': 0.1 * jax.random.normal(ks[9], (L, 4, DIFF_D), f32),
        'diff_subln': gain(ks[10], (L, 2 * DIFF_D)),
        'mem_w_kv': w(ks[11], (L, D, 2 * MEM_OUT), D),
        'w_branch': w(ks[12], (L, BRANCH_W, D), BRANCH_W // N_BRANCH),
        'w_out': w(ks[13], (L, D, D), D, DEEPNORM_BETA),
        'ln1_g': gain(ks[14], (L, D)),
        'ln1_b': bias(ks[15], (L, D)),
        'mlp_w1': w(ks[16], (L, D, D_FF), D),
        'mlp_w2': w(ks[17], (L, D_FF, D), D_FF, DEEPNORM_BETA),
        'ln2_g': gain(ks[18], (L, D)),
        'ln2_b': bias(ks[19], (L, D)),
    }


def reference(x, mem, positions, w_in, b_gate, mla_q_norm, mla_kv_norm, mla_w_uq, mla_w_ukv,
              diff_lambda, diff_subln, mem_w_kv, w_branch, w_out, ln1_g, ln1_b,
              mlp_w1, mlp_w2, ln2_g, ln2_b):
    b, s, d = x.shape
    cos_m, sin_m = _rope_tables(positions, MLA_ROPE, x.dtype)
    cos_d, sin_d = _rope_tables(positions, DIFF_ROT, x.dtype)
    splits = _split_points()
    for l in range(DEPTH):
        lambda_init = 0.8 - 0.6 * math.exp(-0.3 * l)
        h = x @ w_in[l]
        c_q, c_kv, k_pe, dq, dk, dv, mq, gate_pre = jnp.split(h, splits, axis=-1)
        o_mla = _mla_branch(c_q, c_kv, k_pe, mla_q_norm[l], mla_kv_norm[l], mla_w_uq[l], mla_w_ukv[l], cos_m, sin_m)
        o_diff = _diff_branch(dq, dk, dv, diff_lambda[l], diff_subln[l], lambda_init, cos_d, sin_d)
        o_mem = _mem_branch(mq, mem, mem_w_kv[l])
        g = jax.nn.sigmoid(gate_pre.reshape(b, s, N_BRANCH, d) + b_gate[l])
        wb = w_branch[l]
        merged = (g[:, :, 0] * (o_mla @ wb[:MLA_OUT])
                  + g[:, :, 1] * (o_diff @ wb[MLA_OUT:MLA_OUT + DIFF_OUT])
                  + g[:, :, 2] * (o_mem @ wb[MLA_OUT + DIFF_OUT:]))
        x = _layernorm(DEEPNORM_ALPHA * x + merged @ w_out[l], ln1_g[l], ln1_b[l])
        f = jnp.square(jax.nn.relu(x @ mlp_w1[l])) @ mlp_w2[l]
        x = _layernorm(DEEPNORM_ALPHA * x + f, ln2_g[l], ln2_b[l])
    return x
```

```python
import math
import os
import numpy as np
KDBG = float(os.environ.get('KDBG', '99'))
KSTOP = int(os.environ.get('KSTOP', '99'))
import concourse.bass as bass
import concourse.mybir as mybir
from concourse.bass_utils import run_bass_kernel_spmd

F32 = mybir.dt.float32
BF16 = mybir.dt.bfloat16
I32 = mybir.dt.int32
ALU = mybir.AluOpType
ACT = mybir.ActivationFunctionType

D = 1024
KC = 8
MEM_LEN = 256
DEPTH = 4
N_CORES = 8
BATCH = 16
SEQ = 2048
ROPE_THETA = 500000.0
ALPHA = (2 * DEPTH) ** 0.25
LN_EPS = 1e-5
C_CQ, C_CKV, C_KPE, C_DQ, C_DK, C_DV, C_MQ, C_GATE = 0, 384, 640, 672, 1696, 2720, 3744, 4256


class Res:
    __slots__ = ("name", "last_writer", "readers", "excl")

    def __init__(self, name="", excl=False):
        self.name = name
        self.last_writer = None
        self.readers = []
        self.excl = excl


class Op:
    __slots__ = ("eng", "emit", "deps", "signal", "seq", "is_dma", "dma_sem", "dma_val", "big")

    def __init__(self, eng, emit, is_dma=False, big=False):
        self.eng = eng
        self.emit = emit
        self.deps = []
        self.signal = False
        self.seq = 0
        self.is_dma = is_dma
        self.dma_sem = None
        self.dma_val = 0
        self.big = big


class FW:
    ENGS = ("tensor", "vector", "scalar", "gpsimd", "sync")

    def __init__(self, nc, same_engine_sync=True):
        self.nc = nc
        self.ops = []
        self.same_engine_sync = same_engine_sync
        self.dma_sems = {}
        self.all_res = []

    def res(self, name="", excl=False):
        r = Res(name, excl)
        self.all_res.append(r)
        return r

    def op(self, eng, emit, reads=(), writes=(), big=False):
        o = Op(eng, emit, big=big)
        self._track(o, reads, writes)
        return o

    def dma(self, eng, emit, semkey, reads=(), writes=()):
        o = Op(eng, emit, is_dma=True)
        if semkey not in self.dma_sems:
            self.dma_sems[semkey] = [self.nc.alloc_semaphore("dq%d" % len(self.dma_sems)), 0]
        ent = self.dma_sems[semkey]
        ent[1] += 16
        o.dma_sem = ent[0]
        o.dma_val = ent[1]
        self._track(o, reads, writes)
        return o

    def _track(self, o, reads, writes):
        ex = [r for r in reads if r.excl]
        if ex:
            reads = [r for r in reads if not r.excl]
            writes = list(writes) + [r for r in ex if r not in writes]
        deps = []
        for r in reads:
            if r.last_writer is not None:
                deps.append(r.last_writer)
        for w in writes:
            if w.last_writer is not None:
                deps.append(w.last_writer)
            deps.extend(w.readers)
        for r in reads:
            r.readers.append(o)
        for w in writes:
            w.last_writer = o
            w.readers = []
        seen = set()
        for d in deps:
            if id(d) not in seen and d is not o:
                seen.add(id(d))
                o.deps.append(d)
        self.ops.append(o)

    def barrier(self):
        for e in self.ENGS:
            self.op(e, lambda eng: eng.nop(), reads=(), writes=self.all_res)

    def finish(self):
        nc = self.nc
        engs = {e: getattr(nc, e) for e in self.ENGS}
        sems = {e: nc.alloc_semaphore("eng_" + e) for e in self.ENGS}
        for o in self.ops:
            kept = []
            for d in o.deps:
                if d.is_dma:
                    kept.append(d)
                    continue
                if d.eng == o.eng and not o.is_dma:
                    if d.eng == "tensor":
                        continue
                    if not self.same_engine_sync:
                        continue
                    if d.big and o.big:
                        continue
                kept.append(d)
                d.signal = True
            o.deps = kept
        cnt = {e: 0 for e in self.ENGS}
        for o in self.ops:
            if o.signal and not o.is_dma:
                cnt[o.eng] += 1
                o.seq = cnt[o.eng]
        waited = {e: {} for e in self.ENGS}
        for o in self.ops:
            eng = engs[o.eng]
            need = {}
            for d in o.deps:
                if d.is_dma:
                    s, v = d.dma_sem, d.dma_val
                else:
                    s, v = sems[d.eng], d.seq
                if need.get(s.num, (None, 0))[1] < v:
                    need[s.num] = (s, v)
            for num, (s, v) in need.items():
                if waited[o.eng].get(num, 0) >= v:
                    continue
                eng.wait_ge(s, v)
                waited[o.eng][num] = v
            ins = o.emit(eng)
            if o.is_dma:
                ins.then_inc(o.dma_sem, 16)
            elif o.signal:
                ins.then_inc(sems[o.eng], 1)
        return cnt


class Buf:
    __slots__ = ("ap", "res", "key")

    def __init__(self, ap, res, key=None):
        self.ap = ap
        self.res = res
        self.key = key


class Ring:
    def __init__(self, items):
        self.items = items
        self.i = 0

    def next(self):
        it = self.items[self.i % len(self.items)]
        self.i += 1
        return it


def _host_consts():
    c = {}
    c["ident"] = np.eye(128, dtype=np.float32)
    inv_m = (ROPE_THETA ** (-np.arange(0, 32, 2, dtype=np.float32) / np.float32(32))).astype(np.float32)
    inv_d = (ROPE_THETA ** (-np.arange(0, 16, 2, dtype=np.float32) / np.float32(16))).astype(np.float32)
    pv = np.zeros((128, 8), dtype=np.float32)
    rot_m = np.zeros((128, 128), dtype=np.float32)
    rot_d = np.zeros((128, 128), dtype=np.float32)
    for i in range(32):
        p = 64 + i
        pv[p, 0] = inv_m[i % 16]
        pv[p, 1] = -1.0 if i < 16 else 1.0
        partner = 64 + (i + 16 if i < 16 else i - 16)
        rot_m[partner, p] = 1.0
    for o in (0, 64):
        for i in range(16):
            p = o + i
            pv[p, 2] = inv_d[i % 8]
            pv[p, 3] = -1.0 if i < 8 else 1.0
            partner = o + (i + 8 if i < 8 else i - 8)
            rot_d[partner, p] = 1.0
    c["pvec"] = pv
    c["rot"] = np.stack([rot_m, rot_d], axis=0)
    return c


def build_program(NSEQ, S, NLAYER, layer_ids=None, upto=99):
    assert S % 512 == 0
    TB = S // 512
    TT = S // 128
    HB = min(1024, S // 2)
    NH = S // HB
    SBH = HB // 512
    nc = bass.Bass("TRN2", target_bir_lowering=False)
    fw = FW(nc)
    L = NLAYER

    def din(name, shape, dt=F32):
        return nc.dram_tensor(name, list(shape), dt, kind="ExternalInput").ap()

    x_d = din("x", [NSEQ, S, D])
    mem_d = din("mem", [NSEQ, MEM_LEN, D])
    pos_d = din("positions", [NSEQ, S], I32)
    w_in_d = din("w_in", [L, D, 7328])
    b_gate_d = din("b_gate", [L, 3, D])
    qn_d = din("mla_q_norm", [L, 384])
    kvn_d = din("mla_kv_norm", [L, 256])
    wuq_d = din("mla_w_uq", [L, 384, 768])
    wukv_d = din("mla_w_ukv", [L, 256, 1024])
    dlam_d = din("diff_lambda", [L, 4, 64])
    subln_d = din("diff_subln", [L, 128])
    wkv_d = din("mem_w_kv", [L, D, 1024])
    wbr_d = din("w_branch", [L, 2048, D])
    wout_d = din("w_out", [L, D, D])
    ln1g_d = din("ln1_g", [L, D])
    ln1b_d = din("ln1_b", [L, D])
    w1_d = din("mlp_w1", [L, D, 4096])
    w2_d = din("mlp_w2", [L, 4096, D])
    ln2g_d = din("ln2_g", [L, D])
    ln2b_d = din("ln2_b", [L, D])
    ident_d = din("c_ident", [128, 128])
    pvec_d = din("c_pvec", [128, 8])
    rot_d_ = din("c_rot", [2, 128, 128])
    y_d = nc.dram_tensor("y", [NSEQ, S, D], F32, kind="ExternalOutput").ap()

    def sb(name, shape, dt):
        return nc.alloc_sbuf_tensor(name, list(shape), dt)

    xhi = sb("xhi", [128, KC, S], BF16)
    xlo = sb("xlo", [128, KC, S], BF16)
    oT = sb("oT", [128, 16, S], BF16)
    memT = sb("memT", [128, KC, MEM_LEN], BF16)
    ident_f = sb("ident_f", [128, 128], F32)
    ones_b = sb("ones_b", [128, 128], BF16)
    onesD = sb("onesD", [128, 128], BF16)
    rot_f = sb("rot_f", [128, 2, 128], F32)
    rot_b = sb("rot_b", [128, 2, 128], BF16)
    pvec = sb("pvec", [128, 8], F32)
    neghalf = sb("neghalf", [128, 512], F32)
    halfpi = sb("halfpi", [128, 1], F32)
    par = sb("par", [128, L, 64], F32)
    bgh = par[:, :, 0:24]
    gq = par[:, :, 24:27]
    gkv = par[:, :, 27:29]
    gsub = par[:, :, 29]
    l1g = par[:, :, 30:38]
    l1b = par[:, :, 38:46]
    l2g = par[:, :, 46:54]
    l2b = par[:, :, 54:62]
    dls = sb("dls", [128, L, 2], F32)
    nlam = sb("nlam", [128, L], F32)
    R_const = fw.res("const")
    R_par = fw.res("par")
    R_memT = fw.res("memT")
    R_X = [[fw.res("x%d_%d" % (c, t)) for t in range(TB)] for c in range(KC)]
    R_O = [[fw.res("o%d_%d" % (c, t)) for t in range(TB)] for c in range(16)]

    AR = (nc.sbuf_bytes_remaining - 2048) // 2 // 16 * 16
    arena = sb("arena", [128, AR], BF16)

    class Carver:
        def __init__(self):
            self.off = 0

        def bf(self, n):
            v = arena[:, self.off:self.off + n]
            self.off += n
            assert self.off <= AR, ("arena overflow", self.off, AR)
            return v

        def f32(self, n):
            return self.bf(2 * n).bitcast(F32)

    psb = [nc.alloc_psum_tensor("ps%d" % i, [128, 512], F32) for i in range(8)]
    PS = [Buf(psb[i], fw.res("ps%d" % i, excl=True)) for i in range(8)]

    def V(fn, reads, writes):
        fw.op("vector", fn, reads, writes)

    def A(fn, reads, writes):
        fw.op("scalar", fn, reads, writes)

    def G(fn, reads, writes):
        fw.op("gpsimd", fn, reads, writes)

    def mm(out_ap, pairs, reads, wres):
        n = len(pairs)
        for i, (l_, r_) in enumerate(pairs):
            fw.op("tensor",
                  lambda e, l_=l_, r_=r_, i=i: e.matmul(out_ap, lhsT=l_, rhs=r_, start=(i == 0), stop=(i == n - 1)),
                  reads, [wres])

    def load_w(slot, dst_ap, src_ap):
        fw.dma("gpsimd", lambda e: e.dma_start(out=dst_ap, in_=src_ap), slot.key, writes=[slot.res])

    def wsrc(w_ap, r0, r1, c0, c1):
        return w_ap[r0:r1, c0:c1].rearrange("(c p) n -> p c n", p=128)

    def tok(tb):
        return slice(tb * 512, (tb + 1) * 512)

    fw.dma("sync", lambda e: e.dma_start(out=ident_f[:], in_=ident_d), "c0", writes=[R_const])
    fw.dma("sync", lambda e: e.dma_start(out=pvec[:], in_=pvec_d), "c1", writes=[R_const])
    fw.dma("sync", lambda e: e.dma_start(out=rot_f[:], in_=rot_d_.rearrange("r k m -> k r m")), "c2", writes=[R_const])
    V(lambda e: e.tensor_copy(out=rot_b[:], in_=rot_f[:]), [R_const], [R_const])
    V(lambda e: e.memset(ones_b[:], 1.0), [], [R_const])
    V(lambda e: e.memset(onesD[:], 1.0 / 1024.0), [], [R_const])
    V(lambda e: e.memset(neghalf[:], -0.5), [], [R_const])
    V(lambda e: e.memset(halfpi[:], math.pi / 2), [], [R_const])

    car0 = Carver()
    dl = car0.f32(L * 256).rearrange("p (l n) -> p l n", l=L)
    dlp = car0.f32(L * 128).rearrange("p (l n) -> p l n", l=L)

    stage = car0.f32(L * 128).rearrange("p (l n) -> p l n", l=L)
    R_stage = fw.res("stage")
    for l in range(L):
        rows = [(0, b_gate_d[l].rearrange("i (c p) -> (i c) p", p=128), 24),
                (24, qn_d[l].rearrange("(c p) -> c p", p=128), 3),
                (27, kvn_d[l].rearrange("(c p) -> c p", p=128), 2),
                (29, subln_d[l].rearrange("(c p) -> c p", p=128), 1),
                (30, ln1g_d[l].rearrange("(c p) -> c p", p=128), 8),
                (38, ln1b_d[l].rearrange("(c p) -> c p", p=128), 8),
                (46, ln2g_d[l].rearrange("(c p) -> c p", p=128), 8),
                (54, ln2b_d[l].rearrange("(c p) -> c p", p=128), 8)]
        for (r0, src, n) in rows:
            fw.dma("sync", lambda e, r0=r0, src=src, n=n, l=l: e.dma_start(out=stage[r0:r0 + n, l, :], in_=src), "pst", writes=[R_stage])
        fw.dma("sync", lambda e, l=l: e.dma_start(out=dl[:, l, :], in_=dlam_d[l].rearrange("a b -> (a b)").partition_broadcast(128)),
               "p8", writes=[R_par])
    for l in range(L):
        pp = PS[l % 8]
        fw.op("tensor", lambda e, l=l, pp=pp: e.transpose(out=pp.ap[:, 0:62], in_=stage[0:62, l, :], identity=ident_f[0:62, 0:62]),
              [R_stage, R_const], [pp.res])
        V(lambda e, l=l, pp=pp: e.tensor_copy(out=par[:, l, 0:62], in_=pp.ap[:, 0:62]), [pp.res], [R_par])
    for l in range(L):
        lam_init = 0.8 - 0.6 * math.exp(-0.3 * ((layer_ids[l]) if layer_ids is not None else l))
        V(lambda e, l=l: e.tensor_scalar(out=bgh[:, l, :], in0=bgh[:, l, :], scalar1=0.5, scalar2=None, op0=ALU.mult), [R_par], [R_par])
        V(lambda e, l=l: e.tensor_tensor(out=dlp[:, l, 0:64], in0=dl[:, l, 0:64], in1=dl[:, l, 64:128], op=ALU.mult), [R_par], [R_par])
        V(lambda e, l=l: e.tensor_tensor(out=dlp[:, l, 64:128], in0=dl[:, l, 128:192], in1=dl[:, l, 192:256], op=ALU.mult), [R_par], [R_par])
        V(lambda e, l=l: e.reduce_sum(out=dls[:, l, 0:1], in_=dlp[:, l, 0:64], axis=mybir.AxisListType.X), [R_par], [R_par])
        V(lambda e, l=l: e.reduce_sum(out=dls[:, l, 1:2], in_=dlp[:, l, 64:128], axis=mybir.AxisListType.X), [R_par], [R_par])
        A(lambda e, l=l: e.activation(out=dls[:, l, :], in_=dls[:, l, :], func=ACT.Exp), [R_par], [R_par])
        V(lambda e, l=l: e.tensor_tensor(out=nlam[:, l:l + 1], in0=dls[:, l, 1:2], in1=dls[:, l, 0:1], op=ALU.subtract), [R_par], [R_par])
        V(lambda e, l=l, li=lam_init: e.tensor_scalar(out=nlam[:, l:l + 1], in0=nlam[:, l:l + 1], scalar1=-li, scalar2=None, op0=ALU.add), [R_par], [R_par])

    def attention(car_state, qT, kT, vfn, kb, Kdim, dv, ro, nk, scale, q_reads, k_reads, v_reads, post):
        SC, ACC, PT = car_state
        for qc in range(TB):
            pv = ACC.next()
            den = ACC.next()
            for kt in range(nk):
                sc = SC.next()
                mm(sc.ap[:, :], [(kT[kb:kb + Kdim, kt * 128:(kt + 1) * 128], qT[kb:kb + Kdim, tok(qc)])],
                   q_reads + k_reads, sc.res)
                pt = PT.next()
                A(lambda e, sc=sc, pt=pt: e.activation(out=pt.ap, in_=sc.ap[:, :], func=ACT.Exp, scale=scale),
                  [sc.res], [pt.res])
                fw.op("tensor", lambda e, pv=pv, pt=pt, kt=kt: e.matmul(pv.ap[ro:ro + dv, :], lhsT=vfn(kt), rhs=pt.ap,
                                                                        start=(kt == 0), stop=(kt == nk - 1)),
                      [pt.res] + v_reads, [pv.res])
                fw.op("tensor", lambda e, den=den, pt=pt, kt=kt: e.matmul(den.ap[ro:ro + dv, :], lhsT=ones_b[:, 0:dv], rhs=pt.ap,
                                                                          start=(kt == 0), stop=(kt == nk - 1)),
                      [pt.res, R_const], [den.res])
            post(qc, pv, den)

    def load_sequence(s):
        car = Carver()
        xin = [Buf(car.f32(1024), fw.res("xin%d" % i), "xin%d" % i) for i in range(2)]
        ring = Ring(xin)
        pr = Ring(PS)
        for tt in range(TT):
            xb = ring.next()
            fw.dma("sync", lambda e, xb=xb, tt=tt: e.dma_start(out=xb.ap, in_=x_d[s, tt * 128:(tt + 1) * 128, :]),
                   xb.key, writes=[xb.res])
            tb = tt // 4
            for g in range(2):
                p = pr.next()
                for cc in range(4):
                    c = g * 4 + cc
                    fw.op("tensor", lambda e, p=p, xb=xb, c=c, cc=cc: e.transpose(out=p.ap[:, cc * 128:(cc + 1) * 128],
                                                                                 in_=xb.ap[:, c * 128:(c + 1) * 128], identity=ident_f[:]),
                          [xb.res, R_const], [p.res])
                wr = [R_X[g * 4 + cc][tb] for cc in range(4)]
                hi_v = xhi[:, g * 4:g * 4 + 4, tt * 128:(tt + 1) * 128]
                lo_v = xlo[:, g * 4:g * 4 + 4, tt * 128:(tt + 1) * 128]
                pv3 = p.ap[:, :].rearrange("p (a b) -> p a b", a=4)
                A(lambda e, hi_v=hi_v, pv3=pv3: e.activation(out=hi_v, in_=pv3, func=ACT.Copy), [p.res], wr)
                V(lambda e, lo_v=lo_v, hi_v=hi_v, pv3=pv3: e.tensor_tensor(out=lo_v, in0=pv3, in1=hi_v, op=ALU.subtract),
                  [p.res] + wr, wr)
        for mt in range(MEM_LEN // 128):
            xb = ring.next()
            fw.dma("sync", lambda e, xb=xb, mt=mt: e.dma_start(out=xb.ap, in_=mem_d[s, mt * 128:(mt + 1) * 128, :]),
                   xb.key, writes=[xb.res])
            for g in range(2):
                p = pr.next()
                for cc in range(4):
                    c = g * 4 + cc
                    fw.op("tensor", lambda e, p=p, xb=xb, c=c, cc=cc: e.transpose(out=p.ap[:, cc * 128:(cc + 1) * 128],
                                                                                 in_=xb.ap[:, c * 128:(c + 1) * 128], identity=ident_f[:]),
                          [xb.res, R_const], [p.res])
                V(lambda e, p=p, g=g, mt=mt: e.tensor_copy(out=memT[:, g * 4:g * 4 + 4, mt * 128:(mt + 1) * 128],
                                                          in_=p.ap[:, :].rearrange("p (a b) -> p a b", a=4)), [p.res], [R_memT])

    tab_d = nc.dram_tensor("tab_scratch", [4, 128, S], BF16, kind="Internal").ap()
    R_tabd = fw.res("tabd")

    def build_tables(s):
        car = Carver()
        posi = car.bf(2 * S).bitcast(I32)
        posf = car.f32(S)
        ang = car.f32(S)
        t1 = car.f32(S)
        t2 = car.f32(S)
        ki = car.bf(2 * S).bitcast(I32)
        tb16 = car.bf(S)
        R_t = fw.res("tabtmp")
        fw.dma("sync", lambda e: e.dma_start(out=posi, in_=pos_d[s, :].partition_broadcast(128)), "posi", writes=[R_t])
        V(lambda e: e.tensor_copy(out=posf, in_=posi), [R_t], [R_t])
        inv2pi = float(1.0 / (2 * math.pi))
        for ti, (fcol, scol) in enumerate([(0, 1), (2, 3)]):
            V(lambda e, fcol=fcol: e.tensor_scalar(out=ang, in0=posf, scalar1=pvec[:, fcol:fcol + 1], scalar2=None, op0=ALU.mult),
              [R_t, R_const], [R_t])
            for which in range(2):
                if which == 0:
                    V(lambda e: e.tensor_scalar(out=t1, in0=ang, scalar1=halfpi[:, 0:1], scalar2=None, op0=ALU.add), [R_t, R_const], [R_t])
                    src = t1
                else:
                    src = ang
                V(lambda e, src=src: e.tensor_scalar(out=t2, in0=src, scalar1=inv2pi, scalar2=None, op0=ALU.mult), [R_t], [R_t])
                V(lambda e: e.tensor_copy(out=ki, in_=t2), [R_t], [R_t])
                V(lambda e: e.tensor_copy(out=t2, in_=ki), [R_t], [R_t])
                V(lambda e, src=src: e.scalar_tensor_tensor(out=t2, in0=t2, scalar=float(-2 * math.pi), in1=src, op0=ALU.mult, op1=ALU.add),
                  [R_t], [R_t])
                A(lambda e: e.activation(out=t2, in_=t2, func=ACT.Sin), [R_t], [R_t])
                if which == 0:
                    V(lambda e: e.tensor_copy(out=tb16, in_=t2), [R_t], [R_t])
                else:
                    V(lambda e, scol=scol: e.tensor_scalar(out=tb16, in0=t2, scalar1=pvec[:, scol:scol + 1], scalar2=None, op0=ALU.mult),
                      [R_t, R_const], [R_t])
                idx = ti * 2 + which
                fw.dma("sync", lambda e, idx=idx: e.dma_start(out=tab_d[idx], in_=tb16), "tabst", reads=[R_t], writes=[R_tabd])

    def layer(s, l, last):
        lid = layer_ids[l] if layer_ids is not None else l
        lam_init = 0.8 - 0.6 * math.exp(-0.3 * lid)
        fw.barrier()
        car = Carver()
        tabs = car.bf(4 * S).rearrange("p (a b) -> p a b", a=4)
        R_tab = fw.res("tab")
        Cm, Sm, Cd, Sd = tabs[:, 0, :], tabs[:, 1, :], tabs[:, 2, :], tabs[:, 3, :]
        fw.dma("sync", lambda e: e.dma_start(out=tabs, in_=tab_d.rearrange("a p s -> p a s")), "tabld", reads=[R_tabd], writes=[R_tab])
        WS = Ring([Buf(car.bf(4096), fw.res("w%d" % i), "w%d" % i) for i in range(2)])
        qTt = car.bf(S)
        kTt = car.bf(S)
        vt = car.bf(TT * 128).rearrange("p (a b) -> p a b", a=TT)
        R_q, R_k, R_v = fw.res("q"), fw.res("k"), fw.res("v")
        PT = Ring([Buf(car.bf(512), fw.res("pt%d" % i)) for i in range(3)])
        TF = Ring([Buf(car.f32(512), fw.res("tf%d" % i)) for i in range(6)])
        TBf = Ring([Buf(car.bf(512), fw.res("tb%d" % i)) for i in range(3)])
        kpe = car.bf(max(S, 2048))
        R_kpe = fw.res("kpe")
        SC = Ring(PS[0:2])
        ACC = Ring(PS[2:6])
        GEN = Ring(PS[6:8])
        car_state = (SC, ACC, PT)
        USE["attn"] = car.off
        cqn = lambda j, tb: oT[:, 4 + j, tok(tb)]
        ckvn = lambda j, tb: oT[:, 7 + j, tok(tb)]
        R_cq = lambda j, tb: R_O[4 + j][tb]
        R_ckv = lambda j, tb: R_O[7 + j][tb]

        def rope_rows(dst_ap, a_ps, rows, tabC, tabS, rotsel, tsl, reads, wres):
            ab = TBf.next()
            A(lambda e: e.activation(out=ab.ap[rows, :], in_=a_ps.ap[rows, :], func=ACT.Copy), [a_ps.res], [ab.res])
            bp = GEN.next()
            mm(bp.ap[:, :], [(rot_b[:, rotsel, :], ab.ap)], [ab.res, R_const], bp.res)
            t1 = TF.next()
            V(lambda e: e.tensor_tensor(out=t1.ap[rows, :], in0=a_ps.ap[rows, :], in1=tabC[rows, tsl], op=ALU.mult),
              [a_ps.res, R_tab], [t1.res])
            t2 = TF.next()
            V(lambda e: e.tensor_tensor(out=t2.ap[rows, :], in0=bp.ap[rows, :], in1=tabS[rows, tsl], op=ALU.mult),
              [bp.res, R_tab], [t2.res])
            G(lambda e: e.tensor_tensor(out=dst_ap, in0=t1.ap[rows, :], in1=t2.ap[rows, :], op=ALU.add),
              [t1.res, t2.res] + reads, [wres])

        for b_ in TBf.items:
            V(lambda e, b_=b_: e.memset(b_.ap, 0.0), [], [b_.res])

        ws = WS.next()
        wcq = ws.ap[:, 0:KC * 384].rearrange("p (c n) -> p c n", c=KC)
        load_w(ws, wcq, wsrc(w_in_d[l], 0, D, C_CQ, C_CQ + 384))
        ws2 = WS.next()
        wck = ws2.ap[:, 0:KC * 288].rearrange("p (c n) -> p c n", c=KC)
        load_w(ws2, wck, wsrc(w_in_d[l], 0, D, C_CKV, C_CKV + 288))
        for tb in range(TB):
            xr = [R_X[k][tb] for k in range(KC)]
            for (wt, wsl, nj, cols, gvec, dstf, rdst, eps, nfe) in (
                    (wcq, ws, 3, 0, gq, cqn, R_cq, 1e-6, 384.0),
                    (wck, ws2, 2, 0, gkv, ckvn, R_ckv, 1e-6, 256.0)):
                pj = []
                sqs = []
                for j in range(nj):
                    p = ACC.next()
                    pj.append(p)
                    mm(p.ap[:, :], [(wt[:, k, cols + j * 128:cols + (j + 1) * 128], xhi[:, k, tok(tb)]) for k in range(KC)],
                       xr + [wsl.res], p.res)
                    sq = PT.next()
                    A(lambda e, p=p, sq=sq: e.activation(out=sq.ap, in_=p.ap[:, :], func=ACT.Square), [p.res], [sq.res])
                    sqs.append(sq)
                pss = GEN.next()
                mm(pss.ap[:, :], [(ones_b[:, :], sq.ap) for sq in sqs], [sq.res for sq in sqs] + [R_const], pss.res)
                vv = TF.next()
                V(lambda e, pss=pss, vv=vv, nfe=nfe, eps=eps: e.tensor_scalar(out=vv.ap, in0=pss.ap[:, :], scalar1=1.0 / nfe, scalar2=eps,
                                                                             op0=ALU.mult, op1=ALU.add), [pss.res], [vv.res])
                rs = TF.next()
                G(lambda e, vv=vv, rs=rs: e.tensor_tensor(out=rs.ap, in0=vv.ap, in1=neghalf[:], op=ALU.pow), [vv.res, R_const], [rs.res])
                for j in range(nj):
                    V(lambda e, j=j, p=pj[j], rs=rs, gvec=gvec, dap=dstf(j, tb): e.scalar_tensor_tensor(
                        out=dap, in0=p.ap[:, :], scalar=gvec[:, l, j:j + 1], in1=rs.ap, op0=ALU.mult, op1=ALU.mult),
                      [pj[j].res, rs.res, R_par], [rdst(j, tb)])
            p = ACC.next()
            mm(p.ap[64:96, :], [(wck[:, k, 256:288], xhi[:, k, tok(tb)]) for k in range(KC)], xr + [ws2.res], p.res)
            rope_rows(kpe[64:96, tok(tb)], p, slice(64, 96), Cm, Sm, 0, tok(tb), [], R_kpe)
        for h in range(8):
            ws = WS.next()
            wq = ws.ap[:, 0:3 * 96].rearrange("p (c n) -> p c n", c=3)
            wkv_ = ws.ap[:, 512:512 + 2 * 128].rearrange("p (c n) -> p c n", c=2)
            load_w(ws, wq, wsrc(wuq_d[l], 0, 384, h * 96, (h + 1) * 96))
            load_w(ws, wkv_, wsrc(wukv_d[l], 0, 256, h * 128, (h + 1) * 128))
            for tb in range(TB):
                cqr = [R_cq(j, tb) for j in range(3)]
                ckr = [R_ckv(j, tb) for j in range(2)]
                p = GEN.next()
                mm(p.ap[0:96, :], [(wq[:, j, :], cqn(j, tb)) for j in range(3)], cqr + [ws.res], p.res)
                V(lambda e, p=p, tb=tb: e.tensor_copy(out=qTt[0:64, tok(tb)], in_=p.ap[0:64, :]), [p.res], [R_q])
                rope_rows(qTt[64:96, tok(tb)], p, slice(64, 96), Cm, Sm, 0, tok(tb), [], R_q)
                p2 = GEN.next()
                mm(p2.ap[0:64, :], [(wkv_[:, j, 0:64], ckvn(j, tb)) for j in range(2)], ckr + [ws.res], p2.res)
                V(lambda e, p2=p2, tb=tb: e.tensor_copy(out=kTt[0:64, tok(tb)], in_=p2.ap[0:64, :]), [p2.res], [R_k])
                G(lambda e, tb=tb: e.tensor_copy(out=kTt[64:96, tok(tb)], in_=kpe[64:96, tok(tb)]), [R_kpe], [R_k])
                p3 = GEN.next()
                for t4 in range(4):
                    tsl = slice(tb * 512 + t4 * 128, tb * 512 + (t4 + 1) * 128)
                    mm(p3.ap[:, t4 * 64:(t4 + 1) * 64], [(oT[:, 7 + j, tsl], wkv_[:, j, 64:128]) for j in range(2)], ckr + [ws.res], p3.res)
                V(lambda e, p3=p3, tb=tb: e.tensor_copy(out=vt[:, tb * 4:(tb + 1) * 4, 0:64],
                                                        in_=p3.ap[:, 0:256].rearrange("p (a b) -> p a b", a=4)), [p3.res], [R_v])
            ro = (h % 2) * 64

            def post_mla(qc, pv, den, h=h, ro=ro):
                r = TF.next()
                V(lambda e: e.reciprocal(out=r.ap[ro:ro + 64, :], in_=den.ap[ro:ro + 64, :]), [den.res], [r.res])
                V(lambda e: e.tensor_tensor(out=oT[ro:ro + 64, h // 2, tok(qc)], in0=pv.ap[ro:ro + 64, :], in1=r.ap[ro:ro + 64, :], op=ALU.mult),
                  [pv.res, r.res], [R_O[h // 2][qc]])

            attention(car_state, qTt, kTt, lambda kt: vt[:, kt, 0:64], 0, 96, 64, ro, TT, 96.0 ** -0.5,
                      [R_q], [R_k], [R_v], post_mla)

        if upto < 4:
            return
        k1 = 1.0 / (128.0 * (1.0 - lam_init) ** 2)
        k2 = 1e-5 / ((1.0 - lam_init) ** 2)
        for h in range(8):
            ws = WS.next()
            w3 = ws.ap[:, 0:KC * 384].rearrange("p (c n) -> p c n", c=KC)
            for i3, c0 in enumerate((C_DQ, C_DK, C_DV)):
                load_w(ws, w3[:, :, i3 * 128:(i3 + 1) * 128], wsrc(w_in_d[l], 0, D, c0 + h * 128, c0 + (h + 1) * 128))
            for tb in range(TB):
                xr = [R_X[k][tb] for k in range(KC)]
                for i3, (dst, rdst) in enumerate(((qTt, R_q), (kTt, R_k))):
                    p = GEN.next()
                    mm(p.ap[:, :], [(w3[:, k, i3 * 128:(i3 + 1) * 128], xhi[:, k, tok(tb)]) for k in range(KC)], xr + [ws.res], p.res)
                    if KDBG <= -1:
                        V(lambda e, p=p, dst=dst, tb=tb: e.tensor_copy(out=dst[:, tok(tb)], in_=p.ap[:, :]), [p.res], [rdst])
                    else:
                        rope_rows(dst[:, tok(tb)], p, slice(0, 128), Cd, Sd, 1, tok(tb), [], rdst)
                if KDBG <= -2:
                    continue
                p3 = GEN.next()
                for t4 in range(4):
                    tsl = slice(tb * 512 + t4 * 128, tb * 512 + (t4 + 1) * 128)
                    mm(p3.ap[:, t4 * 128:(t4 + 1) * 128], [(xhi[:, k, tsl], w3[:, k, 256:384]) for k in range(KC)], xr + [ws.res], p3.res)
                V(lambda e, p3=p3, tb=tb: e.tensor_copy(out=vt[:, tb * 4:(tb + 1) * 4, :],
                                                        in_=p3.ap[:, :].rearrange("p (a b) -> p a b", a=4)), [p3.res], [R_v])
            for qc in range(TB):
                if KDBG < 1:
                    break
                accs = []
                for cmap in range(2 if KDBG >= 2 else 1):
                    pv = ACC.next()
                    den = ACC.next()
                    kb = cmap * 64
                    for kt in range(TT):
                        sc = SC.next()
                        mm(sc.ap[:, :], [(kTt[kb:kb + 64, kt * 128:(kt + 1) * 128], qTt[kb:kb + 64, tok(qc)])], [R_q, R_k], sc.res)
                        pt = PT.next()
                        A(lambda e, sc=sc, pt=pt: e.activation(out=pt.ap, in_=sc.ap[:, :], func=ACT.Exp, scale=64.0 ** -0.5),
                          [sc.res], [pt.res])
                        fw.op("tensor", lambda e, pv=pv, pt=pt, kt=kt: e.matmul(pv.ap[:, :], lhsT=vt[:, kt, :], rhs=pt.ap,
                                                                                start=(kt == 0), stop=(kt == TT - 1)),
                              [pt.res, R_v], [pv.res])
                        fw.op("tensor", lambda e, den=den, pt=pt, kt=kt: e.matmul(den.ap[:, :], lhsT=ones_b[:, :], rhs=pt.ap,
                                                                                  start=(kt == 0), stop=(kt == TT - 1)),
                              [pt.res, R_const], [den.res])
                    accs.append((pv, den))
                if KDBG < 3:
                    continue
                (pv0, den0), (pv1, den1) = accs
                r0 = TF.next()
                V(lambda e, r0=r0, den0=den0: e.reciprocal(out=r0.ap, in_=den0.ap[:, :]), [den0.res], [r0.res])
                a_ = TF.next()
                V(lambda e, a_=a_, pv0=pv0, r0=r0: e.tensor_tensor(out=a_.ap, in0=pv0.ap[:, :], in1=r0.ap, op=ALU.mult),
                  [pv0.res, r0.res], [a_.res])
                r1 = TF.next()
                V(lambda e, r1=r1, den1=den1: e.reciprocal(out=r1.ap, in_=den1.ap[:, :]), [den1.res], [r1.res])
                b_ = TF.next()
                V(lambda e, b_=b_, pv1=pv1, r1=r1: e.scalar_tensor_tensor(out=b_.ap, in0=pv1.ap[:, :], scalar=nlam[:, l:l + 1], in1=r1.ap,
                                                                          op0=ALU.mult, op1=ALU.mult), [pv1.res, r1.res, R_par], [b_.res])
                G(lambda e, a_=a_, b_=b_: e.tensor_tensor(out=a_.ap, in0=a_.ap, in1=b_.ap, op=ALU.add), [a_.res, b_.res], [a_.res])
                sq = PT.next()
                A(lambda e, sq=sq, a_=a_: e.activation(out=sq.ap, in_=a_.ap, func=ACT.Square), [a_.res], [sq.res])
                pss = GEN.next()
                mm(pss.ap[:, :], [(ones_b[:, :], sq.ap)], [sq.res, R_const], pss.res)
                vv = TF.next()
                V(lambda e, pss=pss, vv=vv: e.tensor_scalar(out=vv.ap, in0=pss.ap[:, :], scalar1=k1, scalar2=k2, op0=ALU.mult, op1=ALU.add),
                  [pss.res], [vv.res])
                G(lambda e, vv=vv: e.tensor_tensor(out=vv.ap, in0=vv.ap, in1=neghalf[:], op=ALU.pow), [vv.res, R_const], [vv.res])
                V(lambda e, a_=a_, vv=vv, h=h, qc=qc: e.scalar_tensor_tensor(out=oT[:, 4 + h, tok(qc)], in0=a_.ap, scalar=par[:, l, 29:30],
                                                                            in1=vv.ap, op0=ALU.mult, op1=ALU.mult),
                  [a_.res, vv.res, R_par], [R_O[4 + h][qc]])

        if upto < 5:
            return
        Km = kpe[:, 0:4 * MEM_LEN].rearrange("p (a b) -> p a b", a=4)
        Vm = kpe[:, 4 * MEM_LEN:4 * MEM_LEN + 2 * 512].rearrange("p (a b) -> p a b", a=2)
        for half in range(2):
            ws = WS.next()
            wk_ = ws.ap[:, 0:KC * 512].rearrange("p (c n) -> p c n", c=KC)
            load_w(ws, wk_, wsrc(wkv_d[l], 0, D, half * 512, (half + 1) * 512))
            if half == 0:
                for hh in range(4):
                    p = GEN.next()
                    mm(p.ap[:, 0:MEM_LEN], [(wk_[:, k, hh * 128:(hh + 1) * 128], memT[:, k, :]) for k in range(KC)], [R_memT, ws.res], p.res)
                    V(lambda e, p=p, hh=hh: e.tensor_copy(out=Km[:, hh, :], in_=p.ap[:, 0:MEM_LEN]), [p.res], [R_kpe])
            else:
                for mt in range(2):
                    p = GEN.next()
                    mm(p.ap[:, :], [(memT[:, k, mt * 128:(mt + 1) * 128], wk_[:, k, :]) for k in range(KC)], [R_memT, ws.res], p.res)
                    V(lambda e, p=p, mt=mt: e.tensor_copy(out=Vm[:, mt, :], in_=p.ap[:, :]), [p.res], [R_kpe])
        ws = WS.next()
        wmq = ws.ap[:, 0:KC * 512].rearrange("p (c n) -> p c n", c=KC)
        load_w(ws, wmq, wsrc(w_in_d[l], 0, D, C_MQ, C_MQ + 512))
        for hh in range(4):
            for tb in range(TB):
                xr = [R_X[k][tb] for k in range(KC)]
                p = GEN.next()
                mm(p.ap[:, :], [(wmq[:, k, hh * 128:(hh + 1) * 128], xhi[:, k, tok(tb)]) for k in range(KC)], xr + [ws.res], p.res)
                V(lambda e, p=p, tb=tb: e.tensor_copy(out=qTt[:, tok(tb)], in_=p.ap[:, :]), [p.res], [R_q])

            def post_mem(qc, pv, den, hh=hh):
                r = TF.next()
                V(lambda e: e.reciprocal(out=r.ap, in_=den.ap[:, :]), [den.res], [r.res])
                V(lambda e: e.tensor_tensor(out=oT[:, 12 + hh, tok(qc)], in0=pv.ap[:, :], in1=r.ap, op=ALU.mult),
                  [pv.res, r.res], [R_O[12 + hh][qc]])

            attention(car_state, qTt, Km[:, hh, :], lambda kt, hh=hh: Vm[:, kt, hh * 128:(hh + 1) * 128], 0, 128, 128, 0, 2,
                      128.0 ** -0.5, [R_q], [R_kpe], [R_kpe], post_mem)

        if upto < 6:
            return
        fw.barrier()
        car = Carver()
        WS = Ring([Buf(car.bf(3072), fw.res("wm%d" % i), "wm%d" % i) for i in range(3)])
        merged = car.bf(KC * HB).rearrange("p (c n) -> p c n", c=KC)
        R_m = [[fw.res("m%d_%d" % (c, sbk)) for sbk in range(SBH)] for c in range(KC)]
        zt = [car.f32(KC * 512).rearrange("p (c n) -> p c n", c=KC) for _ in range(1)]
        R_z = [[fw.res("z%d_%d" % (i, c)) for c in range(KC)] for i in range(1)]
        TF = Ring([Buf(car.f32(512), fw.res("tf%d" % i)) for i in range(6)])
        ZB = Ring([Buf(car.bf(512), fw.res("zb%d" % i)) for i in range(4)])
        GEN = Ring(PS[0:6])
        STAT = PS[6:8]
        USE["merge"] = car.off
        br_ranges = ((0, 4), (4, 12), (12, 16))

        def layernorm(tb, zi, eps, gvec, bvec, final_out):
            z = zt[zi]
            s1, s2 = STAT
            for c in range(KC):
                zb = ZB.next()
                A(lambda e, zb=zb, c=c: e.activation(out=zb.ap, in_=z[:, c, :], func=ACT.Copy), [R_z[zi][c]], [zb.res])
                fw.op("tensor", lambda e, zb=zb, c=c: e.matmul(s1.ap[:, :], lhsT=onesD[:, :], rhs=zb.ap, start=(c == 0), stop=(c == KC - 1)),
                      [zb.res, R_const], [s1.res])
                zq = ZB.next()
                A(lambda e, zq=zq, c=c: e.activation(out=zq.ap, in_=z[:, c, :], func=ACT.Square), [R_z[zi][c]], [zq.res])
                fw.op("tensor", lambda e, zq=zq, c=c: e.matmul(s2.ap[:, :], lhsT=onesD[:, :], rhs=zq.ap, start=(c == 0), stop=(c == KC - 1)),
                      [zq.res, R_const], [s2.res])
            msq = TF.next()
            A(lambda e: e.activation(out=msq.ap, in_=s1.ap[:, :], func=ACT.Square), [s1.res], [msq.res])
            vv = TF.next()
            V(lambda e: e.scalar_tensor_tensor(out=vv.ap, in0=s2.ap[:, :], scalar=float(eps), in1=msq.ap, op0=ALU.add, op1=ALU.subtract),
              [s2.res, msq.res], [vv.res])
            G(lambda e: e.tensor_tensor(out=vv.ap, in0=vv.ap, in1=neghalf[:], op=ALU.pow), [vv.res, R_const], [vv.res])
            nmr = TF.next()
            V(lambda e: e.scalar_tensor_tensor(out=nmr.ap, in0=s1.ap[:, :], scalar=-1.0, in1=vv.ap, op0=ALU.mult, op1=ALU.mult),
              [s1.res, vv.res], [nmr.res])
            for c in range(KC):
                rz = R_z[zi][c]
                V(lambda e, c=c: e.tensor_tensor(out=z[:, c, :], in0=z[:, c, :], in1=vv.ap, op=ALU.mult), [rz, vv.res], [rz])
                G(lambda e, c=c: e.tensor_tensor(out=z[:, c, :], in0=z[:, c, :], in1=nmr.ap, op=ALU.add), [rz, nmr.res], [rz])
                A(lambda e, c=c: e.activation(out=z[:, c, :], in_=z[:, c, :], func=ACT.Identity, bias=bvec[:, l, c:c + 1],
                                              scale=gvec[:, l, c:c + 1]), [rz, R_par], [rz])
                rx = R_X[c][tb]
                A(lambda e, c=c: e.activation(out=xhi[:, c, tok(tb)], in_=z[:, c, :], func=ACT.Copy), [rz], [rx])
                V(lambda e, c=c: e.tensor_tensor(out=xlo[:, c, tok(tb)], in0=z[:, c, :], in1=xhi[:, c, tok(tb)], op=ALU.subtract),
                  [rz, rx], [rx])
            if final_out:
                for t4 in range(4):
                    ob = OUTB.next()
                    for g in range(2):
                        p = GEN.next()
                        for cc in range(4):
                            c = g * 4 + cc
                            fw.op("tensor", lambda e, p=p, c=c, cc=cc, t4=t4: e.transpose(out=p.ap[:, cc * 128:(cc + 1) * 128],
                                                                                         in_=z[:, c, t4 * 128:(t4 + 1) * 128], identity=ident_f[:]),
                                  [R_z[zi][c], R_const], [p.res])
                        A(lambda e, p=p, ob=ob, g=g: e.activation(out=ob.ap[:, g * 512:(g + 1) * 512], in_=p.ap[:, :], func=ACT.Copy), [p.res], [ob.res])
                    t0 = tb * 512 + t4 * 128
                    ry = fw.res("y")
                    R_y.append(ry)
                    fw.dma("sync", lambda e, ob=ob, t0=t0: e.dma_start(out=y_d[s, t0:t0 + 128, :], in_=ob.ap), ob.key,
                           reads=[ob.res], writes=[ry])

        for th in range(NH):
            for c in range(KC):
                wsg = WS.next()
                wg = wsg.ap[:, 0:KC * 384].rearrange("p (k n) -> p k n", k=KC)
                for i in range(3):
                    c0 = C_GATE + i * D + c * 128
                    load_w(wsg, wg[:, :, i * 128:(i + 1) * 128], wsrc(w_in_d[l], 0, D, c0, c0 + 128))
                wsb = WS.next()
                wb = wsb.ap[:, 0:16 * 128].rearrange("p (k n) -> p k n", k=16)
                load_w(wsb, wb, wsrc(wbr_d[l], 0, 2048, c * 128, (c + 1) * 128))
                for sbk in range(SBH):
                    tb = th * SBH + sbk
                    xr = [R_X[k][tb] for k in range(KC)]
                    us = []
                    for i in range(3):
                        pg = GEN.next()
                        mm(pg.ap[:, :], [(wg[:, k, i * 128:(i + 1) * 128], xhi[:, k, tok(tb)]) for k in range(KC)], xr + [wsg.res], pg.res)
                        t_ = TF.next()
                        A(lambda e, pg=pg, t_=t_, i=i, c=c: e.activation(out=t_.ap, in_=pg.ap[:, :], func=ACT.Tanh,
                                                                        bias=bgh[:, l, i * 8 + c:i * 8 + c + 1], scale=0.5),
                          [pg.res, R_par], [t_.res])
                        pb = GEN.next()
                        k0, k1_ = br_ranges[i]
                        mm(pb.ap[:, :], [(wb[:, kk, :], oT[:, kk, tok(tb)]) for kk in range(k0, k1_)],
                           [R_O[kk][tb] for kk in range(k0, k1_)] + [wsb.res], pb.res)
                        V(lambda e, t_=t_, pb=pb: e.scalar_tensor_tensor(out=t_.ap, in0=t_.ap, scalar=1.0, in1=pb.ap[:, :],
                                                                        op0=ALU.add, op1=ALU.mult), [t_.res, pb.res], [t_.res])
                        us.append(t_)
                    G(lambda e, us=us: e.tensor_tensor(out=us[0].ap, in0=us[0].ap, in1=us[1].ap, op=ALU.add),
                      [us[0].res, us[1].res], [us[0].res])
                    G(lambda e, us=us, c=c, sbk=sbk: e.tensor_tensor(out=merged[:, c, sbk * 512:(sbk + 1) * 512], in0=us[0].ap, in1=us[2].ap, op=ALU.add),
                      [us[0].res, us[2].res], [R_m[c][sbk]])
            for sbk in range(SBH):
                tb = th * SBH + sbk
                for c2 in range(KC):
                    wso = WS.next()
                    wo = wso.ap[:, 0:KC * 128].rearrange("p (k n) -> p k n", k=KC)
                    load_w(wso, wo, wsrc(wout_d[l], 0, D, c2 * 128, (c2 + 1) * 128))
                    py = GEN.next()
                    mm(py.ap[:, :], [(wo[:, k, :], merged[:, k, sbk * 512:(sbk + 1) * 512]) for k in range(KC)],
                       [R_m[k][sbk] for k in range(KC)] + [wso.res], py.res)
                    rz = R_z[0][c2]
                    V(lambda e, py=py, c2=c2, tb=tb, z0=zt[0]: e.scalar_tensor_tensor(out=z0[:, c2, :], in0=xhi[:, c2, tok(tb)], scalar=2.0 * ALPHA,
                                                                           in1=py.ap[:, :], op0=ALU.mult, op1=ALU.add),
                      [py.res, R_X[c2][tb]], [rz])
                    V(lambda e, c2=c2, tb=tb, z0=zt[0]: e.scalar_tensor_tensor(out=z0[:, c2, :], in0=xlo[:, c2, tok(tb)], scalar=2.0 * ALPHA,
                                                                    in1=z0[:, c2, :], op0=ALU.mult, op1=ALU.add),
                      [R_X[c2][tb], rz], [rz])
                layernorm(tb, 0, 4.0 * LN_EPS, l1g, l1b, False)

        if upto < 7:
            return
        fw.barrier()
        car = Carver()
        W1 = Ring([Buf(car.bf(1024), fw.res("wa%d" % i), "wa%d" % i) for i in range(3)])
        W2 = Ring([Buf(car.bf(2048), fw.res("wb%d" % i), "wb%d" % i) for i in range(2)])
        zt = [car.f32(KC * 512).rearrange("p (c n) -> p c n", c=KC) for _ in range(SBH)]
        R_z = [[fw.res("zf%d_%d" % (i, c)) for c in range(KC)] for i in range(SBH)]
        TF = Ring([Buf(car.f32(512), fw.res("tg%d" % i)) for i in range(6)])
        ZB = Ring([Buf(car.bf(512), fw.res("zc%d" % i)) for i in range(4)])
        OUTB = Ring([Buf(car.f32(1024), fw.res("ob%d" % i), "ob%d" % i) for i in range(1)]) if last else None
        GEN = Ring(PS[0:6])
        STAT = PS[6:8]
        USE["ffn"] = car.off
        h1 = oT[:, :, :].rearrange("p c s -> p (c s)")[:, 0:32 * HB].rearrange("p (j t) -> p j t", j=32)

        def R_h(j, sbk):
            flat = j * HB + sbk * 512
            return R_O[flat // S][(flat % S) // 512]

        if KSTOP <= 0:
            return
        for th in range(NH):
            for j in range(32):
                ws = W1.next()
                w1c = ws.ap[:, 0:KC * 128].rearrange("p (k n) -> p k n", k=KC)
                load_w(ws, w1c, wsrc(w1_d[l], 0, D, j * 128, (j + 1) * 128))
                for sbk in range(SBH):
                    tb = th * SBH + sbk
                    xr = [R_X[k][tb] for k in range(KC)]
                    ph = GEN.next()
                    mm(ph.ap[:, :], [(w1c[:, k, :], xhi[:, k, tok(tb)]) for k in range(KC)], xr + [ws.res], ph.res)
                    r_ = TF.next()
                    V(lambda e, ph=ph, r_=r_: e.tensor_scalar(out=r_.ap, in0=ph.ap[:, :], scalar1=0.0, scalar2=None, op0=ALU.max), [ph.res], [r_.res])
                    A(lambda e, r_=r_, j=j, sbk=sbk: e.activation(out=h1[:, j, sbk * 512:(sbk + 1) * 512], in_=r_.ap, func=ACT.Square),
                      [r_.res], [R_h(j, sbk)])
                    if KDBG == 78 and j == 0 and sbk == 0 and th == 0:
                        dr = nc.dram_tensor("dbg_r", [128, 512], F32, kind="ExternalOutput").ap()
                        dw = nc.dram_tensor("dbg_w", [128, KC, 128], BF16, kind="ExternalOutput").ap()
                        dxx = nc.dram_tensor("dbg_x", [128, KC, 512], BF16, kind="ExternalOutput").ap()
                        dp = nc.dram_tensor("dbg_p", [128, 512], F32, kind="ExternalOutput").ap()
                        for nm_, dst_, src_, rd_ in (("r", dr, r_.ap, [r_.res]), ("w", dw, w1c, [ws.res]), ("x", dxx, xhi[:, :, tok(tb)], xr)):
                            rr_ = fw.res("dbg" + nm_)
                            R_y.append(rr_)
                            fw.dma("sync", lambda e, dst_=dst_, src_=src_: e.dma_start(out=dst_, in_=src_), "dbgk" + nm_, reads=rd_, writes=[rr_])
                        ph2 = GEN.next()
                        mm(ph2.ap[:, :], [(w1c[:, k, :], xhi[:, k, tok(tb)]) for k in range(KC)], xr + [ws.res], ph2.res)
                        r2_ = TF.next()
                        A(lambda e, ph2=ph2, r2_=r2_: e.activation(out=r2_.ap, in_=ph2.ap[:, :], func=ACT.Copy), [ph2.res], [r2_.res])
                        rr_ = fw.res("dbgp")
                        R_y.append(rr_)
                        fw.dma("sync", lambda e, dp=dp, r2_=r2_: e.dma_start(out=dp, in_=r2_.ap), "dbgkp", reads=[r2_.res], writes=[rr_])
            if KSTOP <= 1:
                return
            for c2 in range(KC):
                wsa = W2.next()
                w2a = wsa.ap[:, 0:16 * 128].rearrange("p (k n) -> p k n", k=16)
                load_w(wsa, w2a, wsrc(w2_d[l], 0, 2048, c2 * 128, (c2 + 1) * 128))
                wsb2 = W2.next()
                w2b = wsb2.ap[:, 0:16 * 128].rearrange("p (k n) -> p k n", k=16)
                load_w(wsb2, w2b, wsrc(w2_d[l], 2048, 4096, c2 * 128, (c2 + 1) * 128))
                for sbk in range(SBH):
                    tb = th * SBH + sbk
                    pf = GEN.next()
                    mm(pf.ap[:, :], [((w2a if j < 16 else w2b)[:, j % 16, :], h1[:, j, sbk * 512:(sbk + 1) * 512]) for j in range(32)],
                       [R_h(j, sbk) for j in range(32)] + [wsa.res, wsb2.res], pf.res)
                    rz = R_z[sbk][c2]
                    V(lambda e, pf=pf, c2=c2, tb=tb, zs=zt[sbk]: e.scalar_tensor_tensor(out=zs[:, c2, :], in0=xhi[:, c2, tok(tb)], scalar=ALPHA,
                                                                                    in1=pf.ap[:, :], op0=ALU.mult, op1=ALU.add),
                      [pf.res, R_X[c2][tb]], [rz])
                    V(lambda e, c2=c2, tb=tb, zs=zt[sbk]: e.scalar_tensor_tensor(out=zs[:, c2, :], in0=xlo[:, c2, tok(tb)], scalar=ALPHA,
                                                                             in1=zs[:, c2, :], op0=ALU.mult, op1=ALU.add),
                      [R_X[c2][tb], rz], [rz])
            if KSTOP <= 2:
                return
            for sbk in range(SBH):
                layernorm(th * SBH + sbk, sbk, LN_EPS, l2g, l2b, last)

    R_y = []
    USE = {}
    for s in range(NSEQ):
        fw.barrier()
        if upto >= 1:
            load_sequence(s)
        fw.barrier()
        if upto >= 2:
            build_tables(s)
        if upto >= 3:
            for l in range(NLAYER):
                layer(s, l, last=(l == NLAYER - 1))
    if KDBG == 77:
        dx = nc.dram_tensor("dbg_xhi", [128, KC, S], BF16, kind="ExternalOutput").ap()
        do = nc.dram_tensor("dbg_oT", [128, 16, S], BF16, kind="ExternalOutput").ap()
        fw.barrier()
        r1_, r2_ = fw.res("d1"), fw.res("d2")
        fw.dma("sync", lambda e: e.dma_start(out=dx, in_=xhi[:]), "dbg1", reads=[R_X[c][t] for c in range(KC) for t in range(TB)], writes=[r1_])
        fw.dma("sync", lambda e: e.dma_start(out=do, in_=oT[:]), "dbg2", reads=[R_O[c][t] for c in range(16) for t in range(TB)], writes=[r2_])
        R_y = R_y + [r1_, r2_]
    fw.op("sync", lambda e: e.nop(), reads=R_y, writes=[])
    counts = fw.finish()
    counts["arena"] = AR
    counts.update(USE)
    counts["nops"] = len(fw.ops)
    return nc, counts


_CACHE = {}
WEIGHT_KEYS = ["w_in", "b_gate", "mla_q_norm", "mla_kv_norm", "mla_w_uq", "mla_w_ukv", "diff_lambda",
               "diff_subln", "mem_w_kv", "w_branch", "w_out", "ln1_g", "ln1_b", "mlp_w1", "mlp_w2", "ln2_g", "ln2_b"]


def kernel(**inputs):
    x = np.ascontiguousarray(np.asarray(inputs["x"], dtype=np.float32))
    mem = np.ascontiguousarray(np.asarray(inputs["mem"], dtype=np.float32))
    pos = np.ascontiguousarray(np.asarray(inputs["positions"], dtype=np.int32))
    B, S, _ = x.shape
    nseq = B // N_CORES
    key = (nseq, S, DEPTH)
    if key not in _CACHE:
        _CACHE[key] = build_program(nseq, S, DEPTH)[0]
    nc = _CACHE[key]
    consts = _host_consts()
    shared = {k: np.ascontiguousarray(np.asarray(inputs[k], dtype=np.float32)) for k in WEIGHT_KEYS}
    shared["c_ident"] = consts["ident"]
    shared["c_pvec"] = consts["pvec"]
    shared["c_rot"] = consts["rot"]
    in_maps = []
    for c in range(N_CORES):
        m = dict(shared)
        m["x"] = x[c * nseq:(c + 1) * nseq]
        m["mem"] = mem[c * nseq:(c + 1) * nseq]
        m["positions"] = pos[c * nseq:(c + 1) * nseq]
        in_maps.append(m)
    res = run_bass_kernel_spmd(nc, in_maps, core_ids=list(range(N_CORES)))
    out = np.concatenate([np.asarray(r["y"]) for r in res.results], axis=0)
    return out.astype(np.float32)
```

```python
import math
import os
import numpy as np
KDBG = float(os.environ.get('KDBG', '99'))
KSTOP = int(os.environ.get('KSTOP', '99'))
import concourse.bass as bass
import concourse.mybir as mybir
from concourse.bass_utils import run_bass_kernel_spmd

F32 = mybir.dt.float32
BF16 = mybir.dt.bfloat16
I32 = mybir.dt.int32
ALU = mybir.AluOpType
ACT = mybir.ActivationFunctionType

D = 1024
KC = 8
MEM_LEN = 256
DEPTH = 4
N_CORES = 8
BATCH = 16
SEQ = 2048
ROPE_THETA = 500000.0
ALPHA = (2 * DEPTH) ** 0.25
LN_EPS = 1e-5
C_CQ, C_CKV, C_KPE, C_DQ, C_DK, C_DV, C_MQ, C_GATE = 0, 384, 640, 672, 1696, 2720, 3744, 4256


class Res:
    __slots__ = ("name", "last_writer", "readers", "excl")

    def __init__(self, name="", excl=False):
        self.name = name
        self.last_writer = None
        self.readers = []
        self.excl = excl


class Op:
    __slots__ = ("eng", "emit", "deps", "signal", "seq", "is_dma", "dma_sem", "dma_val", "big")

    def __init__(self, eng, emit, is_dma=False, big=False):
        self.eng = eng
        self.emit = emit
        self.deps = []
        self.signal = False
        self.seq = 0
        self.is_dma = is_dma
        self.dma_sem = None
        self.dma_val = 0
        self.big = big


class FW:
    ENGS = ("tensor", "vector", "scalar", "gpsimd", "sync")

    def __init__(self, nc, same_engine_sync=True):
        self.nc = nc
        self.ops = []
        self.same_engine_sync = same_engine_sync
        self.dma_sems = {}
        self.all_res = []

    def res(self, name="", excl=False):
        r = Res(name, excl)
        self.all_res.append(r)
        return r

    def op(self, eng, emit, reads=(), writes=(), big=False):
        o = Op(eng, emit, big=big)
        self._track(o, reads, writes)
        return o

    def dma(self, eng, emit, semkey, reads=(), writes=()):
        o = Op(eng, emit, is_dma=True)
        if semkey not in self.dma_sems:
            self.dma_sems[semkey] = [self.nc.alloc_semaphore("dq%d" % len(self.dma_sems)), 0]
        ent = self.dma_sems[semkey]
        ent[1] += 16
        o.dma_sem = ent[0]
        o.dma_val = ent[1]
        self._track(o, reads, writes)
        return o

    def _track(self, o, reads, writes):
        ex = [r for r in reads if r.excl]
        if ex:
            reads = [r for r in reads if not r.excl]
            writes = list(writes) + [r for r in ex if r not in writes]
        deps = []
        for r in reads:
            if r.last_writer is not None:
                deps.append(r.last_writer)
        for w in writes:
            if w.last_writer is not None:
                deps.append(w.last_writer)
            deps.extend(w.readers)
        for r in reads:
            r.readers.append(o)
        for w in writes:
            w.last_writer = o
            w.readers = []
        seen = set()
        for d in deps:
            if id(d) not in seen and d is not o:
                seen.add(id(d))
                o.deps.append(d)
        self.ops.append(o)

    def barrier(self):
        for e in self.ENGS:
            self.op(e, lambda eng: eng.nop(), reads=(), writes=self.all_res)

    def finish(self):
        nc = self.nc
        engs = {e: getattr(nc, e) for e in self.ENGS}
        sems = {e: nc.alloc_semaphore("eng_" + e) for e in self.ENGS}
        for o in self.ops:
            kept = []
            for d in o.deps:
                if d.is_dma:
                    kept.append(d)
                    continue
                if d.eng == o.eng and not o.is_dma:
                    if d.eng == "tensor":
                        continue
                    if not self.same_engine_sync:
                        continue
                    if d.big and o.big:
                        continue
                kept.append(d)
                d.signal = True
            o.deps = kept
        cnt = {e: 0 for e in self.ENGS}
        for o in self.ops:
            if o.signal and not o.is_dma:
                cnt[o.eng] += 1
                o.seq = cnt[o.eng]
        waited = {e: {} for e in self.ENGS}
        for o in self.ops:
            eng = engs[o.eng]
            need = {}
            for d in o.deps:
                if d.is_dma:
                    s, v = d.dma_sem, d.dma_val
                else:
                    s, v = sems[d.eng], d.seq
                if need.get(s.num, (None, 0))[1] < v:
                    need[s.num] = (s, v)
            for num, (s, v) in need.items():
                if waited[o.eng].get(num, 0) >= v:
                    continue
                eng.wait_ge(s, v)
                waited[o.eng][num] = v
            ins = o.emit(eng)
            if o.is_dma:
                ins.then_inc(o.dma_sem, 16)
            elif o.signal:
                ins.then_inc(sems[o.eng], 1)
        return cnt


class Buf:
    __slots__ = ("ap", "res", "key")

    def __init__(self, ap, res, key=None):
        self.ap = ap
        self.res = res
        self.key = key


class Ring:
    def __init__(self, items):
        self.items = items
        self.i = 0

    def next(self):
        it = self.items[self.i % len(self.items)]
        self.i += 1
        return it


def _host_consts():
    c = {}
    c["ident"] = np.eye(128, dtype=np.float32)
    inv_m = (ROPE_THETA ** (-np.arange(0, 32, 2, dtype=np.float32) / np.float32(32))).astype(np.float32)
    inv_d = (ROPE_THETA ** (-np.arange(0, 16, 2, dtype=np.float32) / np.float32(16))).astype(np.float32)
    pv = np.zeros((128, 8), dtype=np.float32)
    rot_m = np.zeros((128, 128), dtype=np.float32)
    rot_d = np.zeros((128, 128), dtype=np.float32)
    for i in range(32):
        p = 64 + i
        pv[p, 0] = inv_m[i % 16]
        pv[p, 1] = -1.0 if i < 16 else 1.0
        partner = 64 + (i + 16 if i < 16 else i - 16)
        rot_m[partner, p] = 1.0
    for o in (0, 64):
        for i in range(16):
            p = o + i
            pv[p, 2] = inv_d[i % 8]
            pv[p, 3] = -1.0 if i < 8 else 1.0
            partner = o + (i + 8 if i < 8 else i - 8)
            rot_d[partner, p] = 1.0
    c["pvec"] = pv
    c["rot"] = np.stack([rot_m, rot_d], axis=0)
    return c


def build_program(NSEQ, S, NLAYER, layer_ids=None, upto=99):
    assert S % 512 == 0
    TB = S // 512
    TT = S // 128
    HB = min(1024, S // 2)
    NH = S // HB
    SBH = HB // 512
    nc = bass.Bass("TRN2", target_bir_lowering=False)
    fw = FW(nc)
    L = NLAYER

    def din(name, shape, dt=F32):
        return nc.dram_tensor(name, list(shape), dt, kind="ExternalInput").ap()

    x_d = din("x", [NSEQ, S, D])
    mem_d = din("mem", [NSEQ, MEM_LEN, D])
    pos_d = din("positions", [NSEQ, S], I32)
    w_in_d = din("w_in", [L, D, 7328])
    b_gate_d = din("b_gate", [L, 3, D])
    qn_d = din("mla_q_norm", [L, 384])
    kvn_d = din("mla_kv_norm", [L, 256])
    wuq_d = din("mla_w_uq", [L, 384, 768])
    wukv_d = din("mla_w_ukv", [L, 256, 1024])
    dlam_d = din("diff_lambda", [L, 4, 64])
    subln_d = din("diff_subln", [L, 128])
    wkv_d = din("mem_w_kv", [L, D, 1024])
    wbr_d = din("w_branch", [L, 2048, D])
    wout_d = din("w_out", [L, D, D])
    ln1g_d = din("ln1_g", [L, D])
    ln1b_d = din("ln1_b", [L, D])
    w1_d = din("mlp_w1", [L, D, 4096])
    w2_d = din("mlp_w2", [L, 4096, D])
    ln2g_d = din("ln2_g", [L, D])
    ln2b_d = din("ln2_b", [L, D])
    ident_d = din("c_ident", [128, 128])
    pvec_d = din("c_pvec", [128, 8])
    rot_d_ = din("c_rot", [2, 128, 128])
    y_d = nc.dram_tensor("y", [NSEQ, S, D], F32, kind="ExternalOutput").ap()

    def sb(name, shape, dt):
        return nc.alloc_sbuf_tensor(name, list(shape), dt)

    xhi = sb("xhi", [128, KC, S], BF16)
    xlo = sb("xlo", [128, KC, S], BF16)
    oT = sb("oT", [128, 16, S], BF16)
    memT = sb("memT", [128, KC, MEM_LEN], BF16)
    ident_f = sb("ident_f", [128, 128], F32)
    ones_b = sb("ones_b", [128, 128], BF16)
    onesD = sb("onesD", [128, 128], BF16)
    rot_f = sb("rot_f", [128, 2, 128], F32)
    rot_b = sb("rot_b", [128, 2, 128], BF16)
    pvec = sb("pvec", [128, 8], F32)
    neghalf = sb("neghalf", [128, 512], F32)
    halfpi = sb("halfpi", [128, 1], F32)
    par = sb("par", [128, L, 64], F32)
    bgh = par[:, :, 0:24]
    gq = par[:, :, 24:27]
    gkv = par[:, :, 27:29]
    gsub = par[:, :, 29]
    l1g = par[:, :, 30:38]
    l1b = par[:, :, 38:46]
    l2g = par[:, :, 46:54]
    l2b = par[:, :, 54:62]
    dls = sb("dls", [128, L, 2], F32)
    nlam = sb("nlam", [128, L], F32)
    R_const = fw.res("const")
    R_par = fw.res("par")
    R_memT = fw.res("memT")
    R_X = [[fw.res("x%d_%d" % (c, t)) for t in range(TB)] for c in range(KC)]
    R_O = [[fw.res("o%d_%d" % (c, t)) for t in range(TB)] for c in range(16)]

    AR = (nc.sbuf_bytes_remaining - 2048) // 2 // 16 * 16
    arena = sb("arena", [128, AR], BF16)

    class Carver:
        def __init__(self):
            self.off = 0

        def bf(self, n):
            v = arena[:, self.off:self.off + n]
            self.off += n
            assert self.off <= AR, ("arena overflow", self.off, AR)
            return v

        def f32(self, n):
            return self.bf(2 * n).bitcast(F32)

    psb = [nc.alloc_psum_tensor("ps%d" % i, [128, 512], F32) for i in range(8)]
    PS = [Buf(psb[i], fw.res("ps%d" % i, excl=True)) for i in range(8)]

    def V(fn, reads, writes):
        fw.op("vector", fn, reads, writes)

    def A(fn, reads, writes):
        fw.op("scalar", fn, reads, writes)

    def G(fn, reads, writes):
        fw.op("gpsimd", fn, reads, writes)

    def mm(out_ap, pairs, reads, wres):
        n = len(pairs)
        for i, (l_, r_) in enumerate(pairs):
            fw.op("tensor",
                  lambda e, l_=l_, r_=r_, i=i: e.matmul(out_ap, lhsT=l_, rhs=r_, start=(i == 0), stop=(i == n - 1)),
                  reads, [wres])

    def load_w(slot, dst_ap, src_ap):
        fw.dma("gpsimd", lambda e: e.dma_start(out=dst_ap, in_=src_ap), slot.key, writes=[slot.res])

    def wsrc(w_ap, r0, r1, c0, c1):
        return w_ap[r0:r1, c0:c1].rearrange("(c p) n -> p c n", p=128)

    def tok(tb):
        return slice(tb * 512, (tb + 1) * 512)

    fw.dma("sync", lambda e: e.dma_start(out=ident_f[:], in_=ident_d), "c0", writes=[R_const])
    fw.dma("sync", lambda e: e.dma_start(out=pvec[:], in_=pvec_d), "c1", writes=[R_const])
    fw.dma("sync", lambda e: e.dma_start(out=rot_f[:], in_=rot_d_.rearrange("r k m -> k r m")), "c2", writes=[R_const])
    V(lambda e: e.tensor_copy(out=rot_b[:], in_=rot_f[:]), [R_const], [R_const])
    V(lambda e: e.memset(ones_b[:], 1.0), [], [R_const])
    V(lambda e: e.memset(onesD[:], 1.0 / 1024.0), [], [R_const])
    V(lambda e: e.memset(neghalf[:], -0.5), [], [R_const])
    V(lambda e: e.memset(halfpi[:], math.pi / 2), [], [R_const])

    car0 = Carver()
    dl = car0.f32(L * 256).rearrange("p (l n) -> p l n", l=L)
    dlp = car0.f32(L * 128).rearrange("p (l n) -> p l n", l=L)

    stage = car0.f32(L * 128).rearrange("p (l n) -> p l n", l=L)
    R_stage = fw.res("stage")
    for l in range(L):
        rows = [(0, b_gate_d[l].rearrange("i (c p) -> (i c) p", p=128), 24),
                (24, qn_d[l].rearrange("(c p) -> c p", p=128), 3),
                (27, kvn_d[l].rearrange("(c p) -> c p", p=128), 2),
                (29, subln_d[l].rearrange("(c p) -> c p", p=128), 1),
                (30, ln1g_d[l].rearrange("(c p) -> c p", p=128), 8),
                (38, ln1b_d[l].rearrange("(c p) -> c p", p=128), 8),
                (46, ln2g_d[l].rearrange("(c p) -> c p", p=128), 8),
                (54, ln2b_d[l].rearrange("(c p) -> c p", p=128), 8)]
        for (r0, src, n) in rows:
            fw.dma("sync", lambda e, r0=r0, src=src, n=n, l=l: e.dma_start(out=stage[r0:r0 + n, l, :], in_=src), "pst", writes=[R_stage])
        fw.dma("sync", lambda e, l=l: e.dma_start(out=dl[:, l, :], in_=dlam_d[l].rearrange("a b -> (a b)").partition_broadcast(128)),
               "p8", writes=[R_par])
    for l in range(L):
        pp = PS[l % 8]
        fw.op("tensor", lambda e, l=l, pp=pp: e.transpose(out=pp.ap[:, 0:62], in_=stage[0:62, l, :], identity=ident_f[0:62, 0:62]),
              [R_stage, R_const], [pp.res])
        V(lambda e, l=l, pp=pp: e.tensor_copy(out=par[:, l, 0:62], in_=pp.ap[:, 0:62]), [pp.res], [R_par])
    for l in range(L):
        lam_init = 0.8 - 0.6 * math.exp(-0.3 * ((layer_ids[l]) if layer_ids is not None else l))
        V(lambda e, l=l: e.tensor_scalar(out=bgh[:, l, :], in0=bgh[:, l, :], scalar1=0.5, scalar2=None, op0=ALU.mult), [R_par], [R_par])
        V(lambda e, l=l: e.tensor_tensor(out=dlp[:, l, 0:64], in0=dl[:, l, 0:64], in1=dl[:, l, 64:128], op=ALU.mult), [R_par], [R_par])
        V(lambda e, l=l: e.tensor_tensor(out=dlp[:, l, 64:128], in0=dl[:, l, 128:192], in1=dl[:, l, 192:256], op=ALU.mult), [R_par], [R_par])
        V(lambda e, l=l: e.reduce_sum(out=dls[:, l, 0:1], in_=dlp[:, l, 0:64], axis=mybir.AxisListType.X), [R_par], [R_par])
        V(lambda e, l=l: e.reduce_sum(out=dls[:, l, 1:2], in_=dlp[:, l, 64:128], axis=mybir.AxisListType.X), [R_par], [R_par])
        A(lambda e, l=l: e.activation(out=dls[:, l, :], in_=dls[:, l, :], func=ACT.Exp), [R_par], [R_par])
        V(lambda e, l=l: e.tensor_tensor(out=nlam[:, l:l + 1], in0=dls[:, l, 1:2], in1=dls[:, l, 0:1], op=ALU.subtract), [R_par], [R_par])
        V(lambda e, l=l, li=lam_init: e.tensor_scalar(out=nlam[:, l:l + 1], in0=nlam[:, l:l + 1], scalar1=-li, scalar2=None, op0=ALU.add), [R_par], [R_par])

    def attn_chunk(car_state, qc, qT, kT, vfn, kb, Kdim, dv, ro, nk, scale, q_reads, k_reads, v_reads, post):
        SC, ACC, PT = car_state
        pv = ACC.next()
        den = ACC.next()

        def scores(kt):
            sc = SC.next()
            mm(sc.ap[:, :], [(kT[kb:kb + Kdim, kt * 128:(kt + 1) * 128], qT[kb:kb + Kdim, tok(qc)])],
               q_reads + k_reads, sc.res)
            return sc

        sc_next = scores(0)
        for kt in range(nk):
            sc = sc_next
            pt = PT.next()
            A(lambda e, sc=sc, pt=pt: e.activation(out=pt.ap, in_=sc.ap[:, :], func=ACT.Exp, scale=scale),
              [sc.res], [pt.res])
            if kt + 1 < nk:
                sc_next = scores(kt + 1)
            fw.op("tensor", lambda e, pt=pt, kt=kt: e.matmul(pv.ap[ro:ro + dv, :], lhsT=vfn(kt), rhs=pt.ap,
                                                             start=(kt == 0), stop=(kt == nk - 1)),
                  [pt.res] + v_reads, [pv.res])
            fw.op("tensor", lambda e, pt=pt, kt=kt: e.matmul(den.ap[ro:ro + dv, :], lhsT=ones_b[:, 0:dv], rhs=pt.ap,
                                                             start=(kt == 0), stop=(kt == nk - 1)),
                  [pt.res, R_const], [den.res])
        post(qc, pv, den)

    def attention(car_state, qT, kT, vfn, kb, Kdim, dv, ro, nk, scale, q_reads, k_reads, v_reads, post):
        for qc in range(TB):
            attn_chunk(car_state, qc, qT, kT, vfn, kb, Kdim, dv, ro, nk, scale, q_reads, k_reads, v_reads, post)

    _csts = {}

    def cst(val):
        key = float(val)
        if key not in _csts:
            t = sb("cst%d" % len(_csts), [128, 1], F32)
            r = fw.res("cst")
            V(lambda e, t=t, key=key: e.memset(t[:], key), [], [r])
            _csts[key] = (t, r)
        return _csts[key]

    def rsqrt_act(dst, src_ap, reads, scale=1.0, bias=None):
        if bias is None:
            A(lambda e: e.activation(out=dst.ap, in_=src_ap, func=ACT.Ln, scale=float(scale)), reads, [dst.res])
        else:
            bt, br = cst(bias)
            A(lambda e: e.activation(out=dst.ap, in_=src_ap, func=ACT.Ln, scale=float(scale), bias=bt[:, 0:1]),
              reads + [br], [dst.res])
        A(lambda e: e.activation(out=dst.ap, in_=dst.ap, func=ACT.Exp, scale=-0.5), [dst.res], [dst.res])

    def load_sequence(s):
        car = Carver()
        xin = [Buf(car.f32(1024), fw.res("xin%d" % i), "xin%d" % i) for i in range(2)]
        ring = Ring(xin)
        pr = Ring(PS)
        for tt in range(TT):
            xb = ring.next()
            fw.dma("sync", lambda e, xb=xb, tt=tt: e.dma_start(out=xb.ap, in_=x_d[s, tt * 128:(tt + 1) * 128, :]),
                   xb.key, writes=[xb.res])
            tb = tt // 4
            for g in range(2):
                p = pr.next()
                for cc in range(4):
                    c = g * 4 + cc
                    fw.op("tensor", lambda e, p=p, xb=xb, c=c, cc=cc: e.transpose(out=p.ap[:, cc * 128:(cc + 1) * 128],
                                                                                 in_=xb.ap[:, c * 128:(c + 1) * 128], identity=ident_f[:]),
                          [xb.res, R_const], [p.res])
                wr = [R_X[g * 4 + cc][tb] for cc in range(4)]
                hi_v = xhi[:, g * 4:g * 4 + 4, tt * 128:(tt + 1) * 128]
                lo_v = xlo[:, g * 4:g * 4 + 4, tt * 128:(tt + 1) * 128]
                pv3 = p.ap[:, :].rearrange("p (a b) -> p a b", a=4)
                A(lambda e, hi_v=hi_v, pv3=pv3: e.activation(out=hi_v, in_=pv3, func=ACT.Copy), [p.res], wr)
                V(lambda e, lo_v=lo_v, hi_v=hi_v, pv3=pv3: e.tensor_tensor(out=lo_v, in0=pv3, in1=hi_v, op=ALU.subtract),
                  [p.res] + wr, wr)
        for mt in range(MEM_LEN // 128):
            xb = ring.next()
            fw.dma("sync", lambda e, xb=xb, mt=mt: e.dma_start(out=xb.ap, in_=mem_d[s, mt * 128:(mt + 1) * 128, :]),
                   xb.key, writes=[xb.res])
            for g in range(2):
                p = pr.next()
                for cc in range(4):
                    c = g * 4 + cc
                    fw.op("tensor", lambda e, p=p, xb=xb, c=c, cc=cc: e.transpose(out=p.ap[:, cc * 128:(cc + 1) * 128],
                                                                                 in_=xb.ap[:, c * 128:(c + 1) * 128], identity=ident_f[:]),
                          [xb.res, R_const], [p.res])
                V(lambda e, p=p, g=g, mt=mt: e.tensor_copy(out=memT[:, g * 4:g * 4 + 4, mt * 128:(mt + 1) * 128],
                                                          in_=p.ap[:, :].rearrange("p (a b) -> p a b", a=4)), [p.res], [R_memT])

    tab_d = nc.dram_tensor("tab_scratch", [4, 128, S], BF16, kind="Internal").ap()
    R_tabd = fw.res("tabd")

    def build_tables(s):
        car = Carver()
        posi = car.bf(2 * S).bitcast(I32)
        posf = car.f32(S)
        ang = car.f32(S)
        t1 = car.f32(S)
        t2 = car.f32(S)
        ki = car.bf(2 * S).bitcast(I32)
        tb16 = car.bf(S)
        R_t = fw.res("tabtmp")
        fw.dma("sync", lambda e: e.dma_start(out=posi, in_=pos_d[s, :].partition_broadcast(128)), "posi", writes=[R_t])
        V(lambda e: e.tensor_copy(out=posf, in_=posi), [R_t], [R_t])
        inv2pi = float(1.0 / (2 * math.pi))
        for ti, (fcol, scol) in enumerate([(0, 1), (2, 3)]):
            V(lambda e, fcol=fcol: e.tensor_scalar(out=ang, in0=posf, scalar1=pvec[:, fcol:fcol + 1], scalar2=None, op0=ALU.mult),
              [R_t, R_const], [R_t])
            for which in range(2):
                if which == 0:
                    V(lambda e: e.tensor_scalar(out=t1, in0=ang, scalar1=halfpi[:, 0:1], scalar2=None, op0=ALU.add), [R_t, R_const], [R_t])
                    src = t1
                else:
                    src = ang
                V(lambda e, src=src: e.tensor_scalar(out=t2, in0=src, scalar1=inv2pi, scalar2=None, op0=ALU.mult), [R_t], [R_t])
                V(lambda e: e.tensor_copy(out=ki, in_=t2), [R_t], [R_t])
                V(lambda e: e.tensor_copy(out=t2, in_=ki), [R_t], [R_t])
                V(lambda e, src=src: e.scalar_tensor_tensor(out=t2, in0=t2, scalar=float(-2 * math.pi), in1=src, op0=ALU.mult, op1=ALU.add),
                  [R_t], [R_t])
                A(lambda e: e.activation(out=t2, in_=t2, func=ACT.Sin), [R_t], [R_t])
                if which == 0:
                    V(lambda e: e.tensor_copy(out=tb16, in_=t2), [R_t], [R_t])
                else:
                    V(lambda e, scol=scol: e.tensor_scalar(out=tb16, in0=t2, scalar1=pvec[:, scol:scol + 1], scalar2=None, op0=ALU.mult),
                      [R_t, R_const], [R_t])
                idx = ti * 2 + which
                fw.dma("sync", lambda e, idx=idx: e.dma_start(out=tab_d[idx], in_=tb16), "tabst", reads=[R_t], writes=[R_tabd])

    def layer(s, l, last):
        lid = layer_ids[l] if layer_ids is not None else l
        lam_init = 0.8 - 0.6 * math.exp(-0.3 * lid)
        fw.barrier()
        car = Carver()
        tabs = car.bf(4 * S).rearrange("p (a b) -> p a b", a=4)
        R_tab = fw.res("tab")
        Cm, Sm, Cd, Sd = tabs[:, 0, :], tabs[:, 1, :], tabs[:, 2, :], tabs[:, 3, :]
        fw.dma("sync", lambda e: e.dma_start(out=tabs, in_=tab_d.rearrange("a p s -> p a s")), "tabld", reads=[R_tabd], writes=[R_tab])
        WS = Ring([Buf(car.bf(4096), fw.res("w%d" % i), "w%d" % i) for i in range(2)])
        qTt = car.bf(S)
        kTt = car.bf(S)
        vt = car.bf(TT * 128).rearrange("p (a b) -> p a b", a=TT)
        R_q, R_k, R_v = fw.res("q"), fw.res("k"), fw.res("v")
        PT = Ring([Buf(car.bf(512), fw.res("pt%d" % i)) for i in range(3)])
        TF = Ring([Buf(car.f32(512), fw.res("tf%d" % i)) for i in range(6)])
        TBf = Ring([Buf(car.bf(512), fw.res("tb%d" % i)) for i in range(3)])
        kpe = car.bf(max(S, 2048))
        R_kpe = fw.res("kpe")
        SC = Ring(PS[0:2])
        ACC = Ring(PS[2:6])
        GEN = Ring(PS[6:8])
        car_state = (SC, ACC, PT)
        USE["attn"] = car.off
        cqn = lambda j, tb: oT[:, 4 + j, tok(tb)]
        ckvn = lambda j, tb: oT[:, 7 + j, tok(tb)]
        R_cq = lambda j, tb: R_O[4 + j][tb]
        R_ckv = lambda j, tb: R_O[7 + j][tb]

        def rope_rows(dst_ap, a_ps, rows, tabC, tabS, rotsel, tsl, reads, wres):
            ab = TBf.next()
            A(lambda e: e.activation(out=ab.ap[rows, :], in_=a_ps.ap[rows, :], func=ACT.Copy), [a_ps.res], [ab.res])
            bp = GEN.next()
            mm(bp.ap[:, :], [(rot_b[:, rotsel, :], ab.ap)], [ab.res, R_const], bp.res)
            t1 = TF.next()
            V(lambda e: e.tensor_tensor(out=t1.ap[rows, :], in0=a_ps.ap[rows, :], in1=tabC[rows, tsl], op=ALU.mult),
              [a_ps.res, R_tab], [t1.res])
            t2 = TF.next()
            V(lambda e: e.tensor_tensor(out=t2.ap[rows, :], in0=bp.ap[rows, :], in1=tabS[rows, tsl], op=ALU.mult),
              [bp.res, R_tab], [t2.res])
            V(lambda e: e.tensor_tensor(out=dst_ap, in0=t1.ap[rows, :], in1=t2.ap[rows, :], op=ALU.add),
              [t1.res, t2.res] + reads, [wres])

        for b_ in TBf.items:
            V(lambda e, b_=b_: e.memset(b_.ap, 0.0), [], [b_.res])

        ws = WS.next()
        wcq = ws.ap[:, 0:KC * 384].rearrange("p (c n) -> p c n", c=KC)
        load_w(ws, wcq, wsrc(w_in_d[l], 0, D, C_CQ, C_CQ + 384))
        ws2 = WS.next()
        wck = ws2.ap[:, 0:KC * 288].rearrange("p (c n) -> p c n", c=KC)
        load_w(ws2, wck, wsrc(w_in_d[l], 0, D, C_CKV, C_CKV + 288))
        for tb in range(TB):
            xr = [R_X[k][tb] for k in range(KC)]
            for (wt, wsl, nj, cols, gvec, dstf, rdst, eps, nfe) in (
                    (wcq, ws, 3, 0, gq, cqn, R_cq, 1e-6, 384.0),
                    (wck, ws2, 2, 0, gkv, ckvn, R_ckv, 1e-6, 256.0)):
                pj = []
                sqs = []
                for j in range(nj):
                    p = ACC.next()
                    pj.append(p)
                    mm(p.ap[:, :], [(wt[:, k, cols + j * 128:cols + (j + 1) * 128], xhi[:, k, tok(tb)]) for k in range(KC)],
                       xr + [wsl.res], p.res)
                    sq = PT.next()
                    A(lambda e, p=p, sq=sq: e.activation(out=sq.ap, in_=p.ap[:, :], func=ACT.Square), [p.res], [sq.res])
                    sqs.append(sq)
                pss = GEN.next()
                mm(pss.ap[:, :], [(ones_b[:, :], sq.ap) for sq in sqs], [sq.res for sq in sqs] + [R_const], pss.res)
                rs = TF.next()
                rsqrt_act(rs, pss.ap[:, :], [pss.res], scale=1.0 / nfe, bias=eps)
                for j in range(nj):
                    V(lambda e, j=j, p=pj[j], rs=rs, gvec=gvec, dap=dstf(j, tb): e.scalar_tensor_tensor(
                        out=dap, in0=p.ap[:, :], scalar=gvec[:, l, j:j + 1], in1=rs.ap, op0=ALU.mult, op1=ALU.mult),
                      [pj[j].res, rs.res, R_par], [rdst(j, tb)])
            p = ACC.next()
            mm(p.ap[64:96, :], [(wck[:, k, 256:288], xhi[:, k, tok(tb)]) for k in range(KC)], xr + [ws2.res], p.res)
            rope_rows(kpe[64:96, tok(tb)], p, slice(64, 96), Cm, Sm, 0, tok(tb), [], R_kpe)
        for h in range(8):
            ws = WS.next()
            wq = ws.ap[:, 0:3 * 96].rearrange("p (c n) -> p c n", c=3)
            wkv_ = ws.ap[:, 512:512 + 2 * 128].rearrange("p (c n) -> p c n", c=2)
            load_w(ws, wq, wsrc(wuq_d[l], 0, 384, h * 96, (h + 1) * 96))
            load_w(ws, wkv_, wsrc(wukv_d[l], 0, 256, h * 128, (h + 1) * 128))
            for tb in range(TB):
                cqr = [R_cq(j, tb) for j in range(3)]
                ckr = [R_ckv(j, tb) for j in range(2)]
                p = GEN.next()
                mm(p.ap[0:96, :], [(wq[:, j, :], cqn(j, tb)) for j in range(3)], cqr + [ws.res], p.res)
                V(lambda e, p=p, tb=tb: e.tensor_copy(out=qTt[0:64, tok(tb)], in_=p.ap[0:64, :]), [p.res], [R_q])
                rope_rows(qTt[64:96, tok(tb)], p, slice(64, 96), Cm, Sm, 0, tok(tb), [], R_q)
                p2 = GEN.next()
                mm(p2.ap[0:64, :], [(wkv_[:, j, 0:64], ckvn(j, tb)) for j in range(2)], ckr + [ws.res], p2.res)
                V(lambda e, p2=p2, tb=tb: e.tensor_copy(out=kTt[0:64, tok(tb)], in_=p2.ap[0:64, :]), [p2.res], [R_k])
                V(lambda e, tb=tb: e.tensor_copy(out=kTt[64:96, tok(tb)], in_=kpe[64:96, tok(tb)]), [R_kpe], [R_k])
                p3 = GEN.next()
                for t4 in range(4):
                    tsl = slice(tb * 512 + t4 * 128, tb * 512 + (t4 + 1) * 128)
                    mm(p3.ap[:, t4 * 64:(t4 + 1) * 64], [(oT[:, 7 + j, tsl], wkv_[:, j, 64:128]) for j in range(2)], ckr + [ws.res], p3.res)
                V(lambda e, p3=p3, tb=tb: e.tensor_copy(out=vt[:, tb * 4:(tb + 1) * 4, 0:64],
                                                        in_=p3.ap[:, 0:256].rearrange("p (a b) -> p a b", a=4)), [p3.res], [R_v])
            ro = (h % 2) * 64

            def post_mla(qc, pv, den, h=h, ro=ro):
                r = TF.next()
                V(lambda e: e.reciprocal(out=r.ap[ro:ro + 64, :], in_=den.ap[ro:ro + 64, :]), [den.res], [r.res])
                V(lambda e: e.tensor_tensor(out=oT[ro:ro + 64, h // 2, tok(qc)], in0=pv.ap[ro:ro + 64, :], in1=r.ap[ro:ro + 64, :], op=ALU.mult),
                  [pv.res, r.res], [R_O[h // 2][qc]])

            attention(car_state, qTt, kTt, lambda kt: vt[:, kt, 0:64], 0, 96, 64, ro, TT, 96.0 ** -0.5,
                      [R_q], [R_k], [R_v], post_mla)

        if upto < 4:
            return
        k1 = 1.0 / (128.0 * (1.0 - lam_init) ** 2)
        k2 = 1e-5 / ((1.0 - lam_init) ** 2)
        for h in range(8):
            ws = WS.next()
            w3 = ws.ap[:, 0:KC * 384].rearrange("p (c n) -> p c n", c=KC)
            for i3, c0 in enumerate((C_DQ, C_DK, C_DV)):
                load_w(ws, w3[:, :, i3 * 128:(i3 + 1) * 128], wsrc(w_in_d[l], 0, D, c0 + h * 128, c0 + (h + 1) * 128))
            for tb in range(TB):
                xr = [R_X[k][tb] for k in range(KC)]
                for i3, (dst, rdst) in enumerate(((qTt, R_q), (kTt, R_k))):
                    p = GEN.next()
                    mm(p.ap[:, :], [(w3[:, k, i3 * 128:(i3 + 1) * 128], xhi[:, k, tok(tb)]) for k in range(KC)], xr + [ws.res], p.res)
                    rope_rows(dst[:, tok(tb)], p, slice(0, 128), Cd, Sd, 1, tok(tb), [], rdst)
                p3 = GEN.next()
                for t4 in range(4):
                    tsl = slice(tb * 512 + t4 * 128, tb * 512 + (t4 + 1) * 128)
                    mm(p3.ap[:, t4 * 128:(t4 + 1) * 128], [(xhi[:, k, tsl], w3[:, k, 256:384]) for k in range(KC)], xr + [ws.res], p3.res)
                for t4 in range(4):
                    V(lambda e, p3=p3, tb=tb, t4=t4: e.tensor_copy(out=vt[:, tb * 4 + t4, :], in_=p3.ap[:, t4 * 128:(t4 + 1) * 128]), [p3.res], [R_v])
            keep = {}

            def post_d0(qc, pv, den):
                r0 = TF.next()
                V(lambda e: e.reciprocal(out=r0.ap, in_=den.ap[:, :]), [den.res], [r0.res])
                a_ = TF.next()
                V(lambda e: e.tensor_tensor(out=a_.ap, in0=pv.ap[:, :], in1=r0.ap, op=ALU.mult), [pv.res, r0.res], [a_.res])
                keep[qc] = a_

            def post_d1(qc, pv, den, h=h):
                a_ = keep[qc]
                r1 = TF.next()
                V(lambda e: e.reciprocal(out=r1.ap, in_=den.ap[:, :]), [den.res], [r1.res])
                b_ = TF.next()
                V(lambda e: e.scalar_tensor_tensor(out=b_.ap, in0=pv.ap[:, :], scalar=nlam[:, l:l + 1], in1=r1.ap,
                                                   op0=ALU.mult, op1=ALU.mult), [pv.res, r1.res, R_par], [b_.res])
                V(lambda e: e.tensor_tensor(out=a_.ap, in0=a_.ap, in1=b_.ap, op=ALU.add), [a_.res, b_.res], [a_.res])
                sq = PT.next()
                A(lambda e: e.activation(out=sq.ap, in_=a_.ap, func=ACT.Square), [a_.res], [sq.res])
                pss = GEN.next()
                mm(pss.ap[:, :], [(ones_b[:, :], sq.ap)], [sq.res, R_const], pss.res)
                vv = TF.next()
                rsqrt_act(vv, pss.ap[:, :], [pss.res], scale=k1, bias=k2)
                V(lambda e: e.scalar_tensor_tensor(out=oT[:, 4 + h, tok(qc)], in0=a_.ap, scalar=par[:, l, 29:30],
                                                   in1=vv.ap, op0=ALU.mult, op1=ALU.mult),
                  [a_.res, vv.res, R_par], [R_O[4 + h][qc]])

            for qc in range(TB):
                for cmap, pst in ((0, post_d0), (1, post_d1)):
                    attn_chunk(car_state, qc, qTt, kTt, lambda kt: vt[:, kt, :], cmap * 64, 64, 128, 0, TT, 64.0 ** -0.5,
                               [R_q], [R_k], [R_v], pst)

        if upto < 5:
            return
        Km = kpe[:, 0:4 * MEM_LEN].rearrange("p (a b) -> p a b", a=4)
        Vm = kpe[:, 4 * MEM_LEN:4 * MEM_LEN + 2 * 512].rearrange("p (a b) -> p a b", a=2)
        for half in range(2):
            ws = WS.next()
            wk_ = ws.ap[:, 0:KC * 512].rearrange("p (c n) -> p c n", c=KC)
            load_w(ws, wk_, wsrc(wkv_d[l], 0, D, half * 512, (half + 1) * 512))
            if half == 0:
                for hh in range(4):
                    p = GEN.next()
                    mm(p.ap[:, 0:MEM_LEN], [(wk_[:, k, hh * 128:(hh + 1) * 128], memT[:, k, :]) for k in range(KC)], [R_memT, ws.res], p.res)
                    V(lambda e, p=p, hh=hh: e.tensor_copy(out=Km[:, hh, :], in_=p.ap[:, 0:MEM_LEN]), [p.res], [R_kpe])
            else:
                for mt in range(2):
                    p = GEN.next()
                    mm(p.ap[:, :], [(memT[:, k, mt * 128:(mt + 1) * 128], wk_[:, k, :]) for k in range(KC)], [R_memT, ws.res], p.res)
                    V(lambda e, p=p, mt=mt: e.tensor_copy(out=Vm[:, mt, :], in_=p.ap[:, :]), [p.res], [R_kpe])
        ws = WS.next()
        wmq = ws.ap[:, 0:KC * 512].rearrange("p (c n) -> p c n", c=KC)
        load_w(ws, wmq, wsrc(w_in_d[l], 0, D, C_MQ, C_MQ + 512))
        for hh in range(4):
            for tb in range(TB):
                xr = [R_X[k][tb] for k in range(KC)]
                p = GEN.next()
                mm(p.ap[:, :], [(wmq[:, k, hh * 128:(hh + 1) * 128], xhi[:, k, tok(tb)]) for k in range(KC)], xr + [ws.res], p.res)
                V(lambda e, p=p, tb=tb: e.tensor_copy(out=qTt[:, tok(tb)], in_=p.ap[:, :]), [p.res], [R_q])

            def post_mem(qc, pv, den, hh=hh):
                r = TF.next()
                V(lambda e: e.reciprocal(out=r.ap, in_=den.ap[:, :]), [den.res], [r.res])
                V(lambda e: e.tensor_tensor(out=oT[:, 12 + hh, tok(qc)], in0=pv.ap[:, :], in1=r.ap, op=ALU.mult),
                  [pv.res, r.res], [R_O[12 + hh][qc]])

            attention(car_state, qTt, Km[:, hh, :], lambda kt, hh=hh: Vm[:, kt, hh * 128:(hh + 1) * 128], 0, 128, 128, 0, 2,
                      128.0 ** -0.5, [R_q], [R_kpe], [R_kpe], post_mem)

        if upto < 6:
            return
        fw.barrier()
        car = Carver()
        WS = Ring([Buf(car.bf(3072), fw.res("wm%d" % i), "wm%d" % i) for i in range(3)])
        merged = car.bf(KC * HB).rearrange("p (c n) -> p c n", c=KC)
        R_m = [[fw.res("m%d_%d" % (c, sbk)) for sbk in range(SBH)] for c in range(KC)]
        zt = [car.f32(KC * 512).rearrange("p (c n) -> p c n", c=KC) for _ in range(1)]
        R_z = [[fw.res("z%d_%d" % (i, c)) for c in range(KC)] for i in range(1)]
        TF = Ring([Buf(car.f32(512), fw.res("tf%d" % i)) for i in range(6)])
        ZB = Ring([Buf(car.bf(512), fw.res("zb%d" % i)) for i in range(4)])
        GEN = Ring(PS[0:6])
        STAT = PS[6:8]
        USE["merge"] = car.off
        br_ranges = ((0, 4), (4, 12), (12, 16))

        def layernorm(tb, zi, eps, gvec, bvec, final_out):
            z = zt[zi]
            s1, s2 = STAT
            for c in range(KC):
                zb = ZB.next()
                A(lambda e, zb=zb, c=c: e.activation(out=zb.ap, in_=z[:, c, :], func=ACT.Copy), [R_z[zi][c]], [zb.res])
                fw.op("tensor", lambda e, zb=zb, c=c: e.matmul(s1.ap[:, :], lhsT=onesD[:, :], rhs=zb.ap, start=(c == 0), stop=(c == KC - 1)),
                      [zb.res, R_const], [s1.res])
                zq = ZB.next()
                A(lambda e, zq=zq, c=c: e.activation(out=zq.ap, in_=z[:, c, :], func=ACT.Square), [R_z[zi][c]], [zq.res])
                fw.op("tensor", lambda e, zq=zq, c=c: e.matmul(s2.ap[:, :], lhsT=onesD[:, :], rhs=zq.ap, start=(c == 0), stop=(c == KC - 1)),
                      [zq.res, R_const], [s2.res])
            msq = TF.next()
            A(lambda e: e.activation(out=msq.ap, in_=s1.ap[:, :], func=ACT.Square), [s1.res], [msq.res])
            vv = TF.next()
            V(lambda e: e.scalar_tensor_tensor(out=vv.ap, in0=s2.ap[:, :], scalar=float(eps), in1=msq.ap, op0=ALU.add, op1=ALU.subtract),
              [s2.res, msq.res], [vv.res])
            rsqrt_act(vv, vv.ap, [vv.res])
            nmr = TF.next()
            V(lambda e: e.scalar_tensor_tensor(out=nmr.ap, in0=s1.ap[:, :], scalar=-1.0, in1=vv.ap, op0=ALU.mult, op1=ALU.mult),
              [s1.res, vv.res], [nmr.res])
            for c in range(KC):
                rz = R_z[zi][c]
                V(lambda e, c=c: e.tensor_tensor(out=z[:, c, :], in0=z[:, c, :], in1=vv.ap, op=ALU.mult), [rz, vv.res], [rz])
                V(lambda e, c=c: e.tensor_tensor(out=z[:, c, :], in0=z[:, c, :], in1=nmr.ap, op=ALU.add), [rz, nmr.res], [rz])
                A(lambda e, c=c: e.activation(out=z[:, c, :], in_=z[:, c, :], func=ACT.Identity, bias=bvec[:, l, c:c + 1],
                                              scale=gvec[:, l, c:c + 1]), [rz, R_par], [rz])
                rx = R_X[c][tb]
                A(lambda e, c=c: e.activation(out=xhi[:, c, tok(tb)], in_=z[:, c, :], func=ACT.Copy), [rz], [rx])
                V(lambda e, c=c: e.tensor_tensor(out=xlo[:, c, tok(tb)], in0=z[:, c, :], in1=xhi[:, c, tok(tb)], op=ALU.subtract),
                  [rz, rx], [rx])
            if final_out:
                for t4 in range(4):
                    ob = OUTB.next()
                    for g in range(2):
                        p = GEN.next()
                        for cc in range(4):
                            c = g * 4 + cc
                            fw.op("tensor", lambda e, p=p, c=c, cc=cc, t4=t4: e.transpose(out=p.ap[:, cc * 128:(cc + 1) * 128],
                                                                                         in_=z[:, c, t4 * 128:(t4 + 1) * 128], identity=ident_f[:]),
                                  [R_z[zi][c], R_const], [p.res])
                        A(lambda e, p=p, ob=ob, g=g: e.activation(out=ob.ap[:, g * 512:(g + 1) * 512], in_=p.ap[:, :], func=ACT.Copy), [p.res], [ob.res])
                    t0 = tb * 512 + t4 * 128
                    ry = fw.res("y")
                    R_y.append(ry)
                    fw.dma("sync", lambda e, ob=ob, t0=t0: e.dma_start(out=y_d[s, t0:t0 + 128, :], in_=ob.ap), ob.key,
                           reads=[ob.res], writes=[ry])

        for th in range(NH):
            for c in range(KC):
                wsg = WS.next()
                wg = wsg.ap[:, 0:KC * 384].rearrange("p (k n) -> p k n", k=KC)
                for i in range(3):
                    c0 = C_GATE + i * D + c * 128
                    load_w(wsg, wg[:, :, i * 128:(i + 1) * 128], wsrc(w_in_d[l], 0, D, c0, c0 + 128))
                wsb = WS.next()
                wb = wsb.ap[:, 0:16 * 128].rearrange("p (k n) -> p k n", k=16)
                load_w(wsb, wb, wsrc(wbr_d[l], 0, 2048, c * 128, (c + 1) * 128))
                for sbk in range(SBH):
                    tb = th * SBH + sbk
                    xr = [R_X[k][tb] for k in range(KC)]
                    us = []
                    for i in range(3):
                        pg = GEN.next()
                        mm(pg.ap[:, :], [(wg[:, k, i * 128:(i + 1) * 128], xhi[:, k, tok(tb)]) for k in range(KC)], xr + [wsg.res], pg.res)
                        t_ = TF.next()
                        A(lambda e, pg=pg, t_=t_, i=i, c=c: e.activation(out=t_.ap, in_=pg.ap[:, :], func=ACT.Tanh,
                                                                        bias=bgh[:, l, i * 8 + c:i * 8 + c + 1], scale=0.5),
                          [pg.res, R_par], [t_.res])
                        pb = GEN.next()
                        k0, k1_ = br_ranges[i]
                        mm(pb.ap[:, :], [(wb[:, kk, :], oT[:, kk, tok(tb)]) for kk in range(k0, k1_)],
                           [R_O[kk][tb] for kk in range(k0, k1_)] + [wsb.res], pb.res)
                        V(lambda e, t_=t_, pb=pb: e.scalar_tensor_tensor(out=t_.ap, in0=t_.ap, scalar=1.0, in1=pb.ap[:, :],
                                                                        op0=ALU.add, op1=ALU.mult), [t_.res, pb.res], [t_.res])
                        us.append(t_)
                    V(lambda e, us=us: e.tensor_tensor(out=us[0].ap, in0=us[0].ap, in1=us[1].ap, op=ALU.add),
                      [us[0].res, us[1].res], [us[0].res])
                    V(lambda e, us=us, c=c, sbk=sbk: e.tensor_tensor(out=merged[:, c, sbk * 512:(sbk + 1) * 512], in0=us[0].ap, in1=us[2].ap, op=ALU.add),
                      [us[0].res, us[2].res], [R_m[c][sbk]])
            for sbk in range(SBH):
                tb = th * SBH + sbk
                for c2 in range(KC):
                    wso = WS.next()
                    wo = wso.ap[:, 0:KC * 128].rearrange("p (k n) -> p k n", k=KC)
                    load_w(wso, wo, wsrc(wout_d[l], 0, D, c2 * 128, (c2 + 1) * 128))
                    py = GEN.next()
                    mm(py.ap[:, :], [(wo[:, k, :], merged[:, k, sbk * 512:(sbk + 1) * 512]) for k in range(KC)],
                       [R_m[k][sbk] for k in range(KC)] + [wso.res], py.res)
                    rz = R_z[0][c2]
                    V(lambda e, py=py, c2=c2, tb=tb, z0=zt[0]: e.scalar_tensor_tensor(out=z0[:, c2, :], in0=xhi[:, c2, tok(tb)], scalar=2.0 * ALPHA,
                                                                           in1=py.ap[:, :], op0=ALU.mult, op1=ALU.add),
                      [py.res, R_X[c2][tb]], [rz])
                    V(lambda e, c2=c2, tb=tb, z0=zt[0]: e.scalar_tensor_tensor(out=z0[:, c2, :], in0=xlo[:, c2, tok(tb)], scalar=2.0 * ALPHA,
                                                                    in1=z0[:, c2, :], op0=ALU.mult, op1=ALU.add),
                      [R_X[c2][tb], rz], [rz])
                layernorm(tb, 0, 4.0 * LN_EPS, l1g, l1b, False)

        if upto < 7:
            return
        fw.barrier()
        car = Carver()
        W1 = Ring([Buf(car.bf(1024), fw.res("wa%d" % i), "wa%d" % i) for i in range(3)])
        W2 = Ring([Buf(car.bf(2048), fw.res("wb%d" % i), "wb%d" % i) for i in range(2)])
        zt = [car.f32(KC * 512).rearrange("p (c n) -> p c n", c=KC) for _ in range(SBH)]
        R_z = [[fw.res("zf%d_%d" % (i, c)) for c in range(KC)] for i in range(SBH)]
        TF = Ring([Buf(car.f32(512), fw.res("tg%d" % i)) for i in range(6)])
        ZB = Ring([Buf(car.bf(512), fw.res("zc%d" % i)) for i in range(4)])
        OUTB = Ring([Buf(car.f32(1024), fw.res("ob%d" % i), "ob%d" % i) for i in range(1)]) if last else None
        GEN = Ring(PS[0:6])
        STAT = PS[6:8]
        USE["ffn"] = car.off
        h1 = oT[:, :, :].rearrange("p c s -> p (c s)")[:, 0:32 * HB].rearrange("p (j t) -> p j t", j=32)

        def R_h(j, sbk):
            flat = j * HB + sbk * 512
            return R_O[flat // S][(flat % S) // 512]

        if KSTOP <= 0:
            return
        for th in range(NH):
            for j in range(32):
                ws = W1.next()
                w1c = ws.ap[:, 0:KC * 128].rearrange("p (k n) -> p k n", k=KC)
                load_w(ws, w1c, wsrc(w1_d[l], 0, D, j * 128, (j + 1) * 128))
                for sbk in range(SBH):
                    tb = th * SBH + sbk
                    xr = [R_X[k][tb] for k in range(KC)]
                    ph = GEN.next()
                    mm(ph.ap[:, :], [(w1c[:, k, :], xhi[:, k, tok(tb)]) for k in range(KC)], xr + [ws.res], ph.res)
                    r_ = TF.next()
                    V(lambda e, ph=ph, r_=r_: e.tensor_scalar(out=r_.ap, in0=ph.ap[:, :], scalar1=0.0, scalar2=None, op0=ALU.max), [ph.res], [r_.res])
                    A(lambda e, r_=r_, j=j, sbk=sbk: e.activation(out=h1[:, j, sbk * 512:(sbk + 1) * 512], in_=r_.ap, func=ACT.Square),
                      [r_.res], [R_h(j, sbk)])
                    if KDBG == 78 and j == 0 and sbk == 0 and th == 0:
                        dr = nc.dram_tensor("dbg_r", [128, 512], F32, kind="ExternalOutput").ap()
                        dw = nc.dram_tensor("dbg_w", [128, KC, 128], BF16, kind="ExternalOutput").ap()
                        dxx = nc.dram_tensor("dbg_x", [128, KC, 512], BF16, kind="ExternalOutput").ap()
                        dp = nc.dram_tensor("dbg_p", [128, 512], F32, kind="ExternalOutput").ap()
                        for nm_, dst_, src_, rd_ in (("r", dr, r_.ap, [r_.res]), ("w", dw, w1c, [ws.res]), ("x", dxx, xhi[:, :, tok(tb)], xr)):
                            rr_ = fw.res("dbg" + nm_)
                            R_y.append(rr_)
                            fw.dma("sync", lambda e, dst_=dst_, src_=src_: e.dma_start(out=dst_, in_=src_), "dbgk" + nm_, reads=rd_, writes=[rr_])
                        ph2 = GEN.next()
                        mm(ph2.ap[:, :], [(w1c[:, k, :], xhi[:, k, tok(tb)]) for k in range(KC)], xr + [ws.res], ph2.res)
                        r2_ = TF.next()
                        A(lambda e, ph2=ph2, r2_=r2_: e.activation(out=r2_.ap, in_=ph2.ap[:, :], func=ACT.Copy), [ph2.res], [r2_.res])
                        rr_ = fw.res("dbgp")
                        R_y.append(rr_)
                        fw.dma("sync", lambda e, dp=dp, r2_=r2_: e.dma_start(out=dp, in_=r2_.ap), "dbgkp", reads=[r2_.res], writes=[rr_])
            if KSTOP <= 1:
                return
            for c2 in range(KC):
                wsa = W2.next()
                w2a = wsa.ap[:, 0:16 * 128].rearrange("p (k n) -> p k n", k=16)
                load_w(wsa, w2a, wsrc(w2_d[l], 0, 2048, c2 * 128, (c2 + 1) * 128))
                wsb2 = W2.next()
                w2b = wsb2.ap[:, 0:16 * 128].rearrange("p (k n) -> p k n", k=16)
                load_w(wsb2, w2b, wsrc(w2_d[l], 2048, 4096, c2 * 128, (c2 + 1) * 128))
                for sbk in range(SBH):
                    tb = th * SBH + sbk
                    pf = GEN.next()
                    mm(pf.ap[:, :], [((w2a if j < 16 else w2b)[:, j % 16, :], h1[:, j, sbk * 512:(sbk + 1) * 512]) for j in range(32)],
                       [R_h(j, sbk) for j in range(32)] + [wsa.res, wsb2.res], pf.res)
                    rz = R_z[sbk][c2]
                    V(lambda e, pf=pf, c2=c2, tb=tb, zs=zt[sbk]: e.scalar_tensor_tensor(out=zs[:, c2, :], in0=xhi[:, c2, tok(tb)], scalar=ALPHA,
                                                                                    in1=pf.ap[:, :], op0=ALU.mult, op1=ALU.add),
                      [pf.res, R_X[c2][tb]], [rz])
                    V(lambda e, c2=c2, tb=tb, zs=zt[sbk]: e.scalar_tensor_tensor(out=zs[:, c2, :], in0=xlo[:, c2, tok(tb)], scalar=ALPHA,
                                                                             in1=zs[:, c2, :], op0=ALU.mult, op1=ALU.add),
                      [R_X[c2][tb], rz], [rz])
            if KSTOP <= 2:
                return
            for sbk in range(SBH):
                layernorm(th * SBH + sbk, sbk, LN_EPS, l2g, l2b, last)

    R_y = []
    USE = {}
    for s in range(NSEQ):
        fw.barrier()
        if upto >= 1:
            load_sequence(s)
        fw.barrier()
        if upto >= 2:
            build_tables(s)
        if upto >= 3:
            for l in range(NLAYER):
                layer(s, l, last=(l == NLAYER - 1))
    if KDBG == 77:
        dx = nc.dram_tensor("dbg_xhi", [128, KC, S], BF16, kind="ExternalOutput").ap()
        do = nc.dram_tensor("dbg_oT", [128, 16, S], BF16, kind="ExternalOutput").ap()
        fw.barrier()
        r1_, r2_ = fw.res("d1"), fw.res("d2")
        fw.dma("sync", lambda e: e.dma_start(out=dx, in_=xhi[:]), "dbg1", reads=[R_X[c][t] for c in range(KC) for t in range(TB)], writes=[r1_])
        fw.dma("sync", lambda e: e.dma_start(out=do, in_=oT[:]), "dbg2", reads=[R_O[c][t] for c in range(16) for t in range(TB)], writes=[r2_])
        R_y = R_y + [r1_, r2_]
    fw.op("sync", lambda e: e.nop(), reads=R_y, writes=[])
    counts = fw.finish()
    counts["arena"] = AR
    counts.update(USE)
    counts["nops"] = len(fw.ops)
    return nc, counts


_CACHE = {}
WEIGHT_KEYS = ["w_in", "b_gate", "mla_q_norm", "mla_kv_norm", "mla_w_uq", "mla_w_ukv", "diff_lambda",
               "diff_subln", "mem_w_kv", "w_branch", "w_out", "ln1_g", "ln1_b", "mlp_w1", "mlp_w2", "ln2_g", "ln2_b"]


def kernel(**inputs):
    x = np.ascontiguousarray(np.asarray(inputs["x"], dtype=np.float32))
    mem = np.ascontiguousarray(np.asarray(inputs["mem"], dtype=np.float32))
    pos = np.ascontiguousarray(np.asarray(inputs["positions"], dtype=np.int32))
    B, S, _ = x.shape
    nseq = B // N_CORES
    key = (nseq, S, DEPTH)
    if key not in _CACHE:
        _CACHE[key] = build_program(nseq, S, DEPTH)[0]
    nc = _CACHE[key]
    consts = _host_consts()
    shared = {k: np.ascontiguousarray(np.asarray(inputs[k], dtype=np.float32)) for k in WEIGHT_KEYS}
    shared["c_ident"] = consts["ident"]
    shared["c_pvec"] = consts["pvec"]
    shared["c_rot"] = consts["rot"]
    in_maps = []
    for c in range(N_CORES):
        m = dict(shared)
        m["x"] = x[c * nseq:(c + 1) * nseq]
        m["mem"] = mem[c * nseq:(c + 1) * nseq]
        m["positions"] = pos[c * nseq:(c + 1) * nseq]
        in_maps.append(m)
    res = run_bass_kernel_spmd(nc, in_maps, core_ids=list(range(N_CORES)))
    out = np.concatenate([np.asarray(r["y"]) for r in res.results], axis=0)
    return out.astype(np.float32)
```

```python
import math
import os
import numpy as np
KDBG = float(os.environ.get('KDBG', '99'))
KSTOP = int(os.environ.get('KSTOP', '99'))
import concourse.bass as bass
import concourse.mybir as mybir
from concourse.bass_utils import run_bass_kernel_spmd

F32 = mybir.dt.float32
BF16 = mybir.dt.bfloat16
I32 = mybir.dt.int32
ALU = mybir.AluOpType
ACT = mybir.ActivationFunctionType

D = 1024
KC = 8
MEM_LEN = 256
DEPTH = 4
N_CORES = 8
BATCH = 16
SEQ = 2048
ROPE_THETA = 500000.0
ALPHA = (2 * DEPTH) ** 0.25
LN_EPS = 1e-5
C_CQ, C_CKV, C_KPE, C_DQ, C_DK, C_DV, C_MQ, C_GATE = 0, 384, 640, 672, 1696, 2720, 3744, 4256


class Res:
    __slots__ = ("name", "last_writer", "readers", "excl")

    def __init__(self, name="", excl=False):
        self.name = name
        self.last_writer = None
        self.readers = []
        self.excl = excl


class Op:
    __slots__ = ("eng", "emit", "deps", "signal", "seq", "is_dma", "dma_sem", "dma_val", "big")

    def __init__(self, eng, emit, is_dma=False, big=False):
        self.eng = eng
        self.emit = emit
        self.deps = []
        self.signal = False
        self.seq = 0
        self.is_dma = is_dma
        self.dma_sem = None
        self.dma_val = 0
        self.big = big


class FW:
    ENGS = ("tensor", "vector", "scalar", "gpsimd", "sync")

    def __init__(self, nc, same_engine_sync=True):
        self.nc = nc
        self.ops = []
        self.same_engine_sync = same_engine_sync
        self.dma_sems = {}
        self.all_res = []

    def res(self, name="", excl=False):
        r = Res(name, excl)
        self.all_res.append(r)
        return r

    def op(self, eng, emit, reads=(), writes=(), big=False):
        o = Op(eng, emit, big=big)
        self._track(o, reads, writes)
        return o

    def dma(self, eng, emit, semkey, reads=(), writes=()):
        o = Op(eng, emit, is_dma=True)
        if semkey not in self.dma_sems:
            self.dma_sems[semkey] = [self.nc.alloc_semaphore("dq%d" % len(self.dma_sems)), 0]
        ent = self.dma_sems[semkey]
        ent[1] += 16
        o.dma_sem = ent[0]
        o.dma_val = ent[1]
        self._track(o, reads, writes)
        return o

    def _track(self, o, reads, writes):
        ex = [r for r in reads if r.excl]
        if ex:
            reads = [r for r in reads if not r.excl]
            writes = list(writes) + [r for r in ex if r not in writes]
        deps = []
        for r in reads:
            if r.last_writer is not None:
                deps.append(r.last_writer)
        for w in writes:
            if w.last_writer is not None:
                deps.append(w.last_writer)
            deps.extend(w.readers)
        for r in reads:
            r.readers.append(o)
        for w in writes:
            w.last_writer = o
            w.readers = []
        seen = set()
        for d in deps:
            if id(d) not in seen and d is not o:
                seen.add(id(d))
                o.deps.append(d)
        self.ops.append(o)

    def barrier(self):
        for e in self.ENGS:
            self.op(e, lambda eng: eng.nop(), reads=(), writes=self.all_res)

    def finish(self):
        nc = self.nc
        engs = {e: getattr(nc, e) for e in self.ENGS}
        sems = {e: nc.alloc_semaphore("eng_" + e) for e in self.ENGS}
        for o in self.ops:
            kept = []
            for d in o.deps:
                if d.is_dma:
                    kept.append(d)
                    continue
                if d.eng == o.eng and not o.is_dma:
                    if d.eng == "tensor":
                        continue
                    if not self.same_engine_sync:
                        continue
                    if d.big and o.big:
                        continue
                kept.append(d)
                d.signal = True
            o.deps = kept
        cnt = {e: 0 for e in self.ENGS}
        for o in self.ops:
            if o.signal and not o.is_dma:
                cnt[o.eng] += 1
                o.seq = cnt[o.eng]
        waited = {e: {} for e in self.ENGS}
        for o in self.ops:
            eng = engs[o.eng]
            need = {}
            for d in o.deps:
                if d.is_dma:
                    s, v = d.dma_sem, d.dma_val
                else:
                    s, v = sems[d.eng], d.seq
                if need.get(s.num, (None, 0))[1] < v:
                    need[s.num] = (s, v)
            for num, (s, v) in need.items():
                if waited[o.eng].get(num, 0) >= v:
                    continue
                eng.wait_ge(s, v)
                waited[o.eng][num] = v
            ins = o.emit(eng)
            if o.is_dma:
                ins.then_inc(o.dma_sem, 16)
            elif o.signal:
                ins.then_inc(sems[o.eng], 1)
        return cnt


class Buf:
    __slots__ = ("ap", "res", "key")

    def __init__(self, ap, res, key=None):
        self.ap = ap
        self.res = res
        self.key = key


class Ring:
    def __init__(self, items):
        self.items = items
        self.i = 0

    def next(self):
        it = self.items[self.i % len(self.items)]
        self.i += 1
        return it


def _host_consts():
    c = {}
    c["ident"] = np.eye(128, dtype=np.float32)
    inv_m = (ROPE_THETA ** (-np.arange(0, 32, 2, dtype=np.float32) / np.float32(32))).astype(np.float32)
    inv_d = (ROPE_THETA ** (-np.arange(0, 16, 2, dtype=np.float32) / np.float32(16))).astype(np.float32)
    pv = np.zeros((128, 8), dtype=np.float32)
    rot_m = np.zeros((128, 128), dtype=np.float32)
    rot_d = np.zeros((128, 128), dtype=np.float32)
    for i in range(32):
        p = 64 + i
        pv[p, 0] = inv_m[i % 16]
        pv[p, 1] = -1.0 if i < 16 else 1.0
        partner = 64 + (i + 16 if i < 16 else i - 16)
        rot_m[partner, p] = 1.0
    for o in (0, 64):
        for i in range(16):
            p = o + i
            pv[p, 2] = inv_d[i % 8]
            pv[p, 3] = -1.0 if i < 8 else 1.0
            partner = o + (i + 8 if i < 8 else i - 8)
            rot_d[partner, p] = 1.0
    c["pvec"] = pv
    c["rot"] = np.stack([rot_m, rot_d], axis=0)
    return c


def build_program(NSEQ, S, NLAYER, layer_ids=None, upto=99):
    assert S % 512 == 0
    TB = S // 512
    TT = S // 128
    HB = min(1024, S // 2)
    NH = S // HB
    SBH = HB // 512
    nc = bass.Bass("TRN2", target_bir_lowering=False)
    fw = FW(nc)
    L = NLAYER

    def din(name, shape, dt=F32):
        return nc.dram_tensor(name, list(shape), dt, kind="ExternalInput").ap()

    x_d = din("x", [NSEQ, S, D])
    mem_d = din("mem", [NSEQ, MEM_LEN, D])
    pos_d = din("positions", [NSEQ, S], I32)
    w_in_d = din("w_in", [L, D, 7328])
    b_gate_d = din("b_gate", [L, 3, D])
    qn_d = din("mla_q_norm", [L, 384])
    kvn_d = din("mla_kv_norm", [L, 256])
    wuq_d = din("mla_w_uq", [L, 384, 768])
    wukv_d = din("mla_w_ukv", [L, 256, 1024])
    dlam_d = din("diff_lambda", [L, 4, 64])
    subln_d = din("diff_subln", [L, 128])
    wkv_d = din("mem_w_kv", [L, D, 1024])
    wbr_d = din("w_branch", [L, 2048, D])
    wout_d = din("w_out", [L, D, D])
    ln1g_d = din("ln1_g", [L, D])
    ln1b_d = din("ln1_b", [L, D])
    w1_d = din("mlp_w1", [L, D, 4096])
    w2_d = din("mlp_w2", [L, 4096, D])
    ln2g_d = din("ln2_g", [L, D])
    ln2b_d = din("ln2_b", [L, D])
    ident_d = din("c_ident", [128, 128])
    pvec_d = din("c_pvec", [128, 8])
    rot_d_ = din("c_rot", [2, 128, 128])
    y_d = nc.dram_tensor("y", [NSEQ, S, D], F32, kind="ExternalOutput").ap()

    def sb(name, shape, dt):
        return nc.alloc_sbuf_tensor(name, list(shape), dt)

    xhi = sb("xhi", [128, KC, S], BF16)
    xlo = sb("xlo", [128, KC, S], BF16)
    oT = sb("oT", [128, 16, S], BF16)
    memT = sb("memT", [128, KC, MEM_LEN], BF16)
    ident_f = sb("ident_f", [128, 128], F32)
    ones_b = sb("ones_b", [128, 128], BF16)
    onesD = sb("onesD", [128, 128], BF16)
    rot_f = sb("rot_f", [128, 2, 128], F32)
    rot_b = sb("rot_b", [128, 2, 128], BF16)
    pvec = sb("pvec", [128, 8], F32)
    neghalf = sb("neghalf", [128, 512], F32)
    halfpi = sb("halfpi", [128, 1], F32)
    par = sb("par", [128, L, 64], F32)
    bgh = par[:, :, 0:24]
    gq = par[:, :, 24:27]
    gkv = par[:, :, 27:29]
    gsub = par[:, :, 29]
    l1g = par[:, :, 30:38]
    l1b = par[:, :, 38:46]
    l2g = par[:, :, 46:54]
    l2b = par[:, :, 54:62]
    dls = sb("dls", [128, L, 2], F32)
    nlam = sb("nlam", [128, L], F32)
    R_const = fw.res("const")
    R_par = fw.res("par")
    R_memT = fw.res("memT")
    R_X = [[fw.res("x%d_%d" % (c, t)) for t in range(TB)] for c in range(KC)]
    R_O = [[fw.res("o%d_%d" % (c, t)) for t in range(TB)] for c in range(16)]

    AR = (nc.sbuf_bytes_remaining - 2048) // 2 // 16 * 16
    arena = sb("arena", [128, AR], BF16)

    class Carver:
        def __init__(self):
            self.off = 0

        def bf(self, n):
            v = arena[:, self.off:self.off + n]
            self.off += n
            assert self.off <= AR, ("arena overflow", self.off, AR)
            return v

        def f32(self, n):
            return self.bf(2 * n).bitcast(F32)

    psb = [nc.alloc_psum_tensor("ps%d" % i, [128, 512], F32) for i in range(8)]
    PS = [Buf(psb[i], fw.res("ps%d" % i, excl=True)) for i in range(8)]

    def V(fn, reads, writes):
        fw.op("vector", fn, reads, writes)

    def A(fn, reads, writes):
        fw.op("scalar", fn, reads, writes)

    def G(fn, reads, writes):
        fw.op("gpsimd", fn, reads, writes)

    def mm(out_ap, pairs, reads, wres):
        n = len(pairs)
        for i, (l_, r_) in enumerate(pairs):
            fw.op("tensor",
                  lambda e, l_=l_, r_=r_, i=i: e.matmul(out_ap, lhsT=l_, rhs=r_, start=(i == 0), stop=(i == n - 1)),
                  reads, [wres])

    def load_w(slot, dst_ap, src_ap):
        fw.dma("gpsimd", lambda e: e.dma_start(out=dst_ap, in_=src_ap), slot.key, writes=[slot.res])

    def wsrc(w_ap, r0, r1, c0, c1):
        return w_ap[r0:r1, c0:c1].rearrange("(c p) n -> p c n", p=128)

    def tok(tb):
        return slice(tb * 512, (tb + 1) * 512)

    fw.dma("sync", lambda e: e.dma_start(out=ident_f[:], in_=ident_d), "c0", writes=[R_const])
    fw.dma("sync", lambda e: e.dma_start(out=pvec[:], in_=pvec_d), "c1", writes=[R_const])
    fw.dma("sync", lambda e: e.dma_start(out=rot_f[:], in_=rot_d_.rearrange("r k m -> k r m")), "c2", writes=[R_const])
    V(lambda e: e.tensor_copy(out=rot_b[:], in_=rot_f[:]), [R_const], [R_const])
    V(lambda e: e.memset(ones_b[:], 1.0), [], [R_const])
    V(lambda e: e.memset(onesD[:], 1.0 / 1024.0), [], [R_const])
    V(lambda e: e.memset(neghalf[:], -0.5), [], [R_const])
    V(lambda e: e.memset(halfpi[:], math.pi / 2), [], [R_const])

    car0 = Carver()
    dl = car0.f32(L * 256).rearrange("p (l n) -> p l n", l=L)
    dlp = car0.f32(L * 128).rearrange("p (l n) -> p l n", l=L)

    stage = car0.f32(L * 128).rearrange("p (l n) -> p l n", l=L)
    R_stage = fw.res("stage")
    for l in range(L):
        rows = [(0, b_gate_d[l].rearrange("i (c p) -> (i c) p", p=128), 24),
                (24, qn_d[l].rearrange("(c p) -> c p", p=128), 3),
                (27, kvn_d[l].rearrange("(c p) -> c p", p=128), 2),
                (29, subln_d[l].rearrange("(c p) -> c p", p=128), 1),
                (30, ln1g_d[l].rearrange("(c p) -> c p", p=128), 8),
                (38, ln1b_d[l].rearrange("(c p) -> c p", p=128), 8),
                (46, ln2g_d[l].rearrange("(c p) -> c p", p=128), 8),
                (54, ln2b_d[l].rearrange("(c p) -> c p", p=128), 8)]
        for (r0, src, n) in rows:
            fw.dma("sync", lambda e, r0=r0, src=src, n=n, l=l: e.dma_start(out=stage[r0:r0 + n, l, :], in_=src), "pst", writes=[R_stage])
        fw.dma("sync", lambda e, l=l: e.dma_start(out=dl[:, l, :], in_=dlam_d[l].rearrange("a b -> (a b)").partition_broadcast(128)),
               "p8", writes=[R_par])
    for l in range(L):
        pp = PS[l % 8]
        fw.op("tensor", lambda e, l=l, pp=pp: e.transpose(out=pp.ap[:, 0:62], in_=stage[0:62, l, :], identity=ident_f[0:62, 0:62]),
              [R_stage, R_const], [pp.res])
        V(lambda e, l=l, pp=pp: e.tensor_copy(out=par[:, l, 0:62], in_=pp.ap[:, 0:62]), [pp.res], [R_par])
    for l in range(L):
        lam_init = 0.8 - 0.6 * math.exp(-0.3 * ((layer_ids[l]) if layer_ids is not None else l))
        V(lambda e, l=l: e.tensor_scalar(out=bgh[:, l, :], in0=bgh[:, l, :], scalar1=0.5, scalar2=None, op0=ALU.mult), [R_par], [R_par])
        V(lambda e, l=l: e.tensor_tensor(out=dlp[:, l, 0:64], in0=dl[:, l, 0:64], in1=dl[:, l, 64:128], op=ALU.mult), [R_par], [R_par])
        V(lambda e, l=l: e.tensor_tensor(out=dlp[:, l, 64:128], in0=dl[:, l, 128:192], in1=dl[:, l, 192:256], op=ALU.mult), [R_par], [R_par])
        V(lambda e, l=l: e.reduce_sum(out=dls[:, l, 0:1], in_=dlp[:, l, 0:64], axis=mybir.AxisListType.X), [R_par], [R_par])
        V(lambda e, l=l: e.reduce_sum(out=dls[:, l, 1:2], in_=dlp[:, l, 64:128], axis=mybir.AxisListType.X), [R_par], [R_par])
        A(lambda e, l=l: e.activation(out=dls[:, l, :], in_=dls[:, l, :], func=ACT.Exp), [R_par], [R_par])
        V(lambda e, l=l: e.tensor_tensor(out=nlam[:, l:l + 1], in0=dls[:, l, 1:2], in1=dls[:, l, 0:1], op=ALU.subtract), [R_par], [R_par])
        V(lambda e, l=l, li=lam_init: e.tensor_scalar(out=nlam[:, l:l + 1], in0=nlam[:, l:l + 1], scalar1=-li, scalar2=None, op0=ALU.add), [R_par], [R_par])

    def attn_chunk(car_state, qc, qT, kT, vfn, kb, Kdim, dv, ro, nk, scale, q_reads, k_reads, v_reads, post):
        SC, ACC, PT = car_state
        pv = ACC.next()
        den = ACC.next()

        def scores(kt):
            sc = SC.next()
            mm(sc.ap[:, :], [(kT[kb:kb + Kdim, kt * 128:(kt + 1) * 128], qT[kb:kb + Kdim, tok(qc)])],
               q_reads + k_reads, sc.res)
            return sc

        sc_next = scores(0)
        for kt in range(nk):
            sc = sc_next
            pt = PT.next()
            A(lambda e, sc=sc, pt=pt: e.activation(out=pt.ap, in_=sc.ap[:, :], func=ACT.Exp, scale=scale),
              [sc.res], [pt.res])
            if kt + 1 < nk:
                sc_next = scores(kt + 1)
            fw.op("tensor", lambda e, pt=pt, kt=kt: e.matmul(pv.ap[ro:ro + dv, :], lhsT=vfn(kt), rhs=pt.ap,
                                                             start=(kt == 0), stop=(kt == nk - 1)),
                  [pt.res] + v_reads, [pv.res])
            fw.op("tensor", lambda e, pt=pt, kt=kt: e.matmul(den.ap[ro:ro + dv, :], lhsT=ones_b[:, 0:dv], rhs=pt.ap,
                                                             start=(kt == 0), stop=(kt == nk - 1)),
                  [pt.res, R_const], [den.res])
        post(qc, pv, den)

    def attention(car_state, qT, kT, vfn, kb, Kdim, dv, ro, nk, scale, q_reads, k_reads, v_reads, post):
        for qc in range(TB):
            attn_chunk(car_state, qc, qT, kT, vfn, kb, Kdim, dv, ro, nk, scale, q_reads, k_reads, v_reads, post)

    _csts = {}

    def cst(val):
        key = float(val)
        if key not in _csts:
            t = sb("cst%d" % len(_csts), [128, 1], F32)
            r = fw.res("cst")
            V(lambda e, t=t, key=key: e.memset(t[:], key), [], [r])
            _csts[key] = (t, r)
        return _csts[key]

    def rsqrt_act(dst, src_ap, reads, scale=1.0, bias=None):
        if bias is None:
            A(lambda e: e.activation(out=dst.ap, in_=src_ap, func=ACT.Ln, scale=float(scale)), reads, [dst.res])
        else:
            bt, br = cst(bias)
            A(lambda e: e.activation(out=dst.ap, in_=src_ap, func=ACT.Ln, scale=float(scale), bias=bt[:, 0:1]),
              reads + [br], [dst.res])
        A(lambda e: e.activation(out=dst.ap, in_=dst.ap, func=ACT.Exp, scale=-0.5), [dst.res], [dst.res])

    def load_sequence(s):
        car = Carver()
        xin = [Buf(car.f32(1024), fw.res("xin%d" % i), "xin%d" % i) for i in range(2)]
        ring = Ring(xin)
        pr = Ring(PS)
        for tt in range(TT):
            xb = ring.next()
            fw.dma("sync", lambda e, xb=xb, tt=tt: e.dma_start(out=xb.ap, in_=x_d[s, tt * 128:(tt + 1) * 128, :]),
                   xb.key, writes=[xb.res])
            tb = tt // 4
            for g in range(2):
                p = pr.next()
                for cc in range(4):
                    c = g * 4 + cc
                    fw.op("tensor", lambda e, p=p, xb=xb, c=c, cc=cc: e.transpose(out=p.ap[:, cc * 128:(cc + 1) * 128],
                                                                                 in_=xb.ap[:, c * 128:(c + 1) * 128], identity=ident_f[:]),
                          [xb.res, R_const], [p.res])
                wr = [R_X[g * 4 + cc][tb] for cc in range(4)]
                hi_v = xhi[:, g * 4:g * 4 + 4, tt * 128:(tt + 1) * 128]
                lo_v = xlo[:, g * 4:g * 4 + 4, tt * 128:(tt + 1) * 128]
                pv3 = p.ap[:, :].rearrange("p (a b) -> p a b", a=4)
                A(lambda e, hi_v=hi_v, pv3=pv3: e.activation(out=hi_v, in_=pv3, func=ACT.Copy), [p.res], wr)
                V(lambda e, lo_v=lo_v, hi_v=hi_v, pv3=pv3: e.tensor_tensor(out=lo_v, in0=pv3, in1=hi_v, op=ALU.subtract),
                  [p.res] + wr, wr)
        for mt in range(MEM_LEN // 128):
            xb = ring.next()
            fw.dma("sync", lambda e, xb=xb, mt=mt: e.dma_start(out=xb.ap, in_=mem_d[s, mt * 128:(mt + 1) * 128, :]),
                   xb.key, writes=[xb.res])
            for g in range(2):
                p = pr.next()
                for cc in range(4):
                    c = g * 4 + cc
                    fw.op("tensor", lambda e, p=p, xb=xb, c=c, cc=cc: e.transpose(out=p.ap[:, cc * 128:(cc + 1) * 128],
                                                                                 in_=xb.ap[:, c * 128:(c + 1) * 128], identity=ident_f[:]),
                          [xb.res, R_const], [p.res])
                V(lambda e, p=p, g=g, mt=mt: e.tensor_copy(out=memT[:, g * 4:g * 4 + 4, mt * 128:(mt + 1) * 128],
                                                          in_=p.ap[:, :].rearrange("p (a b) -> p a b", a=4)), [p.res], [R_memT])

    tab_d = nc.dram_tensor("tab_scratch", [4, 128, S], BF16, kind="Internal").ap()
    R_tabd = fw.res("tabd")

    def build_tables(s):
        car = Carver()
        posi = car.bf(2 * S).bitcast(I32)
        posf = car.f32(S)
        ang = car.f32(S)
        t1 = car.f32(S)
        t2 = car.f32(S)
        ki = car.bf(2 * S).bitcast(I32)
        tb16 = car.bf(S)
        R_t = fw.res("tabtmp")
        fw.dma("sync", lambda e: e.dma_start(out=posi, in_=pos_d[s, :].partition_broadcast(128)), "posi", writes=[R_t])
        V(lambda e: e.tensor_copy(out=posf, in_=posi), [R_t], [R_t])
        inv2pi = float(1.0 / (2 * math.pi))
        for ti, (fcol, scol) in enumerate([(0, 1), (2, 3)]):
            V(lambda e, fcol=fcol: e.tensor_scalar(out=ang, in0=posf, scalar1=pvec[:, fcol:fcol + 1], scalar2=None, op0=ALU.mult),
              [R_t, R_const], [R_t])
            for which in range(2):
                if which == 0:
                    V(lambda e: e.tensor_scalar(out=t1, in0=ang, scalar1=halfpi[:, 0:1], scalar2=None, op0=ALU.add), [R_t, R_const], [R_t])
                    src = t1
                else:
                    src = ang
                V(lambda e, src=src: e.tensor_scalar(out=t2, in0=src, scalar1=inv2pi, scalar2=None, op0=ALU.mult), [R_t], [R_t])
                V(lambda e: e.tensor_copy(out=ki, in_=t2), [R_t], [R_t])
                V(lambda e: e.tensor_copy(out=t2, in_=ki), [R_t], [R_t])
                V(lambda e, src=src: e.scalar_tensor_tensor(out=t2, in0=t2, scalar=float(-2 * math.pi), in1=src, op0=ALU.mult, op1=ALU.add),
                  [R_t], [R_t])
                A(lambda e: e.activation(out=t2, in_=t2, func=ACT.Sin), [R_t], [R_t])
                if which == 0:
                    V(lambda e: e.tensor_copy(out=tb16, in_=t2), [R_t], [R_t])
                else:
                    V(lambda e, scol=scol: e.tensor_scalar(out=tb16, in0=t2, scalar1=pvec[:, scol:scol + 1], scalar2=None, op0=ALU.mult),
                      [R_t, R_const], [R_t])
                idx = ti * 2 + which
                fw.dma("sync", lambda e, idx=idx: e.dma_start(out=tab_d[idx], in_=tb16), "tabst", reads=[R_t], writes=[R_tabd])

    def layer(s, l, last):
        lid = layer_ids[l] if layer_ids is not None else l
        lam_init = 0.8 - 0.6 * math.exp(-0.3 * lid)
        fw.barrier()
        car = Carver()
        tabs = car.bf(4 * S).rearrange("p (a b) -> p a b", a=4)
        R_tab = fw.res("tab")
        Cm, Sm, Cd, Sd = tabs[:, 0, :], tabs[:, 1, :], tabs[:, 2, :], tabs[:, 3, :]
        fw.dma("sync", lambda e: e.dma_start(out=tabs, in_=tab_d.rearrange("a p s -> p a s")), "tabld", reads=[R_tabd], writes=[R_tab])
        WS = Ring([Buf(car.bf(4096), fw.res("w%d" % i), "w%d" % i) for i in range(2)])
        qTt = car.bf(S)
        kTt = car.bf(S)
        vt = car.bf(TT * 128).rearrange("p (a b) -> p a b", a=TT)
        R_q, R_k, R_v = fw.res("q"), fw.res("k"), fw.res("v")
        PT = Ring([Buf(car.bf(512), fw.res("pt%d" % i)) for i in range(3)])
        TF = Ring([Buf(car.f32(512), fw.res("tf%d" % i)) for i in range(5)])
        TBf = Ring([Buf(car.bf(512), fw.res("tb%d" % i)) for i in range(2)])
        qz1 = car.bf(S)
        R_q1 = fw.res("q1")
        kpe = car.bf(max(S, 2048))
        R_kpe = fw.res("kpe")
        SC = Ring(PS[0:2])
        ACC = Ring(PS[2:6])
        GEN = Ring(PS[6:8])
        car_state = (SC, ACC, PT)
        USE["attn"] = car.off
        cqn = lambda j, tb: oT[:, 4 + j, tok(tb)]
        ckvn = lambda j, tb: oT[:, 7 + j, tok(tb)]
        R_cq = lambda j, tb: R_O[4 + j][tb]
        R_ckv = lambda j, tb: R_O[7 + j][tb]

        def rope_rows(dst_ap, a_ps, rows, tabC, tabS, rotsel, tsl, reads, wres, split=None):
            ab = TBf.next()
            A(lambda e: e.activation(out=ab.ap[rows, :], in_=a_ps.ap[rows, :], func=ACT.Copy), [a_ps.res], [ab.res])
            bp = GEN.next()
            mm(bp.ap[:, :], [(rot_b[:, rotsel, :], ab.ap)], [ab.res, R_const], bp.res)
            t1 = TF.next()
            V(lambda e: e.tensor_tensor(out=t1.ap[rows, :], in0=a_ps.ap[rows, :], in1=tabC[rows, tsl], op=ALU.mult),
              [a_ps.res, R_tab], [t1.res])
            t2 = TF.next()
            V(lambda e: e.tensor_tensor(out=t2.ap[rows, :], in0=bp.ap[rows, :], in1=tabS[rows, tsl], op=ALU.mult),
              [bp.res, R_tab], [t2.res])
            if split is None:
                V(lambda e: e.tensor_tensor(out=dst_ap, in0=t1.ap[rows, :], in1=t2.ap[rows, :], op=ALU.add),
                  [t1.res, t2.res] + reads, [wres])
            else:
                for (d_ap, d_res, pr) in split:
                    V(lambda e, d_ap=d_ap, pr=pr: e.tensor_tensor(out=d_ap, in0=t1.ap[pr, :], in1=t2.ap[pr, :], op=ALU.add),
                      [t1.res, t2.res] + reads, [d_res])

        for b_ in TBf.items:
            V(lambda e, b_=b_: e.memset(b_.ap, 0.0), [], [b_.res])

        ws = WS.next()
        wcq = ws.ap[:, 0:KC * 384].rearrange("p (c n) -> p c n", c=KC)
        load_w(ws, wcq, wsrc(w_in_d[l], 0, D, C_CQ, C_CQ + 384))
        ws2 = WS.next()
        wck = ws2.ap[:, 0:KC * 288].rearrange("p (c n) -> p c n", c=KC)
        load_w(ws2, wck, wsrc(w_in_d[l], 0, D, C_CKV, C_CKV + 288))
        for tb in range(TB):
            xr = [R_X[k][tb] for k in range(KC)]
            for (wt, wsl, nj, cols, gvec, dstf, rdst, eps, nfe) in (
                    (wcq, ws, 3, 0, gq, cqn, R_cq, 1e-6, 384.0),
                    (wck, ws2, 2, 0, gkv, ckvn, R_ckv, 1e-6, 256.0)):
                pj = []
                sqs = []
                for j in range(nj):
                    p = ACC.next()
                    pj.append(p)
                    mm(p.ap[:, :], [(wt[:, k, cols + j * 128:cols + (j + 1) * 128], xhi[:, k, tok(tb)]) for k in range(KC)],
                       xr + [wsl.res], p.res)
                    sq = PT.next()
                    A(lambda e, p=p, sq=sq: e.activation(out=sq.ap, in_=p.ap[:, :], func=ACT.Square), [p.res], [sq.res])
                    sqs.append(sq)
                pss = GEN.next()
                mm(pss.ap[:, :], [(ones_b[:, :], sq.ap) for sq in sqs], [sq.res for sq in sqs] + [R_const], pss.res)
                rs = TF.next()
                rsqrt_act(rs, pss.ap[:, :], [pss.res], scale=1.0 / nfe, bias=eps)
                for j in range(nj):
                    V(lambda e, j=j, p=pj[j], rs=rs, gvec=gvec, dap=dstf(j, tb): e.scalar_tensor_tensor(
                        out=dap, in0=p.ap[:, :], scalar=gvec[:, l, j:j + 1], in1=rs.ap, op0=ALU.mult, op1=ALU.mult),
                      [pj[j].res, rs.res, R_par], [rdst(j, tb)])
            p = ACC.next()
            mm(p.ap[64:96, :], [(wck[:, k, 256:288], xhi[:, k, tok(tb)]) for k in range(KC)], xr + [ws2.res], p.res)
            rope_rows(kpe[64:96, tok(tb)], p, slice(64, 96), Cm, Sm, 0, tok(tb), [], R_kpe)
        for h in range(8):
            ws = WS.next()
            wq = ws.ap[:, 0:3 * 96].rearrange("p (c n) -> p c n", c=3)
            wkv_ = ws.ap[:, 512:512 + 2 * 128].rearrange("p (c n) -> p c n", c=2)
            load_w(ws, wq, wsrc(wuq_d[l], 0, 384, h * 96, (h + 1) * 96))
            load_w(ws, wkv_, wsrc(wukv_d[l], 0, 256, h * 128, (h + 1) * 128))
            for tb in range(TB):
                cqr = [R_cq(j, tb) for j in range(3)]
                ckr = [R_ckv(j, tb) for j in range(2)]
                p = GEN.next()
                mm(p.ap[0:96, :], [(wq[:, j, :], cqn(j, tb)) for j in range(3)], cqr + [ws.res], p.res)
                V(lambda e, p=p, tb=tb: e.tensor_copy(out=qTt[0:64, tok(tb)], in_=p.ap[0:64, :]), [p.res], [R_q])
                rope_rows(qTt[64:96, tok(tb)], p, slice(64, 96), Cm, Sm, 0, tok(tb), [], R_q)
                p2 = GEN.next()
                mm(p2.ap[0:64, :], [(wkv_[:, j, 0:64], ckvn(j, tb)) for j in range(2)], ckr + [ws.res], p2.res)
                V(lambda e, p2=p2, tb=tb: e.tensor_copy(out=kTt[0:64, tok(tb)], in_=p2.ap[0:64, :]), [p2.res], [R_k])
                V(lambda e, tb=tb: e.tensor_copy(out=kTt[64:96, tok(tb)], in_=kpe[64:96, tok(tb)]), [R_kpe], [R_k])
                p3 = GEN.next()
                for t4 in range(4):
                    tsl = slice(tb * 512 + t4 * 128, tb * 512 + (t4 + 1) * 128)
                    mm(p3.ap[:, t4 * 64:(t4 + 1) * 64], [(oT[:, 7 + j, tsl], wkv_[:, j, 64:128]) for j in range(2)], ckr + [ws.res], p3.res)
                V(lambda e, p3=p3, tb=tb: e.tensor_copy(out=vt[:, tb * 4:(tb + 1) * 4, 0:64],
                                                        in_=p3.ap[:, 0:256].rearrange("p (a b) -> p a b", a=4)), [p3.res], [R_v])
            ro = (h % 2) * 64

            def post_mla(qc, pv, den, h=h, ro=ro):
                r = TF.next()
                V(lambda e: e.reciprocal(out=r.ap[ro:ro + 64, :], in_=den.ap[ro:ro + 64, :]), [den.res], [r.res])
                V(lambda e: e.tensor_tensor(out=oT[ro:ro + 64, h // 2, tok(qc)], in0=pv.ap[ro:ro + 64, :], in1=r.ap[ro:ro + 64, :], op=ALU.mult),
                  [pv.res, r.res], [R_O[h // 2][qc]])

            attention(car_state, qTt, kTt, lambda kt: vt[:, kt, 0:64], 0, 96, 64, ro, TT, 96.0 ** -0.5,
                      [R_q], [R_k], [R_v], post_mla)

        if upto < 4:
            return
        k1 = 1.0 / (128.0 * (1.0 - lam_init) ** 2)
        k2 = 1e-5 / ((1.0 - lam_init) ** 2)
        V(lambda e: e.memset(qTt[64:128, :], 0.0), [], [R_q])
        V(lambda e: e.memset(qz1[0:64, :], 0.0), [], [R_q1])
        for h in range(8):
            ws = WS.next()
            w3 = ws.ap[:, 0:KC * 384].rearrange("p (c n) -> p c n", c=KC)
            for i3, c0 in enumerate((C_DQ, C_DK, C_DV)):
                load_w(ws, w3[:, :, i3 * 128:(i3 + 1) * 128], wsrc(w_in_d[l], 0, D, c0 + h * 128, c0 + (h + 1) * 128))
            for tb in range(TB):
                xr = [R_X[k][tb] for k in range(KC)]
                for i3 in range(2):
                    p = GEN.next()
                    mm(p.ap[:, :], [(w3[:, k, i3 * 128:(i3 + 1) * 128], xhi[:, k, tok(tb)]) for k in range(KC)], xr + [ws.res], p.res)
                    if i3 == 0:
                        rope_rows(None, p, slice(0, 128), Cd, Sd, 1, tok(tb), [], None,
                                  split=[(qTt[0:64, tok(tb)], R_q, slice(0, 64)), (qz1[64:128, tok(tb)], R_q1, slice(64, 128))])
                    else:
                        rope_rows(kTt[:, tok(tb)], p, slice(0, 128), Cd, Sd, 1, tok(tb), [], R_k)
                p3 = GEN.next()
                for t4 in range(4):
                    tsl = slice(tb * 512 + t4 * 128, tb * 512 + (t4 + 1) * 128)
                    mm(p3.ap[:, t4 * 128:(t4 + 1) * 128], [(xhi[:, k, tsl], w3[:, k, 256:384]) for k in range(KC)], xr + [ws.res], p3.res)
                for t4 in range(4):
                    V(lambda e, p3=p3, tb=tb, t4=t4: e.tensor_copy(out=vt[:, tb * 4 + t4, :], in_=p3.ap[:, t4 * 128:(t4 + 1) * 128]), [p3.res], [R_v])
            keep = {}

            def post_d0(qc, pv, den):
                r0 = TF.next()
                V(lambda e: e.reciprocal(out=r0.ap, in_=den.ap[:, :]), [den.res], [r0.res])
                a_ = TF.next()
                V(lambda e: e.tensor_tensor(out=a_.ap, in0=pv.ap[:, :], in1=r0.ap, op=ALU.mult), [pv.res, r0.res], [a_.res])
                keep[qc] = a_

            def post_d1(qc, pv, den, h=h):
                a_ = keep[qc]
                r1 = TF.next()
                V(lambda e: e.reciprocal(out=r1.ap, in_=den.ap[:, :]), [den.res], [r1.res])
                b_ = TF.next()
                V(lambda e: e.scalar_tensor_tensor(out=b_.ap, in0=pv.ap[:, :], scalar=nlam[:, l:l + 1], in1=r1.ap,
                                                   op0=ALU.mult, op1=ALU.mult), [pv.res, r1.res, R_par], [b_.res])
                V(lambda e: e.tensor_tensor(out=a_.ap, in0=a_.ap, in1=b_.ap, op=ALU.add), [a_.res, b_.res], [a_.res])
                sq = PT.next()
                A(lambda e: e.activation(out=sq.ap, in_=a_.ap, func=ACT.Square), [a_.res], [sq.res])
                pss = GEN.next()
                mm(pss.ap[:, :], [(ones_b[:, :], sq.ap)], [sq.res, R_const], pss.res)
                vv = TF.next()
                rsqrt_act(vv, pss.ap[:, :], [pss.res], scale=k1, bias=k2)
                V(lambda e: e.scalar_tensor_tensor(out=oT[:, 4 + h, tok(qc)], in0=a_.ap, scalar=par[:, l, 29:30],
                                                   in1=vv.ap, op0=ALU.mult, op1=ALU.mult),
                  [a_.res, vv.res, R_par], [R_O[4 + h][qc]])

            for qc in range(TB):
                for qz, rq, pst in ((qTt, R_q, post_d0), (qz1, R_q1, post_d1)):
                    attn_chunk(car_state, qc, qz, kTt, lambda kt: vt[:, kt, :], 0, 128, 128, 0, TT, 64.0 ** -0.5,
                               [rq], [R_k], [R_v], pst)

        if upto < 5:
            return
        Km = kpe[:, 0:4 * MEM_LEN].rearrange("p (a b) -> p a b", a=4)
        Vm = kpe[:, 4 * MEM_LEN:4 * MEM_LEN + 2 * 512].rearrange("p (a b) -> p a b", a=2)
        for half in range(2):
            ws = WS.next()
            wk_ = ws.ap[:, 0:KC * 512].rearrange("p (c n) -> p c n", c=KC)
            load_w(ws, wk_, wsrc(wkv_d[l], 0, D, half * 512, (half + 1) * 512))
            if half == 0:
                for hh in range(4):
                    p = GEN.next()
                    mm(p.ap[:, 0:MEM_LEN], [(wk_[:, k, hh * 128:(hh + 1) * 128], memT[:, k, :]) for k in range(KC)], [R_memT, ws.res], p.res)
                    V(lambda e, p=p, hh=hh: e.tensor_copy(out=Km[:, hh, :], in_=p.ap[:, 0:MEM_LEN]), [p.res], [R_kpe])
            else:
                for mt in range(2):
                    p = GEN.next()
                    mm(p.ap[:, :], [(memT[:, k, mt * 128:(mt + 1) * 128], wk_[:, k, :]) for k in range(KC)], [R_memT, ws.res], p.res)
                    V(lambda e, p=p, mt=mt: e.tensor_copy(out=Vm[:, mt, :], in_=p.ap[:, :]), [p.res], [R_kpe])
        ws = WS.next()
        wmq = ws.ap[:, 0:KC * 512].rearrange("p (c n) -> p c n", c=KC)
        load_w(ws, wmq, wsrc(w_in_d[l], 0, D, C_MQ, C_MQ + 512))
        for hh in range(4):
            for tb in range(TB):
                xr = [R_X[k][tb] for k in range(KC)]
                p = GEN.next()
                mm(p.ap[:, :], [(wmq[:, k, hh * 128:(hh + 1) * 128], xhi[:, k, tok(tb)]) for k in range(KC)], xr + [ws.res], p.res)
                V(lambda e, p=p, tb=tb: e.tensor_copy(out=qTt[:, tok(tb)], in_=p.ap[:, :]), [p.res], [R_q])

            def post_mem(qc, pv, den, hh=hh):
                r = TF.next()
                V(lambda e: e.reciprocal(out=r.ap, in_=den.ap[:, :]), [den.res], [r.res])
                V(lambda e: e.tensor_tensor(out=oT[:, 12 + hh, tok(qc)], in0=pv.ap[:, :], in1=r.ap, op=ALU.mult),
                  [pv.res, r.res], [R_O[12 + hh][qc]])

            attention(car_state, qTt, Km[:, hh, :], lambda kt, hh=hh: Vm[:, kt, hh * 128:(hh + 1) * 128], 0, 128, 128, 0, 2,
                      128.0 ** -0.5, [R_q], [R_kpe], [R_kpe], post_mem)

        if upto < 6:
            return
        fw.barrier()
        car = Carver()
        WG = Ring([Buf(car.bf(3072), fw.res("wm%d" % i), "wm%d" % i) for i in range(2)])
        WB = Ring([Buf(car.bf(2048), fw.res("wn%d" % i), "wn%d" % i) for i in range(2)])
        merged = car.bf(KC * HB).rearrange("p (c n) -> p c n", c=KC)
        R_m = [[fw.res("m%d_%d" % (c, sbk)) for sbk in range(SBH)] for c in range(KC)]
        zt = [car.f32(KC * 512).rearrange("p (c n) -> p c n", c=KC) for _ in range(1)]
        R_z = [[fw.res("z%d_%d" % (i, c)) for c in range(KC)] for i in range(1)]
        TF = Ring([Buf(car.f32(512), fw.res("tf%d" % i)) for i in range(6)])
        ZB = Ring([Buf(car.bf(512), fw.res("zb%d" % i)) for i in range(2)])
        GEN = Ring(PS[0:6])
        STAT = PS[6:8]
        USE["merge"] = car.off
        br_ranges = ((0, 4), (4, 12), (12, 16))

        def layernorm(tb, zi, eps, gvec, bvec, final_out):
            z = zt[zi]
            s1, s2 = STAT
            for c in range(KC):
                zb = ZB.next()
                A(lambda e, zb=zb, c=c: e.activation(out=zb.ap, in_=z[:, c, :], func=ACT.Copy), [R_z[zi][c]], [zb.res])
                fw.op("tensor", lambda e, zb=zb, c=c: e.matmul(s1.ap[:, :], lhsT=onesD[:, :], rhs=zb.ap, start=(c == 0), stop=(c == KC - 1)),
                      [zb.res, R_const], [s1.res])
                zq = ZB.next()
                A(lambda e, zq=zq, c=c: e.activation(out=zq.ap, in_=z[:, c, :], func=ACT.Square), [R_z[zi][c]], [zq.res])
                fw.op("tensor", lambda e, zq=zq, c=c: e.matmul(s2.ap[:, :], lhsT=onesD[:, :], rhs=zq.ap, start=(c == 0), stop=(c == KC - 1)),
                      [zq.res, R_const], [s2.res])
            msq = TF.next()
            A(lambda e: e.activation(out=msq.ap, in_=s1.ap[:, :], func=ACT.Square), [s1.res], [msq.res])
            vv = TF.next()
            V(lambda e: e.scalar_tensor_tensor(out=vv.ap, in0=s2.ap[:, :], scalar=float(eps), in1=msq.ap, op0=ALU.add, op1=ALU.subtract),
              [s2.res, msq.res], [vv.res])
            rsqrt_act(vv, vv.ap, [vv.res])
            nmr = TF.next()
            V(lambda e: e.scalar_tensor_tensor(out=nmr.ap, in0=s1.ap[:, :], scalar=-1.0, in1=vv.ap, op0=ALU.mult, op1=ALU.mult),
              [s1.res, vv.res], [nmr.res])
            for c in range(KC):
                rz = R_z[zi][c]
                V(lambda e, c=c: e.tensor_tensor(out=z[:, c, :], in0=z[:, c, :], in1=vv.ap, op=ALU.mult), [rz, vv.res], [rz])
                V(lambda e, c=c: e.tensor_tensor(out=z[:, c, :], in0=z[:, c, :], in1=nmr.ap, op=ALU.add), [rz, nmr.res], [rz])
                A(lambda e, c=c: e.activation(out=z[:, c, :], in_=z[:, c, :], func=ACT.Identity, bias=bvec[:, l, c:c + 1],
                                              scale=gvec[:, l, c:c + 1]), [rz, R_par], [rz])
                rx = R_X[c][tb]
                A(lambda e, c=c: e.activation(out=xhi[:, c, tok(tb)], in_=z[:, c, :], func=ACT.Copy), [rz], [rx])
                V(lambda e, c=c: e.tensor_tensor(out=xlo[:, c, tok(tb)], in0=z[:, c, :], in1=xhi[:, c, tok(tb)], op=ALU.subtract),
                  [rz, rx], [rx])
            if final_out:
                for t4 in range(4):
                    ob = OUTB.next()
                    for g in range(2):
                        p = GEN.next()
                        for cc in range(4):
                            c = g * 4 + cc
                            fw.op("tensor", lambda e, p=p, c=c, cc=cc, t4=t4: e.transpose(out=p.ap[:, cc * 128:(cc + 1) * 128],
                                                                                         in_=z[:, c, t4 * 128:(t4 + 1) * 128], identity=ident_f[:]),
                                  [R_z[zi][c], R_const], [p.res])
                        A(lambda e, p=p, ob=ob, g=g: e.activation(out=ob.ap[:, g * 512:(g + 1) * 512], in_=p.ap[:, :], func=ACT.Copy), [p.res], [ob.res])
                    t0 = tb * 512 + t4 * 128
                    ry = fw.res("y")
                    R_y.append(ry)
                    fw.dma("sync", lambda e, ob=ob, t0=t0: e.dma_start(out=y_d[s, t0:t0 + 128, :], in_=ob.ap), ob.key,
                           reads=[ob.res], writes=[ry])

        for th in range(NH):
            for c in range(KC):
                wsg = WG.next()
                wg = wsg.ap[:, 0:KC * 384].rearrange("p (k n) -> p k n", k=KC)
                for i in range(3):
                    c0 = C_GATE + i * D + c * 128
                    load_w(wsg, wg[:, :, i * 128:(i + 1) * 128], wsrc(w_in_d[l], 0, D, c0, c0 + 128))
                wsb = WB.next()
                wb = wsb.ap[:, 0:16 * 128].rearrange("p (k n) -> p k n", k=16)
                load_w(wsb, wb, wsrc(wbr_d[l], 0, 2048, c * 128, (c + 1) * 128))
                for sbk in range(SBH):
                    tb = th * SBH + sbk
                    xr = [R_X[k][tb] for k in range(KC)]
                    us = []
                    for i in range(3):
                        pg = GEN.next()
                        mm(pg.ap[:, :], [(wg[:, k, i * 128:(i + 1) * 128], xhi[:, k, tok(tb)]) for k in range(KC)], xr + [wsg.res], pg.res)
                        t_ = TF.next()
                        A(lambda e, pg=pg, t_=t_, i=i, c=c: e.activation(out=t_.ap, in_=pg.ap[:, :], func=ACT.Tanh,
                                                                        bias=bgh[:, l, i * 8 + c:i * 8 + c + 1], scale=0.5),
                          [pg.res, R_par], [t_.res])
                        pb = GEN.next()
                        k0, k1_ = br_ranges[i]
                        mm(pb.ap[:, :], [(wb[:, kk, :], oT[:, kk, tok(tb)]) for kk in range(k0, k1_)],
                           [R_O[kk][tb] for kk in range(k0, k1_)] + [wsb.res], pb.res)
                        V(lambda e, t_=t_, pb=pb: e.scalar_tensor_tensor(out=t_.ap, in0=t_.ap, scalar=1.0, in1=pb.ap[:, :],
                                                                        op0=ALU.add, op1=ALU.mult), [t_.res, pb.res], [t_.res])
                        us.append(t_)
                    V(lambda e, us=us: e.tensor_tensor(out=us[0].ap, in0=us[0].ap, in1=us[1].ap, op=ALU.add),
                      [us[0].res, us[1].res], [us[0].res])
                    V(lambda e, us=us, c=c, sbk=sbk: e.tensor_tensor(out=merged[:, c, sbk * 512:(sbk + 1) * 512], in0=us[0].ap, in1=us[2].ap, op=ALU.add),
                      [us[0].res, us[2].res], [R_m[c][sbk]])
            for sbk in range(SBH):
                tb = th * SBH + sbk
                for c2 in range(KC):
                    wso = WB.next()
                    wo = wso.ap[:, 0:KC * 128].rearrange("p (k n) -> p k n", k=KC)
                    load_w(wso, wo, wsrc(wout_d[l], 0, D, c2 * 128, (c2 + 1) * 128))
                    py = GEN.next()
                    mm(py.ap[:, :], [(wo[:, k, :], merged[:, k, sbk * 512:(sbk + 1) * 512]) for k in range(KC)],
                       [R_m[k][sbk] for k in range(KC)] + [wso.res], py.res)
                    rz = R_z[0][c2]
                    V(lambda e, py=py, c2=c2, tb=tb, z0=zt[0]: e.scalar_tensor_tensor(out=z0[:, c2, :], in0=xhi[:, c2, tok(tb)], scalar=2.0 * ALPHA,
                                                                           in1=py.ap[:, :], op0=ALU.mult, op1=ALU.add),
                      [py.res, R_X[c2][tb]], [rz])
                    V(lambda e, c2=c2, tb=tb, z0=zt[0]: e.scalar_tensor_tensor(out=z0[:, c2, :], in0=xlo[:, c2, tok(tb)], scalar=2.0 * ALPHA,
                                                                    in1=z0[:, c2, :], op0=ALU.mult, op1=ALU.add),
                      [R_X[c2][tb], rz], [rz])
                layernorm(tb, 0, 4.0 * LN_EPS, l1g, l1b, False)

        if upto < 7:
            return
        fw.barrier()
        car = Carver()
        W1 = Ring([Buf(car.bf(1024), fw.res("wa%d" % i), "wa%d" % i) for i in range(2)])
        W2 = Ring([Buf(car.bf(2048), fw.res("wb%d" % i), "wb%d" % i) for i in range(4)])
        zt = [car.f32(KC * 512).rearrange("p (c n) -> p c n", c=KC) for _ in range(SBH)]
        R_z = [[fw.res("zf%d_%d" % (i, c)) for c in range(KC)] for i in range(SBH)]
        TF = Ring([Buf(car.f32(512), fw.res("tg%d" % i)) for i in range(4)])
        ZB = Ring([Buf(car.bf(512), fw.res("zc%d" % i)) for i in range(2)])
        OUTB = Ring([Buf(car.f32(1024), fw.res("ob%d" % i), "ob%d" % i) for i in range(1)]) if last else None
        GEN = Ring(PS[0:6])
        STAT = PS[6:8]
        USE["ffn"] = car.off
        h1 = oT[:, :, :].rearrange("p c s -> p (c s)")[:, 0:32 * HB].rearrange("p (j t) -> p j t", j=32)

        def R_h(j, sbk):
            flat = j * HB + sbk * 512
            return R_O[flat // S][(flat % S) // 512]

        if KSTOP <= 0:
            return
        for th in range(NH):
            for j in range(32):
                ws = W1.next()
                w1c = ws.ap[:, 0:KC * 128].rearrange("p (k n) -> p k n", k=KC)
                load_w(ws, w1c, wsrc(w1_d[l], 0, D, j * 128, (j + 1) * 128))
                for sbk in range(SBH):
                    tb = th * SBH + sbk
                    xr = [R_X[k][tb] for k in range(KC)]
                    ph = GEN.next()
                    mm(ph.ap[:, :], [(w1c[:, k, :], xhi[:, k, tok(tb)]) for k in range(KC)], xr + [ws.res], ph.res)
                    r_ = TF.next()
                    V(lambda e, ph=ph, r_=r_: e.tensor_scalar(out=r_.ap, in0=ph.ap[:, :], scalar1=0.0, scalar2=None, op0=ALU.max), [ph.res], [r_.res])
                    A(lambda e, r_=r_, j=j, sbk=sbk: e.activation(out=h1[:, j, sbk * 512:(sbk + 1) * 512], in_=r_.ap, func=ACT.Square),
                      [r_.res], [R_h(j, sbk)])
                    if KDBG == 78 and j == 0 and sbk == 0 and th == 0:
                        dr = nc.dram_tensor("dbg_r", [128, 512], F32, kind="ExternalOutput").ap()
                        dw = nc.dram_tensor("dbg_w", [128, KC, 128], BF16, kind="ExternalOutput").ap()
                        dxx = nc.dram_tensor("dbg_x", [128, KC, 512], BF16, kind="ExternalOutput").ap()
                        dp = nc.dram_tensor("dbg_p", [128, 512], F32, kind="ExternalOutput").ap()
                        for nm_, dst_, src_, rd_ in (("r", dr, r_.ap, [r_.res]), ("w", dw, w1c, [ws.res]), ("x", dxx, xhi[:, :, tok(tb)], xr)):
                            rr_ = fw.res("dbg" + nm_)
                            R_y.append(rr_)
                            fw.dma("sync", lambda e, dst_=dst_, src_=src_: e.dma_start(out=dst_, in_=src_), "dbgk" + nm_, reads=rd_, writes=[rr_])
                        ph2 = GEN.next()
                        mm(ph2.ap[:, :], [(w1c[:, k, :], xhi[:, k, tok(tb)]) for k in range(KC)], xr + [ws.res], ph2.res)
                        r2_ = TF.next()
                        A(lambda e, ph2=ph2, r2_=r2_: e.activation(out=r2_.ap, in_=ph2.ap[:, :], func=ACT.Copy), [ph2.res], [r2_.res])
                        rr_ = fw.res("dbgp")
                        R_y.append(rr_)
                        fw.dma("sync", lambda e, dp=dp, r2_=r2_: e.dma_start(out=dp, in_=r2_.ap), "dbgkp", reads=[r2_.res], writes=[rr_])
            if KSTOP <= 1:
                return
            for c2 in range(KC):
                wsa = W2.next()
                w2a = wsa.ap[:, 0:16 * 128].rearrange("p (k n) -> p k n", k=16)
                load_w(wsa, w2a, wsrc(w2_d[l], 0, 2048, c2 * 128, (c2 + 1) * 128))
                wsb2 = W2.next()
                w2b = wsb2.ap[:, 0:16 * 128].rearrange("p (k n) -> p k n", k=16)
                load_w(wsb2, w2b, wsrc(w2_d[l], 2048, 4096, c2 * 128, (c2 + 1) * 128))
                for sbk in range(SBH):
                    tb = th * SBH + sbk
                    pf = GEN.next()
                    mm(pf.ap[:, :], [((w2a if j < 16 else w2b)[:, j % 16, :], h1[:, j, sbk * 512:(sbk + 1) * 512]) for j in range(32)],
                       [R_h(j, sbk) for j in range(32)] + [wsa.res, wsb2.res], pf.res)
                    rz = R_z[sbk][c2]
                    V(lambda e, pf=pf, c2=c2, tb=tb, zs=zt[sbk]: e.scalar_tensor_tensor(out=zs[:, c2, :], in0=xhi[:, c2, tok(tb)], scalar=ALPHA,
                                                                                    in1=pf.ap[:, :], op0=ALU.mult, op1=ALU.add),
                      [pf.res, R_X[c2][tb]], [rz])
                    V(lambda e, c2=c2, tb=tb, zs=zt[sbk]: e.scalar_tensor_tensor(out=zs[:, c2, :], in0=xlo[:, c2, tok(tb)], scalar=ALPHA,
                                                                             in1=zs[:, c2, :], op0=ALU.mult, op1=ALU.add),
                      [R_X[c2][tb], rz], [rz])
            if KSTOP <= 2:
                return
            for sbk in range(SBH):
                layernorm(th * SBH + sbk, sbk, LN_EPS, l2g, l2b, last)

    R_y = []
    USE = {}
    for s in range(NSEQ):
        fw.barrier()
        if upto >= 1:
            load_sequence(s)
        fw.barrier()
        if upto >= 2:
            build_tables(s)
        if upto >= 3:
            for l in range(NLAYER):
                layer(s, l, last=(l == NLAYER - 1))
    if KDBG == 77:
        dx = nc.dram_tensor("dbg_xhi", [128, KC, S], BF16, kind="ExternalOutput").ap()
        do = nc.dram_tensor("dbg_oT", [128, 16, S], BF16, kind="ExternalOutput").ap()
        fw.barrier()
        r1_, r2_ = fw.res("d1"), fw.res("d2")
        fw.dma("sync", lambda e: e.dma_start(out=dx, in_=xhi[:]), "dbg1", reads=[R_X[c][t] for c in range(KC) for t in range(TB)], writes=[r1_])
        fw.dma("sync", lambda e: e.dma_start(out=do, in_=oT[:]), "dbg2", reads=[R_O[c][t] for c in range(16) for t in range(TB)], writes=[r2_])
        R_y = R_y + [r1_, r2_]
    fw.op("sync", lambda e: e.nop(), reads=R_y, writes=[])
    counts = fw.finish()
    counts["arena"] = AR
    counts.update(USE)
    counts["nops"] = len(fw.ops)
    return nc, counts


_CACHE = {}
WEIGHT_KEYS = ["w_in", "b_gate", "mla_q_norm", "mla_kv_norm", "mla_w_uq", "mla_w_ukv", "diff_lambda",
               "diff_subln", "mem_w_kv", "w_branch", "w_out", "ln1_g", "ln1_b", "mlp_w1", "mlp_w2", "ln2_g", "ln2_b"]


def kernel(**inputs):
    x = np.ascontiguousarray(np.asarray(inputs["x"], dtype=np.float32))
    mem = np.ascontiguousarray(np.asarray(inputs["mem"], dtype=np.float32))
    pos = np.ascontiguousarray(np.asarray(inputs["positions"], dtype=np.int32))
    B, S, _ = x.shape
    nseq = B // N_CORES
    key = (nseq, S, DEPTH)
    if key not in _CACHE:
        _CACHE[key] = build_program(nseq, S, DEPTH)[0]
    nc = _CACHE[key]
    consts = _host_consts()
    shared = {k: np.ascontiguousarray(np.asarray(inputs[k], dtype=np.float32)) for k in WEIGHT_KEYS}
    shared["c_ident"] = consts["ident"]
    shared["c_pvec"] = consts["pvec"]
    shared["c_rot"] = consts["rot"]
    in_maps = []
    for c in range(N_CORES):
        m = dict(shared)
        m["x"] = x[c * nseq:(c + 1) * nseq]
        m["mem"] = mem[c * nseq:(c + 1) * nseq]
        m["positions"] = pos[c * nseq:(c + 1) * nseq]
        in_maps.append(m)
    res = run_bass_kernel_spmd(nc, in_maps, core_ids=list(range(N_CORES)))
    out = np.concatenate([np.asarray(r["y"]) for r in res.results], axis=0)
    return out.astype(np.float32)
```

```python
import math
import os
import numpy as np
KDBG = float(os.environ.get('KDBG', '99'))
KSTOP = int(os.environ.get('KSTOP', '99'))
import concourse.bass as bass
import concourse.mybir as mybir
from concourse.bass_utils import run_bass_kernel_spmd

F32 = mybir.dt.float32
BF16 = mybir.dt.bfloat16
I32 = mybir.dt.int32
ALU = mybir.AluOpType
ACT = mybir.ActivationFunctionType

D = 1024
KC = 8
MEM_LEN = 256
DEPTH = 4
N_CORES = 8
BATCH = 16
SEQ = 2048
ROPE_THETA = 500000.0
ALPHA = (2 * DEPTH) ** 0.25
LN_EPS = 1e-5
C_CQ, C_CKV, C_KPE, C_DQ, C_DK, C_DV, C_MQ, C_GATE = 0, 384, 640, 672, 1696, 2720, 3744, 4256


class Res:
    __slots__ = ("name", "last_writer", "readers", "excl")

    def __init__(self, name="", excl=False):
        self.name = name
        self.last_writer = None
        self.readers = []
        self.excl = excl


class Op:
    __slots__ = ("eng", "emit", "deps", "signal", "seq", "is_dma", "dma_sem", "dma_val", "big")

    def __init__(self, eng, emit, is_dma=False, big=False):
        self.eng = eng
        self.emit = emit
        self.deps = []
        self.signal = False
        self.seq = 0
        self.is_dma = is_dma
        self.dma_sem = None
        self.dma_val = 0
        self.big = big


class FW:
    ENGS = ("tensor", "vector", "scalar", "gpsimd", "sync")

    def __init__(self, nc, same_engine_sync=True):
        self.nc = nc
        self.ops = []
        self.same_engine_sync = same_engine_sync
        self.dma_sems = {}
        self.all_res = []

    def res(self, name="", excl=False):
        r = Res(name, excl)
        self.all_res.append(r)
        return r

    def op(self, eng, emit, reads=(), writes=(), big=False):
        o = Op(eng, emit, big=big)
        self._track(o, reads, writes)
        return o

    def dma(self, eng, emit, semkey, reads=(), writes=()):
        o = Op(eng, emit, is_dma=True)
        if semkey not in self.dma_sems:
            self.dma_sems[semkey] = [self.nc.alloc_semaphore("dq%d" % len(self.dma_sems)), 0]
        ent = self.dma_sems[semkey]
        ent[1] += 16
        o.dma_sem = ent[0]
        o.dma_val = ent[1]
        self._track(o, reads, writes)
        return o

    def _track(self, o, reads, writes):
        ex = [r for r in reads if r.excl]
        if ex:
            reads = [r for r in reads if not r.excl]
            writes = list(writes) + [r for r in ex if r not in writes]
        deps = []
        for r in reads:
            if r.last_writer is not None:
                deps.append(r.last_writer)
        for w in writes:
            if w.last_writer is not None:
                deps.append(w.last_writer)
            deps.extend(w.readers)
        for r in reads:
            r.readers.append(o)
        for w in writes:
            w.last_writer = o
            w.readers = []
        seen = set()
        for d in deps:
            if id(d) not in seen and d is not o:
                seen.add(id(d))
                o.deps.append(d)
        self.ops.append(o)

    def barrier(self):
        for e in self.ENGS:
            self.op(e, lambda eng: eng.nop(), reads=(), writes=self.all_res)

    def finish(self):
        nc = self.nc
        engs = {e: getattr(nc, e) for e in self.ENGS}
        sems = {e: nc.alloc_semaphore("eng_" + e) for e in self.ENGS}
        for o in self.ops:
            kept = []
            for d in o.deps:
                if d.is_dma:
                    kept.append(d)
                    continue
                if d.eng == o.eng and not o.is_dma:
                    if d.eng == "tensor":
                        continue
                    if not self.same_engine_sync:
                        continue
                    if d.big and o.big:
                        continue
                kept.append(d)
                d.signal = True
            o.deps = kept
        cnt = {e: 0 for e in self.ENGS}
        for o in self.ops:
            if o.signal and not o.is_dma:
                cnt[o.eng] += 1
                o.seq = cnt[o.eng]
        waited = {e: {} for e in self.ENGS}
        for o in self.ops:
            eng = engs[o.eng]
            need = {}
            for d in o.deps:
                if d.is_dma:
                    s, v = d.dma_sem, d.dma_val
                else:
                    s, v = sems[d.eng], d.seq
                if need.get(s.num, (None, 0))[1] < v:
                    need[s.num] = (s, v)
            for num, (s, v) in need.items():
                if waited[o.eng].get(num, 0) >= v:
                    continue
                eng.wait_ge(s, v)
                waited[o.eng][num] = v
            ins = o.emit(eng)
            if o.is_dma:
                ins.then_inc(o.dma_sem, 16)
            elif o.signal:
                ins.then_inc(sems[o.eng], 1)
        return cnt


class Buf:
    __slots__ = ("ap", "res", "key")

    def __init__(self, ap, res, key=None):
        self.ap = ap
        self.res = res
        self.key = key


class Ring:
    def __init__(self, items):
        self.items = items
        self.i = 0

    def next(self):
        it = self.items[self.i % len(self.items)]
        self.i += 1
        return it


def _host_consts():
    c = {}
    c["ident"] = np.eye(128, dtype=np.float32)
    inv_m = (ROPE_THETA ** (-np.arange(0, 32, 2, dtype=np.float32) / np.float32(32))).astype(np.float32)
    inv_d = (ROPE_THETA ** (-np.arange(0, 16, 2, dtype=np.float32) / np.float32(16))).astype(np.float32)
    pv = np.zeros((128, 8), dtype=np.float32)
    rot_m = np.zeros((128, 128), dtype=np.float32)
    rot_d = np.zeros((128, 128), dtype=np.float32)
    for i in range(32):
        p = 64 + i
        pv[p, 0] = inv_m[i % 16]
        pv[p, 1] = -1.0 if i < 16 else 1.0
        partner = 64 + (i + 16 if i < 16 else i - 16)
        rot_m[partner, p] = 1.0
    for o in (0, 64):
        for i in range(16):
            p = o + i
            pv[p, 2] = inv_d[i % 8]
            pv[p, 3] = -1.0 if i < 8 else 1.0
            partner = o + (i + 8 if i < 8 else i - 8)
            rot_d[partner, p] = 1.0
    c["pvec"] = pv
    c["rot"] = np.stack([rot_m, rot_d], axis=0)
    return c


def build_program(NSEQ, S, NLAYER, layer_ids=None, upto=99):
    assert S % 512 == 0
    TB = S // 512
    TT = S // 128
    HB = min(1024, S // 2)
    NH = S // HB
    SBH = HB // 512
    nc = bass.Bass("TRN2", target_bir_lowering=False)
    fw = FW(nc)
    L = NLAYER

    def din(name, shape, dt=F32):
        return nc.dram_tensor(name, list(shape), dt, kind="ExternalInput").ap()

    x_d = din("x", [NSEQ, S, D])
    mem_d = din("mem", [NSEQ, MEM_LEN, D])
    pos_d = din("positions", [NSEQ, S], I32)
    w_in_d = din("w_in", [L, D, 7328])
    b_gate_d = din("b_gate", [L, 3, D])
    qn_d = din("mla_q_norm", [L, 384])
    kvn_d = din("mla_kv_norm", [L, 256])
    wuq_d = din("mla_w_uq", [L, 384, 768])
    wukv_d = din("mla_w_ukv", [L, 256, 1024])
    dlam_d = din("diff_lambda", [L, 4, 64])
    subln_d = din("diff_subln", [L, 128])
    wkv_d = din("mem_w_kv", [L, D, 1024])
    wbr_d = din("w_branch", [L, 2048, D])
    wout_d = din("w_out", [L, D, D])
    ln1g_d = din("ln1_g", [L, D])
    ln1b_d = din("ln1_b", [L, D])
    w1_d = din("mlp_w1", [L, D, 4096])
    w2_d = din("mlp_w2", [L, 4096, D])
    ln2g_d = din("ln2_g", [L, D])
    ln2b_d = din("ln2_b", [L, D])
    ident_d = din("c_ident", [128, 128])
    pvec_d = din("c_pvec", [128, 8])
    rot_d_ = din("c_rot", [2, 128, 128])
    y_d = nc.dram_tensor("y", [NSEQ, S, D], F32, kind="ExternalOutput").ap()

    def sb(name, shape, dt):
        return nc.alloc_sbuf_tensor(name, list(shape), dt)

    xhi = sb("xhi", [128, KC, S], BF16)
    xlo = sb("xlo", [128, KC, S], BF16)
    oT = sb("oT", [128, 16, S], BF16)
    memT = sb("memT", [128, KC, MEM_LEN], BF16)
    ident_f = sb("ident_f", [128, 128], F32)
    ones_b = sb("ones_b", [128, 128], BF16)
    onesD = sb("onesD", [128, 128], BF16)
    rot_f = sb("rot_f", [128, 2, 128], F32)
    rot_b = sb("rot_b", [128, 2, 128], BF16)
    pvec = sb("pvec", [128, 8], F32)
    neghalf = sb("neghalf", [128, 512], F32)
    halfpi = sb("halfpi", [128, 1], F32)
    par = sb("par", [128, L, 64], F32)
    bgh = par[:, :, 0:24]
    gq = par[:, :, 24:27]
    gkv = par[:, :, 27:29]
    gsub = par[:, :, 29]
    l1g = par[:, :, 30:38]
    l1b = par[:, :, 38:46]
    l2g = par[:, :, 46:54]
    l2b = par[:, :, 54:62]
    dls = sb("dls", [128, L, 2], F32)
    nlam = sb("nlam", [128, L], F32)
    R_const = fw.res("const")
    R_par = fw.res("par")
    R_memT = fw.res("memT")
    R_X = [[fw.res("x%d_%d" % (c, t)) for t in range(TB)] for c in range(KC)]
    R_O = [[fw.res("o%d_%d" % (c, t)) for t in range(TB)] for c in range(16)]

    AR = (nc.sbuf_bytes_remaining - 2048) // 2 // 16 * 16
    arena = sb("arena", [128, AR], BF16)

    class Carver:
        def __init__(self):
            self.off = 0

        def bf(self, n):
            v = arena[:, self.off:self.off + n]
            self.off += n
            assert self.off <= AR, ("arena overflow", self.off, AR)
            return v

        def f32(self, n):
            return self.bf(2 * n).bitcast(F32)

    psb = [nc.alloc_psum_tensor("ps%d" % i, [128, 512], F32) for i in range(8)]
    PS = [Buf(psb[i], fw.res("ps%d" % i, excl=True)) for i in range(8)]

    def V(fn, reads, writes):
        fw.op("vector", fn, reads, writes)

    def A(fn, reads, writes):
        fw.op("scalar", fn, reads, writes)

    def G(fn, reads, writes):
        fw.op("gpsimd", fn, reads, writes)

    def mm(out_ap, pairs, reads, wres):
        n = len(pairs)
        for i, (l_, r_) in enumerate(pairs):
            fw.op("tensor",
                  lambda e, l_=l_, r_=r_, i=i: e.matmul(out_ap, lhsT=l_, rhs=r_, start=(i == 0), stop=(i == n - 1)),
                  reads, [wres])

    def load_w(slot, dst_ap, src_ap):
        fw.dma("gpsimd", lambda e: e.dma_start(out=dst_ap, in_=src_ap), slot.key, writes=[slot.res])

    def wsrc(w_ap, r0, r1, c0, c1):
        return w_ap[r0:r1, c0:c1].rearrange("(c p) n -> p c n", p=128)

    def tok(tb):
        return slice(tb * 512, (tb + 1) * 512)

    fw.dma("sync", lambda e: e.dma_start(out=ident_f[:], in_=ident_d), "c0", writes=[R_const])
    fw.dma("sync", lambda e: e.dma_start(out=pvec[:], in_=pvec_d), "c1", writes=[R_const])
    fw.dma("sync", lambda e: e.dma_start(out=rot_f[:], in_=rot_d_.rearrange("r k m -> k r m")), "c2", writes=[R_const])
    V(lambda e: e.tensor_copy(out=rot_b[:], in_=rot_f[:]), [R_const], [R_const])
    V(lambda e: e.memset(ones_b[:], 1.0), [], [R_const])
    V(lambda e: e.memset(onesD[:], 1.0 / 1024.0), [], [R_const])
    V(lambda e: e.memset(neghalf[:], -0.5), [], [R_const])
    V(lambda e: e.memset(halfpi[:], math.pi / 2), [], [R_const])

    car0 = Carver()
    dl = car0.f32(L * 256).rearrange("p (l n) -> p l n", l=L)
    dlp = car0.f32(L * 128).rearrange("p (l n) -> p l n", l=L)

    stage = car0.f32(L * 128).rearrange("p (l n) -> p l n", l=L)
    R_stage = fw.res("stage")
    for l in range(L):
        rows = [(0, b_gate_d[l].rearrange("i (c p) -> (i c) p", p=128), 24),
                (24, qn_d[l].rearrange("(c p) -> c p", p=128), 3),
                (27, kvn_d[l].rearrange("(c p) -> c p", p=128), 2),
                (29, subln_d[l].rearrange("(c p) -> c p", p=128), 1),
                (30, ln1g_d[l].rearrange("(c p) -> c p", p=128), 8),
                (38, ln1b_d[l].rearrange("(c p) -> c p", p=128), 8),
                (46, ln2g_d[l].rearrange("(c p) -> c p", p=128), 8),
                (54, ln2b_d[l].rearrange("(c p) -> c p", p=128), 8)]
        for (r0, src, n) in rows:
            fw.dma("sync", lambda e, r0=r0, src=src, n=n, l=l: e.dma_start(out=stage[r0:r0 + n, l, :], in_=src), "pst", writes=[R_stage])
        fw.dma("sync", lambda e, l=l: e.dma_start(out=dl[:, l, :], in_=dlam_d[l].rearrange("a b -> (a b)").partition_broadcast(128)),
               "p8", writes=[R_par])
    for l in range(L):
        pp = PS[l % 8]
        fw.op("tensor", lambda e, l=l, pp=pp: e.transpose(out=pp.ap[:, 0:62], in_=stage[0:62, l, :], identity=ident_f[0:62, 0:62]),
              [R_stage, R_const], [pp.res])
        V(lambda e, l=l, pp=pp: e.tensor_copy(out=par[:, l, 0:62], in_=pp.ap[:, 0:62]), [pp.res], [R_par])
    for l in range(L):
        lam_init = 0.8 - 0.6 * math.exp(-0.3 * ((layer_ids[l]) if layer_ids is not None else l))
        V(lambda e, l=l: e.tensor_scalar(out=bgh[:, l, :], in0=bgh[:, l, :], scalar1=0.5, scalar2=None, op0=ALU.mult), [R_par], [R_par])
        V(lambda e, l=l: e.tensor_tensor(out=dlp[:, l, 0:64], in0=dl[:, l, 0:64], in1=dl[:, l, 64:128], op=ALU.mult), [R_par], [R_par])
        V(lambda e, l=l: e.tensor_tensor(out=dlp[:, l, 64:128], in0=dl[:, l, 128:192], in1=dl[:, l, 192:256], op=ALU.mult), [R_par], [R_par])
        V(lambda e, l=l: e.reduce_sum(out=dls[:, l, 0:1], in_=dlp[:, l, 0:64], axis=mybir.AxisListType.X), [R_par], [R_par])
        V(lambda e, l=l: e.reduce_sum(out=dls[:, l, 1:2], in_=dlp[:, l, 64:128], axis=mybir.AxisListType.X), [R_par], [R_par])
        A(lambda e, l=l: e.activation(out=dls[:, l, :], in_=dls[:, l, :], func=ACT.Exp), [R_par], [R_par])
        V(lambda e, l=l: e.tensor_tensor(out=nlam[:, l:l + 1], in0=dls[:, l, 1:2], in1=dls[:, l, 0:1], op=ALU.subtract), [R_par], [R_par])
        V(lambda e, l=l, li=lam_init: e.tensor_scalar(out=nlam[:, l:l + 1], in0=nlam[:, l:l + 1], scalar1=-li, scalar2=None, op0=ALU.add), [R_par], [R_par])

    def attn_chunk(car_state, qc, qT, kT, vfn, kb, Kdim, dv, ro, nk, scale, q_reads, k_reads, v_reads, post, merged_den=False):
        SC, ACC, PT, pending = car_state
        pv = ACC.next()
        den = None if merged_den else ACC.next()

        def scores(kt):
            sc = SC.next()
            mm(sc.ap[:, :], [(kT[kb:kb + Kdim, kt * 128:(kt + 1) * 128], qT[kb:kb + Kdim, tok(qc)])],
               q_reads + k_reads, sc.res)
            return sc

        sc_next = scores(0)
        for kt in range(nk):
            sc = sc_next
            pt = PT.next()
            A(lambda e, sc=sc, pt=pt: e.activation(out=pt.ap, in_=sc.ap[:, :], func=ACT.Exp, scale=scale),
              [sc.res], [pt.res])
            if kt + 1 < nk:
                sc_next = scores(kt + 1)
            fw.op("tensor", lambda e, pt=pt, kt=kt: e.matmul(pv.ap[ro:ro + dv, :], lhsT=vfn(kt), rhs=pt.ap,
                                                             start=(kt == 0), stop=(kt == nk - 1)),
                  [pt.res] + v_reads, [pv.res])
            if not merged_den:
                fw.op("tensor", lambda e, pt=pt, kt=kt: e.matmul(den.ap[ro:ro + dv, :], lhsT=ones_b[:, 0:dv], rhs=pt.ap,
                                                                 start=(kt == 0), stop=(kt == nk - 1)),
                      [pt.res, R_const], [den.res])
            if kt == min(3, nk - 1) and pending:
                for f_ in pending:
                    f_()
                del pending[:]
        post(qc, pv, den)

    def attention(car_state, qT, kT, vfn, kb, Kdim, dv, ro, nk, scale, q_reads, k_reads, v_reads, post):
        for qc in range(TB):
            attn_chunk(car_state, qc, qT, kT, vfn, kb, Kdim, dv, ro, nk, scale, q_reads, k_reads, v_reads, post)

    _csts = {}

    def cst(val):
        key = float(val)
        if key not in _csts:
            t = sb("cst%d" % len(_csts), [128, 1], F32)
            r = fw.res("cst")
            V(lambda e, t=t, key=key: e.memset(t[:], key), [], [r])
            _csts[key] = (t, r)
        return _csts[key]

    def rsqrt_act(dst, src_ap, reads, scale=1.0, bias=None):
        if bias is None:
            A(lambda e: e.activation(out=dst.ap, in_=src_ap, func=ACT.Ln, scale=float(scale)), reads, [dst.res])
        else:
            bt, br = cst(bias)
            A(lambda e: e.activation(out=dst.ap, in_=src_ap, func=ACT.Ln, scale=float(scale), bias=bt[:, 0:1]),
              reads + [br], [dst.res])
        A(lambda e: e.activation(out=dst.ap, in_=dst.ap, func=ACT.Exp, scale=-0.5), [dst.res], [dst.res])

    def load_sequence(s):
        car = Carver()
        xin = [Buf(car.f32(1024), fw.res("xin%d" % i), "xin%d" % i) for i in range(2)]
        ring = Ring(xin)
        pr = Ring(PS)
        for tt in range(TT):
            xb = ring.next()
            fw.dma("sync", lambda e, xb=xb, tt=tt: e.dma_start(out=xb.ap, in_=x_d[s, tt * 128:(tt + 1) * 128, :]),
                   xb.key, writes=[xb.res])
            tb = tt // 4
            for g in range(2):
                p = pr.next()
                for cc in range(4):
                    c = g * 4 + cc
                    fw.op("tensor", lambda e, p=p, xb=xb, c=c, cc=cc: e.transpose(out=p.ap[:, cc * 128:(cc + 1) * 128],
                                                                                 in_=xb.ap[:, c * 128:(c + 1) * 128], identity=ident_f[:]),
                          [xb.res, R_const], [p.res])
                wr = [R_X[g * 4 + cc][tb] for cc in range(4)]
                hi_v = xhi[:, g * 4:g * 4 + 4, tt * 128:(tt + 1) * 128]
                lo_v = xlo[:, g * 4:g * 4 + 4, tt * 128:(tt + 1) * 128]
                pv3 = p.ap[:, :].rearrange("p (a b) -> p a b", a=4)
                A(lambda e, hi_v=hi_v, pv3=pv3: e.activation(out=hi_v, in_=pv3, func=ACT.Copy), [p.res], wr)
                V(lambda e, lo_v=lo_v, hi_v=hi_v, pv3=pv3: e.tensor_tensor(out=lo_v, in0=pv3, in1=hi_v, op=ALU.subtract),
                  [p.res] + wr, wr)
        for mt in range(MEM_LEN // 128):
            xb = ring.next()
            fw.dma("sync", lambda e, xb=xb, mt=mt: e.dma_start(out=xb.ap, in_=mem_d[s, mt * 128:(mt + 1) * 128, :]),
                   xb.key, writes=[xb.res])
            for g in range(2):
                p = pr.next()
                for cc in range(4):
                    c = g * 4 + cc
                    fw.op("tensor", lambda e, p=p, xb=xb, c=c, cc=cc: e.transpose(out=p.ap[:, cc * 128:(cc + 1) * 128],
                                                                                 in_=xb.ap[:, c * 128:(c + 1) * 128], identity=ident_f[:]),
                          [xb.res, R_const], [p.res])
                V(lambda e, p=p, g=g, mt=mt: e.tensor_copy(out=memT[:, g * 4:g * 4 + 4, mt * 128:(mt + 1) * 128],
                                                          in_=p.ap[:, :].rearrange("p (a b) -> p a b", a=4)), [p.res], [R_memT])

    tab_d = nc.dram_tensor("tab_scratch", [4, 128, S], BF16, kind="Internal").ap()
    R_tabd = fw.res("tabd")

    def build_tables(s):
        car = Carver()
        posi = car.bf(2 * S).bitcast(I32)
        posf = car.f32(S)
        ang = car.f32(S)
        t1 = car.f32(S)
        t2 = car.f32(S)
        ki = car.bf(2 * S).bitcast(I32)
        tb16 = car.bf(S)
        R_t = fw.res("tabtmp")
        fw.dma("sync", lambda e: e.dma_start(out=posi, in_=pos_d[s, :].partition_broadcast(128)), "posi", writes=[R_t])
        V(lambda e: e.tensor_copy(out=posf, in_=posi), [R_t], [R_t])
        inv2pi = float(1.0 / (2 * math.pi))
        for ti, (fcol, scol) in enumerate([(0, 1), (2, 3)]):
            V(lambda e, fcol=fcol: e.tensor_scalar(out=ang, in0=posf, scalar1=pvec[:, fcol:fcol + 1], scalar2=None, op0=ALU.mult),
              [R_t, R_const], [R_t])
            for which in range(2):
                if which == 0:
                    V(lambda e: e.tensor_scalar(out=t1, in0=ang, scalar1=halfpi[:, 0:1], scalar2=None, op0=ALU.add), [R_t, R_const], [R_t])
                    src = t1
                else:
                    src = ang
                V(lambda e, src=src: e.tensor_scalar(out=t2, in0=src, scalar1=inv2pi, scalar2=None, op0=ALU.mult), [R_t], [R_t])
                V(lambda e: e.tensor_copy(out=ki, in_=t2), [R_t], [R_t])
                V(lambda e: e.tensor_copy(out=t2, in_=ki), [R_t], [R_t])
                V(lambda e, src=src: e.scalar_tensor_tensor(out=t2, in0=t2, scalar=float(-2 * math.pi), in1=src, op0=ALU.mult, op1=ALU.add),
                  [R_t], [R_t])
                A(lambda e: e.activation(out=t2, in_=t2, func=ACT.Sin), [R_t], [R_t])
                if which == 0:
                    V(lambda e: e.tensor_copy(out=tb16, in_=t2), [R_t], [R_t])
                else:
                    V(lambda e, scol=scol: e.tensor_scalar(out=tb16, in0=t2, scalar1=pvec[:, scol:scol + 1], scalar2=None, op0=ALU.mult),
                      [R_t, R_const], [R_t])
                idx = ti * 2 + which
                fw.dma("sync", lambda e, idx=idx: e.dma_start(out=tab_d[idx], in_=tb16), "tabst", reads=[R_t], writes=[R_tabd])

    def layer(s, l, last):
        lid = layer_ids[l] if layer_ids is not None else l
        lam_init = 0.8 - 0.6 * math.exp(-0.3 * lid)
        fw.barrier()
        car = Carver()
        tabs = car.bf(4 * S).rearrange("p (a b) -> p a b", a=4)
        R_tab = fw.res("tab")
        Cm, Sm, Cd, Sd = tabs[:, 0, :], tabs[:, 1, :], tabs[:, 2, :], tabs[:, 3, :]
        fw.dma("sync", lambda e: e.dma_start(out=tabs, in_=tab_d.rearrange("a p s -> p a s")), "tabld", reads=[R_tabd], writes=[R_tab])
        WS = Ring([Buf(car.bf(4096), fw.res("w%d" % i), "w%d" % i) for i in range(2)])
        qTt = car.bf(S)
        kTt = car.bf(S)
        vt = car.bf(TT * 128).rearrange("p (a b) -> p a b", a=TT)
        R_q, R_k, R_v = fw.res("q"), fw.res("k"), fw.res("v")
        PT = Ring([Buf(car.bf(512), fw.res("pt%d" % i)) for i in range(3)])
        TF = Ring([Buf(car.f32(512), fw.res("tf%d" % i)) for i in range(5)])
        TBf = Ring([Buf(car.bf(512), fw.res("tb%d" % i)) for i in range(2)])
        qz1 = car.bf(S)
        R_q1 = fw.res("q1")
        kpe = car.bf(max(S, 2048))
        R_kpe = fw.res("kpe")
        SC = Ring(PS[0:2])
        ACC = Ring(PS[2:6])
        GEN = Ring(PS[6:8])
        pending = []
        car_state = (SC, ACC, PT, pending)

        def flush():
            for f_ in pending:
                f_()
            del pending[:]
        USE["attn"] = car.off
        cqn = lambda j, tb: oT[:, 4 + j, tok(tb)]
        ckvn = lambda j, tb: oT[:, 7 + j, tok(tb)]
        R_cq = lambda j, tb: R_O[4 + j][tb]
        R_ckv = lambda j, tb: R_O[7 + j][tb]

        def rope_rows(dst_ap, a_ps, rows, tabC, tabS, rotsel, tsl, reads, wres, split=None):
            ab = TBf.next()
            A(lambda e: e.activation(out=ab.ap[rows, :], in_=a_ps.ap[rows, :], func=ACT.Copy), [a_ps.res], [ab.res])
            bp = GEN.next()
            mm(bp.ap[:, :], [(rot_b[:, rotsel, :], ab.ap)], [ab.res, R_const], bp.res)
            t1 = TF.next()
            V(lambda e: e.tensor_tensor(out=t1.ap[rows, :], in0=a_ps.ap[rows, :], in1=tabC[rows, tsl], op=ALU.mult),
              [a_ps.res, R_tab], [t1.res])
            t2 = TF.next()
            V(lambda e: e.tensor_tensor(out=t2.ap[rows, :], in0=bp.ap[rows, :], in1=tabS[rows, tsl], op=ALU.mult),
              [bp.res, R_tab], [t2.res])
            if split is None:
                V(lambda e: e.tensor_tensor(out=dst_ap, in0=t1.ap[rows, :], in1=t2.ap[rows, :], op=ALU.add),
                  [t1.res, t2.res] + reads, [wres])
            else:
                for (d_ap, d_res, pr) in split:
                    V(lambda e, d_ap=d_ap, pr=pr: e.tensor_tensor(out=d_ap, in0=t1.ap[pr, :], in1=t2.ap[pr, :], op=ALU.add),
                      [t1.res, t2.res] + reads, [d_res])

        for b_ in TBf.items:
            V(lambda e, b_=b_: e.memset(b_.ap, 0.0), [], [b_.res])

        ws = WS.next()
        wcq = ws.ap[:, 0:KC * 384].rearrange("p (c n) -> p c n", c=KC)
        load_w(ws, wcq, wsrc(w_in_d[l], 0, D, C_CQ, C_CQ + 384))
        ws2 = WS.next()
        wck = ws2.ap[:, 0:KC * 288].rearrange("p (c n) -> p c n", c=KC)
        load_w(ws2, wck, wsrc(w_in_d[l], 0, D, C_CKV, C_CKV + 288))
        for tb in range(TB):
            xr = [R_X[k][tb] for k in range(KC)]
            for (wt, wsl, nj, cols, gvec, dstf, rdst, eps, nfe) in (
                    (wcq, ws, 3, 0, gq, cqn, R_cq, 1e-6, 384.0),
                    (wck, ws2, 2, 0, gkv, ckvn, R_ckv, 1e-6, 256.0)):
                pj = []
                sqs = []
                for j in range(nj):
                    p = ACC.next()
                    pj.append(p)
                    mm(p.ap[:, :], [(wt[:, k, cols + j * 128:cols + (j + 1) * 128], xhi[:, k, tok(tb)]) for k in range(KC)],
                       xr + [wsl.res], p.res)
                    sq = PT.next()
                    A(lambda e, p=p, sq=sq: e.activation(out=sq.ap, in_=p.ap[:, :], func=ACT.Square), [p.res], [sq.res])
                    sqs.append(sq)
                pss = GEN.next()
                mm(pss.ap[:, :], [(ones_b[:, :], sq.ap) for sq in sqs], [sq.res for sq in sqs] + [R_const], pss.res)
                rs = TF.next()
                rsqrt_act(rs, pss.ap[:, :], [pss.res], scale=1.0 / nfe, bias=eps)
                for j in range(nj):
                    V(lambda e, j=j, p=pj[j], rs=rs, gvec=gvec, dap=dstf(j, tb): e.scalar_tensor_tensor(
                        out=dap, in0=p.ap[:, :], scalar=gvec[:, l, j:j + 1], in1=rs.ap, op0=ALU.mult, op1=ALU.mult),
                      [pj[j].res, rs.res, R_par], [rdst(j, tb)])
            p = ACC.next()
            mm(p.ap[64:96, :], [(wck[:, k, 256:288], xhi[:, k, tok(tb)]) for k in range(KC)], xr + [ws2.res], p.res)
            rope_rows(kpe[64:96, tok(tb)], p, slice(64, 96), Cm, Sm, 0, tok(tb), [], R_kpe)
        for h in range(8):
            ws = WS.next()
            wq = ws.ap[:, 0:3 * 96].rearrange("p (c n) -> p c n", c=3)
            wkv_ = ws.ap[:, 512:512 + 2 * 128].rearrange("p (c n) -> p c n", c=2)
            load_w(ws, wq, wsrc(wuq_d[l], 0, 384, h * 96, (h + 1) * 96))
            load_w(ws, wkv_, wsrc(wukv_d[l], 0, 256, h * 128, (h + 1) * 128))
            for tb in range(TB):
                cqr = [R_cq(j, tb) for j in range(3)]
                ckr = [R_ckv(j, tb) for j in range(2)]
                p = GEN.next()
                mm(p.ap[0:96, :], [(wq[:, j, :], cqn(j, tb)) for j in range(3)], cqr + [ws.res], p.res)
                V(lambda e, p=p, tb=tb: e.tensor_copy(out=qTt[0:64, tok(tb)], in_=p.ap[0:64, :]), [p.res], [R_q])
                rope_rows(qTt[64:96, tok(tb)], p, slice(64, 96), Cm, Sm, 0, tok(tb), [], R_q)
                p2 = GEN.next()
                mm(p2.ap[0:64, :], [(wkv_[:, j, 0:64], ckvn(j, tb)) for j in range(2)], ckr + [ws.res], p2.res)
                V(lambda e, p2=p2, tb=tb: e.tensor_copy(out=kTt[0:64, tok(tb)], in_=p2.ap[0:64, :]), [p2.res], [R_k])
                V(lambda e, tb=tb: e.tensor_copy(out=kTt[64:96, tok(tb)], in_=kpe[64:96, tok(tb)]), [R_kpe], [R_k])
                p3 = GEN.next()
                for t4 in range(4):
                    tsl = slice(tb * 512 + t4 * 128, tb * 512 + (t4 + 1) * 128)
                    mm(p3.ap[:, t4 * 64:(t4 + 1) * 64], [(oT[:, 7 + j, tsl], wkv_[:, j, 64:128]) for j in range(2)], ckr + [ws.res], p3.res)
                V(lambda e, p3=p3, tb=tb, ro=(h % 2) * 64: e.tensor_copy(out=vt[:, tb * 4:(tb + 1) * 4, ro:ro + 64],
                                                        in_=p3.ap[:, 0:256].rearrange("p (a b) -> p a b", a=4)), [p3.res], [R_v])
            ro = (h % 2) * 64
            V(lambda e, ro=ro: e.memset(vt[:, :, 64 - ro:128 - ro], 1.0), [], [R_v])

            def post_mla(qc, acc, _den, h=h, ro=ro):
                prow = slice(ro, ro + 64)
                drow = slice(64 - ro, 128 - ro)
                T = TF.next()
                V(lambda e: e.reciprocal(out=T.ap[drow, :], in_=acc.ap[drow, :]), [acc.res], [T.res])

                def part_b():
                    g = GEN.next()
                    fw.op("tensor", lambda e: e.matmul(g.ap[prow, :], lhsT=ident_f[drow, drow], rhs=T.ap[drow, :], start=True, stop=True),
                          [T.res, R_const], [g.res])
                    rs = TF.next()
                    A(lambda e: e.activation(out=rs.ap[prow, :], in_=g.ap[prow, :], func=ACT.Copy), [g.res], [rs.res])
                    V(lambda e: e.tensor_tensor(out=oT[prow, h // 2, tok(qc)], in0=acc.ap[prow, :], in1=rs.ap[prow, :], op=ALU.mult),
                      [acc.res, rs.res], [R_O[h // 2][qc]])
                pending.append(part_b)

            for qc in range(TB):
                attn_chunk(car_state, qc, qTt, kTt, lambda kt: vt[:, kt, :], 0, 96, 128, 0, TT, 96.0 ** -0.5,
                           [R_q], [R_k], [R_v], post_mla, merged_den=True)
            flush()

        if upto < 4:
            return
        k1 = 1.0 / (128.0 * (1.0 - lam_init) ** 2)
        k2 = 1e-5 / ((1.0 - lam_init) ** 2)
        V(lambda e: e.memset(qTt[64:128, :], 0.0), [], [R_q])
        V(lambda e: e.memset(qz1[0:64, :], 0.0), [], [R_q1])
        for h in range(8):
            ws = WS.next()
            w3 = ws.ap[:, 0:KC * 384].rearrange("p (c n) -> p c n", c=KC)
            for i3, c0 in enumerate((C_DQ, C_DK, C_DV)):
                load_w(ws, w3[:, :, i3 * 128:(i3 + 1) * 128], wsrc(w_in_d[l], 0, D, c0 + h * 128, c0 + (h + 1) * 128))
            for tb in range(TB):
                xr = [R_X[k][tb] for k in range(KC)]
                for i3 in range(2):
                    p = GEN.next()
                    mm(p.ap[:, :], [(w3[:, k, i3 * 128:(i3 + 1) * 128], xhi[:, k, tok(tb)]) for k in range(KC)], xr + [ws.res], p.res)
                    if i3 == 0:
                        rope_rows(None, p, slice(0, 128), Cd, Sd, 1, tok(tb), [], None,
                                  split=[(qTt[0:64, tok(tb)], R_q, slice(0, 64)), (qz1[64:128, tok(tb)], R_q1, slice(64, 128))])
                    else:
                        rope_rows(kTt[:, tok(tb)], p, slice(0, 128), Cd, Sd, 1, tok(tb), [], R_k)
                p3 = GEN.next()
                for t4 in range(4):
                    tsl = slice(tb * 512 + t4 * 128, tb * 512 + (t4 + 1) * 128)
                    mm(p3.ap[:, t4 * 128:(t4 + 1) * 128], [(xhi[:, k, tsl], w3[:, k, 256:384]) for k in range(KC)], xr + [ws.res], p3.res)
                for t4 in range(4):
                    V(lambda e, p3=p3, tb=tb, t4=t4: e.tensor_copy(out=vt[:, tb * 4 + t4, :], in_=p3.ap[:, t4 * 128:(t4 + 1) * 128]), [p3.res], [R_v])
            keep = {}

            def post_d0(qc, pv, den):
                r0 = TF.next()
                V(lambda e: e.reciprocal(out=r0.ap, in_=den.ap[:, :]), [den.res], [r0.res])
                a_ = TF.next()
                V(lambda e: e.tensor_tensor(out=a_.ap, in0=pv.ap[:, :], in1=r0.ap, op=ALU.mult), [pv.res, r0.res], [a_.res])
                keep[qc] = a_

            def post_d1(qc, pv, den, h=h):
                a_ = keep[qc]
                r1 = TF.next()
                V(lambda e: e.reciprocal(out=r1.ap, in_=den.ap[:, :]), [den.res], [r1.res])
                b_ = TF.next()
                V(lambda e: e.scalar_tensor_tensor(out=b_.ap, in0=pv.ap[:, :], scalar=nlam[:, l:l + 1], in1=r1.ap,
                                                   op0=ALU.mult, op1=ALU.mult), [pv.res, r1.res, R_par], [b_.res])
                V(lambda e: e.tensor_tensor(out=a_.ap, in0=a_.ap, in1=b_.ap, op=ALU.add), [a_.res, b_.res], [a_.res])
                sq = TBf.next()
                A(lambda e: e.activation(out=sq.ap, in_=a_.ap, func=ACT.Square), [a_.res], [sq.res])

                def part_b():
                    pss = GEN.next()
                    mm(pss.ap[:, :], [(ones_b[:, :], sq.ap)], [sq.res, R_const], pss.res)
                    vv = TF.next()
                    rsqrt_act(vv, pss.ap[:, :], [pss.res], scale=k1, bias=k2)
                    V(lambda e: e.scalar_tensor_tensor(out=oT[:, 4 + h, tok(qc)], in0=a_.ap, scalar=par[:, l, 29:30],
                                                       in1=vv.ap, op0=ALU.mult, op1=ALU.mult),
                      [a_.res, vv.res, R_par], [R_O[4 + h][qc]])
                pending.append(part_b)

            for qc in range(TB):
                for qz, rq, pst in ((qTt, R_q, post_d0), (qz1, R_q1, post_d1)):
                    attn_chunk(car_state, qc, qz, kTt, lambda kt: vt[:, kt, :], 0, 128, 128, 0, TT, 64.0 ** -0.5,
                               [rq], [R_k], [R_v], pst)
            flush()

        if upto < 5:
            return
        Km = kpe[:, 0:4 * MEM_LEN].rearrange("p (a b) -> p a b", a=4)
        Vm = kpe[:, 4 * MEM_LEN:4 * MEM_LEN + 2 * 512].rearrange("p (a b) -> p a b", a=2)
        for half in range(2):
            ws = WS.next()
            wk_ = ws.ap[:, 0:KC * 512].rearrange("p (c n) -> p c n", c=KC)
            load_w(ws, wk_, wsrc(wkv_d[l], 0, D, half * 512, (half + 1) * 512))
            if half == 0:
                for hh in range(4):
                    p = GEN.next()
                    mm(p.ap[:, 0:MEM_LEN], [(wk_[:, k, hh * 128:(hh + 1) * 128], memT[:, k, :]) for k in range(KC)], [R_memT, ws.res], p.res)
                    V(lambda e, p=p, hh=hh: e.tensor_copy(out=Km[:, hh, :], in_=p.ap[:, 0:MEM_LEN]), [p.res], [R_kpe])
            else:
                for mt in range(2):
                    p = GEN.next()
                    mm(p.ap[:, :], [(memT[:, k, mt * 128:(mt + 1) * 128], wk_[:, k, :]) for k in range(KC)], [R_memT, ws.res], p.res)
                    V(lambda e, p=p, mt=mt: e.tensor_copy(out=Vm[:, mt, :], in_=p.ap[:, :]), [p.res], [R_kpe])
        ws = WS.next()
        wmq = ws.ap[:, 0:KC * 512].rearrange("p (c n) -> p c n", c=KC)
        load_w(ws, wmq, wsrc(w_in_d[l], 0, D, C_MQ, C_MQ + 512))
        for hh in range(4):
            for tb in range(TB):
                xr = [R_X[k][tb] for k in range(KC)]
                p = GEN.next()
                mm(p.ap[:, :], [(wmq[:, k, hh * 128:(hh + 1) * 128], xhi[:, k, tok(tb)]) for k in range(KC)], xr + [ws.res], p.res)
                V(lambda e, p=p, tb=tb: e.tensor_copy(out=qTt[:, tok(tb)], in_=p.ap[:, :]), [p.res], [R_q])

            def post_mem(qc, pv, den, hh=hh):
                r = TF.next()
                V(lambda e: e.reciprocal(out=r.ap, in_=den.ap[:, :]), [den.res], [r.res])
                V(lambda e: e.tensor_tensor(out=oT[:, 12 + hh, tok(qc)], in0=pv.ap[:, :], in1=r.ap, op=ALU.mult),
                  [pv.res, r.res], [R_O[12 + hh][qc]])

            attention(car_state, qTt, Km[:, hh, :], lambda kt, hh=hh: Vm[:, kt, hh * 128:(hh + 1) * 128], 0, 128, 128, 0, 2,
                      128.0 ** -0.5, [R_q], [R_kpe], [R_kpe], post_mem)

        if upto < 6:
            return
        fw.barrier()
        car = Carver()
        WG = Ring([Buf(car.bf(3072), fw.res("wm%d" % i), "wm%d" % i) for i in range(2)])
        WB = Ring([Buf(car.bf(2048), fw.res("wn%d" % i), "wn%d" % i) for i in range(2)])
        merged = car.bf(KC * HB).rearrange("p (c n) -> p c n", c=KC)
        R_m = [[fw.res("m%d_%d" % (c, sbk)) for sbk in range(SBH)] for c in range(KC)]
        zt = [car.f32(KC * 512).rearrange("p (c n) -> p c n", c=KC) for _ in range(1)]
        R_z = [[fw.res("z%d_%d" % (i, c)) for c in range(KC)] for i in range(1)]
        TF = Ring([Buf(car.f32(512), fw.res("tf%d" % i)) for i in range(6)])
        ZB = Ring([Buf(car.bf(512), fw.res("zb%d" % i)) for i in range(2)])
        GEN = Ring(PS[0:6])
        STAT = PS[6:8]
        USE["merge"] = car.off
        br_ranges = ((0, 4), (4, 12), (12, 16))

        def layernorm(tb, zi, eps, gvec, bvec, final_out):
            z = zt[zi]
            s1, s2 = STAT
            for c in range(KC):
                zb = ZB.next()
                A(lambda e, zb=zb, c=c: e.activation(out=zb.ap, in_=z[:, c, :], func=ACT.Copy), [R_z[zi][c]], [zb.res])
                fw.op("tensor", lambda e, zb=zb, c=c: e.matmul(s1.ap[:, :], lhsT=onesD[:, :], rhs=zb.ap, start=(c == 0), stop=(c == KC - 1)),
                      [zb.res, R_const], [s1.res])
                zq = ZB.next()
                A(lambda e, zq=zq, c=c: e.activation(out=zq.ap, in_=z[:, c, :], func=ACT.Square), [R_z[zi][c]], [zq.res])
                fw.op("tensor", lambda e, zq=zq, c=c: e.matmul(s2.ap[:, :], lhsT=onesD[:, :], rhs=zq.ap, start=(c == 0), stop=(c == KC - 1)),
                      [zq.res, R_const], [s2.res])
            msq = TF.next()
            A(lambda e: e.activation(out=msq.ap, in_=s1.ap[:, :], func=ACT.Square), [s1.res], [msq.res])
            vv = TF.next()
            V(lambda e: e.scalar_tensor_tensor(out=vv.ap, in0=s2.ap[:, :], scalar=float(eps), in1=msq.ap, op0=ALU.add, op1=ALU.subtract),
              [s2.res, msq.res], [vv.res])
            rsqrt_act(vv, vv.ap, [vv.res])
            nmr = TF.next()
            V(lambda e: e.scalar_tensor_tensor(out=nmr.ap, in0=s1.ap[:, :], scalar=-1.0, in1=vv.ap, op0=ALU.mult, op1=ALU.mult),
              [s1.res, vv.res], [nmr.res])
            for c in range(KC):
                rz = R_z[zi][c]
                V(lambda e, c=c: e.tensor_tensor(out=z[:, c, :], in0=z[:, c, :], in1=vv.ap, op=ALU.mult), [rz, vv.res], [rz])
                V(lambda e, c=c: e.tensor_tensor(out=z[:, c, :], in0=z[:, c, :], in1=nmr.ap, op=ALU.add), [rz, nmr.res], [rz])
                A(lambda e, c=c: e.activation(out=z[:, c, :], in_=z[:, c, :], func=ACT.Identity, bias=bvec[:, l, c:c + 1],
                                              scale=gvec[:, l, c:c + 1]), [rz, R_par], [rz])
                rx = R_X[c][tb]
                A(lambda e, c=c: e.activation(out=xhi[:, c, tok(tb)], in_=z[:, c, :], func=ACT.Copy), [rz], [rx])
                V(lambda e, c=c: e.tensor_tensor(out=xlo[:, c, tok(tb)], in0=z[:, c, :], in1=xhi[:, c, tok(tb)], op=ALU.subtract),
                  [rz, rx], [rx])
            if final_out:
                for t4 in range(4):
                    ob = OUTB.next()
                    for g in range(2):
                        p = GEN.next()
                        for cc in range(4):
                            c = g * 4 + cc
                            fw.op("tensor", lambda e, p=p, c=c, cc=cc, t4=t4: e.transpose(out=p.ap[:, cc * 128:(cc + 1) * 128],
                                                                                         in_=z[:, c, t4 * 128:(t4 + 1) * 128], identity=ident_f[:]),
                                  [R_z[zi][c], R_const], [p.res])
                        A(lambda e, p=p, ob=ob, g=g: e.activation(out=ob.ap[:, g * 512:(g + 1) * 512], in_=p.ap[:, :], func=ACT.Copy), [p.res], [ob.res])
                    t0 = tb * 512 + t4 * 128
                    ry = fw.res("y")
                    R_y.append(ry)
                    fw.dma("sync", lambda e, ob=ob, t0=t0: e.dma_start(out=y_d[s, t0:t0 + 128, :], in_=ob.ap), ob.key,
                           reads=[ob.res], writes=[ry])

        for th in range(NH):
            for c in range(KC):
                wsg = WG.next()
                wg = wsg.ap[:, 0:KC * 384].rearrange("p (k n) -> p k n", k=KC)
                for i in range(3):
                    c0 = C_GATE + i * D + c * 128
                    load_w(wsg, wg[:, :, i * 128:(i + 1) * 128], wsrc(w_in_d[l], 0, D, c0, c0 + 128))
                wsb = WB.next()
                wb = wsb.ap[:, 0:16 * 128].rearrange("p (k n) -> p k n", k=16)
                load_w(wsb, wb, wsrc(wbr_d[l], 0, 2048, c * 128, (c + 1) * 128))
                for sbk in range(SBH):
                    tb = th * SBH + sbk
                    xr = [R_X[k][tb] for k in range(KC)]
                    us = []
                    for i in range(3):
                        pg = GEN.next()
                        mm(pg.ap[:, :], [(wg[:, k, i * 128:(i + 1) * 128], xhi[:, k, tok(tb)]) for k in range(KC)], xr + [wsg.res], pg.res)
                        t_ = TF.next()
                        A(lambda e, pg=pg, t_=t_, i=i, c=c: e.activation(out=t_.ap, in_=pg.ap[:, :], func=ACT.Tanh,
                                                                        bias=bgh[:, l, i * 8 + c:i * 8 + c + 1], scale=0.5),
                          [pg.res, R_par], [t_.res])
                        pb = GEN.next()
                        k0, k1_ = br_ranges[i]
                        mm(pb.ap[:, :], [(wb[:, kk, :], oT[:, kk, tok(tb)]) for kk in range(k0, k1_)],
                           [R_O[kk][tb] for kk in range(k0, k1_)] + [wsb.res], pb.res)
                        V(lambda e, t_=t_, pb=pb: e.scalar_tensor_tensor(out=t_.ap, in0=t_.ap, scalar=1.0, in1=pb.ap[:, :],
                                                                        op0=ALU.add, op1=ALU.mult), [t_.res, pb.res], [t_.res])
                        us.append(t_)
                    V(lambda e, us=us: e.tensor_tensor(out=us[0].ap, in0=us[0].ap, in1=us[1].ap, op=ALU.add),
                      [us[0].res, us[1].res], [us[0].res])
                    V(lambda e, us=us, c=c, sbk=sbk: e.tensor_tensor(out=merged[:, c, sbk * 512:(sbk + 1) * 512], in0=us[0].ap, in1=us[2].ap, op=ALU.add),
                      [us[0].res, us[2].res], [R_m[c][sbk]])
            for sbk in range(SBH):
                tb = th * SBH + sbk
                for c2 in range(KC):
                    wso = WB.next()
                    wo = wso.ap[:, 0:KC * 128].rearrange("p (k n) -> p k n", k=KC)
                    load_w(wso, wo, wsrc(wout_d[l], 0, D, c2 * 128, (c2 + 1) * 128))
                    py = GEN.next()
                    mm(py.ap[:, :], [(wo[:, k, :], merged[:, k, sbk * 512:(sbk + 1) * 512]) for k in range(KC)],
                       [R_m[k][sbk] for k in range(KC)] + [wso.res], py.res)
                    rz = R_z[0][c2]
                    V(lambda e, py=py, c2=c2, tb=tb, z0=zt[0]: e.scalar_tensor_tensor(out=z0[:, c2, :], in0=xhi[:, c2, tok(tb)], scalar=2.0 * ALPHA,
                                                                           in1=py.ap[:, :], op0=ALU.mult, op1=ALU.add),
                      [py.res, R_X[c2][tb]], [rz])
                    V(lambda e, c2=c2, tb=tb, z0=zt[0]: e.scalar_tensor_tensor(out=z0[:, c2, :], in0=xlo[:, c2, tok(tb)], scalar=2.0 * ALPHA,
                                                                    in1=z0[:, c2, :], op0=ALU.mult, op1=ALU.add),
                      [R_X[c2][tb], rz], [rz])
                layernorm(tb, 0, 4.0 * LN_EPS, l1g, l1b, False)

        if upto < 7:
            return
        fw.barrier()
        car = Carver()
        W1 = Ring([Buf(car.bf(1024), fw.res("wa%d" % i), "wa%d" % i) for i in range(2)])
        W2 = Ring([Buf(car.bf(2048), fw.res("wb%d" % i), "wb%d" % i) for i in range(4)])
        zt = [car.f32(KC * 512).rearrange("p (c n) -> p c n", c=KC) for _ in range(SBH)]
        R_z = [[fw.res("zf%d_%d" % (i, c)) for c in range(KC)] for i in range(SBH)]
        TF = Ring([Buf(car.f32(512), fw.res("tg%d" % i)) for i in range(4)])
        ZB = Ring([Buf(car.bf(512), fw.res("zc%d" % i)) for i in range(2)])
        OUTB = Ring([Buf(car.f32(1024), fw.res("ob%d" % i), "ob%d" % i) for i in range(1)]) if last else None
        GEN = Ring(PS[0:6])
        STAT = PS[6:8]
        USE["ffn"] = car.off
        h1 = oT[:, :, :].rearrange("p c s -> p (c s)")[:, 0:32 * HB].rearrange("p (j t) -> p j t", j=32)

        def R_h(j, sbk):
            flat = j * HB + sbk * 512
            return R_O[flat // S][(flat % S) // 512]

        if KSTOP <= 0:
            return
        for th in range(NH):
            for j in range(32):
                ws = W1.next()
                w1c = ws.ap[:, 0:KC * 128].rearrange("p (k n) -> p k n", k=KC)
                load_w(ws, w1c, wsrc(w1_d[l], 0, D, j * 128, (j + 1) * 128))
                for sbk in range(SBH):
                    tb = th * SBH + sbk
                    xr = [R_X[k][tb] for k in range(KC)]
                    ph = GEN.next()
                    mm(ph.ap[:, :], [(w1c[:, k, :], xhi[:, k, tok(tb)]) for k in range(KC)], xr + [ws.res], ph.res)
                    r_ = TF.next()
                    V(lambda e, ph=ph, r_=r_: e.tensor_scalar(out=r_.ap, in0=ph.ap[:, :], scalar1=0.0, scalar2=None, op0=ALU.max), [ph.res], [r_.res])
                    A(lambda e, r_=r_, j=j, sbk=sbk: e.activation(out=h1[:, j, sbk * 512:(sbk + 1) * 512], in_=r_.ap, func=ACT.Square),
                      [r_.res], [R_h(j, sbk)])
                    if KDBG == 78 and j == 0 and sbk == 0 and th == 0:
                        dr = nc.dram_tensor("dbg_r", [128, 512], F32, kind="ExternalOutput").ap()
                        dw = nc.dram_tensor("dbg_w", [128, KC, 128], BF16, kind="ExternalOutput").ap()
                        dxx = nc.dram_tensor("dbg_x", [128, KC, 512], BF16, kind="ExternalOutput").ap()
                        dp = nc.dram_tensor("dbg_p", [128, 512], F32, kind="ExternalOutput").ap()
                        for nm_, dst_, src_, rd_ in (("r", dr, r_.ap, [r_.res]), ("w", dw, w1c, [ws.res]), ("x", dxx, xhi[:, :, tok(tb)], xr)):
                            rr_ = fw.res("dbg" + nm_)
                            R_y.append(rr_)
                            fw.dma("sync", lambda e, dst_=dst_, src_=src_: e.dma_start(out=dst_, in_=src_), "dbgk" + nm_, reads=rd_, writes=[rr_])
                        ph2 = GEN.next()
                        mm(ph2.ap[:, :], [(w1c[:, k, :], xhi[:, k, tok(tb)]) for k in range(KC)], xr + [ws.res], ph2.res)
                        r2_ = TF.next()
                        A(lambda e, ph2=ph2, r2_=r2_: e.activation(out=r2_.ap, in_=ph2.ap[:, :], func=ACT.Copy), [ph2.res], [r2_.res])
                        rr_ = fw.res("dbgp")
                        R_y.append(rr_)
                        fw.dma("sync", lambda e, dp=dp, r2_=r2_: e.dma_start(out=dp, in_=r2_.ap), "dbgkp", reads=[r2_.res], writes=[rr_])
            if KSTOP <= 1:
                return
            for c2 in range(KC):
                wsa = W2.next()
                w2a = wsa.ap[:, 0:16 * 128].rearrange("p (k n) -> p k n", k=16)
                load_w(wsa, w2a, wsrc(w2_d[l], 0, 2048, c2 * 128, (c2 + 1) * 128))
                wsb2 = W2.next()
                w2b = wsb2.ap[:, 0:16 * 128].rearrange("p (k n) -> p k n", k=16)
                load_w(wsb2, w2b, wsrc(w2_d[l], 2048, 4096, c2 * 128, (c2 + 1) * 128))
                for sbk in range(SBH):
                    tb = th * SBH + sbk
                    pf = GEN.next()
                    mm(pf.ap[:, :], [((w2a if j < 16 else w2b)[:, j % 16, :], h1[:, j, sbk * 512:(sbk + 1) * 512]) for j in range(32)],
                       [R_h(j, sbk) for j in range(32)] + [wsa.res, wsb2.res], pf.res)
                    rz = R_z[sbk][c2]
                    V(lambda e, pf=pf, c2=c2, tb=tb, zs=zt[sbk]: e.scalar_tensor_tensor(out=zs[:, c2, :], in0=xhi[:, c2, tok(tb)], scalar=ALPHA,
                                                                                    in1=pf.ap[:, :], op0=ALU.mult, op1=ALU.add),
                      [pf.res, R_X[c2][tb]], [rz])
                    V(lambda e, c2=c2, tb=tb, zs=zt[sbk]: e.scalar_tensor_tensor(out=zs[:, c2, :], in0=xlo[:, c2, tok(tb)], scalar=ALPHA,
                                                                             in1=zs[:, c2, :], op0=ALU.mult, op1=ALU.add),
                      [R_X[c2][tb], rz], [rz])
            if KSTOP <= 2:
                return
            for sbk in range(SBH):
                layernorm(th * SBH + sbk, sbk, LN_EPS, l2g, l2b, last)

    R_y = []
    USE = {}
    for s in range(NSEQ):
        fw.barrier()
        if upto >= 1:
            load_sequence(s)
        fw.barrier()
        if upto >= 2:
            build_tables(s)
        if upto >= 3:
            for l in range(NLAYER):
                layer(s, l, last=(l == NLAYER - 1))
    if KDBG == 77:
        dx = nc.dram_tensor("dbg_xhi", [128, KC, S], BF16, kind="ExternalOutput").ap()
        do = nc.dram_tensor("dbg_oT", [128, 16, S], BF16, kind="ExternalOutput").ap()
        fw.barrier()
        r1_, r2_ = fw.res("d1"), fw.res("d2")
        fw.dma("sync", lambda e: e.dma_start(out=dx, in_=xhi[:]), "dbg1", reads=[R_X[c][t] for c in range(KC) for t in range(TB)], writes=[r1_])
        fw.dma("sync", lambda e: e.dma_start(out=do, in_=oT[:]), "dbg2", reads=[R_O[c][t] for c in range(16) for t in range(TB)], writes=[r2_])
        R_y = R_y + [r1_, r2_]
    fw.op("sync", lambda e: e.nop(), reads=R_y, writes=[])
    counts = fw.finish()
    counts["arena"] = AR
    counts.update(USE)
    counts["nops"] = len(fw.ops)
    return nc, counts


_CACHE = {}
WEIGHT_KEYS = ["w_in", "b_gate", "mla_q_norm", "mla_kv_norm", "mla_w_uq", "mla_w_ukv", "diff_lambda",
               "diff_subln", "mem_w_kv", "w_branch", "w_out", "ln1_g", "ln1_b", "mlp_w1", "mlp_w2", "ln2_g", "ln2_b"]


def kernel(**inputs):
    x = np.ascontiguousarray(np.asarray(inputs["x"], dtype=np.float32))
    mem = np.ascontiguousarray(np.asarray(inputs["mem"], dtype=np.float32))
    pos = np.ascontiguousarray(np.asarray(inputs["positions"], dtype=np.int32))
    B, S, _ = x.shape
    nseq = B // N_CORES
    key = (nseq, S, DEPTH)
    if key not in _CACHE:
        _CACHE[key] = build_program(nseq, S, DEPTH)[0]
    nc = _CACHE[key]
    consts = _host_consts()
    shared = {k: np.ascontiguousarray(np.asarray(inputs[k], dtype=np.float32)) for k in WEIGHT_KEYS}
    shared["c_ident"] = consts["ident"]
    shared["c_pvec"] = consts["pvec"]
    shared["c_rot"] = consts["rot"]
    in_maps = []
    for c in range(N_CORES):
        m = dict(shared)
        m["x"] = x[c * nseq:(c + 1) * nseq]
        m["mem"] = mem[c * nseq:(c + 1) * nseq]
        m["positions"] = pos[c * nseq:(c + 1) * nseq]
        in_maps.append(m)
    res = run_bass_kernel_spmd(nc, in_maps, core_ids=list(range(N_CORES)))
    out = np.concatenate([np.asarray(r["y"]) for r in res.results], axis=0)
    return out.astype(np.float32)
```

```python
import math
import os
import numpy as np
KDBG = float(os.environ.get('KDBG', '99'))
KSTOP = int(os.environ.get('KSTOP', '99'))
import concourse.bass as bass
import concourse.mybir as mybir
from concourse.bass_utils import run_bass_kernel_spmd

F32 = mybir.dt.float32
BF16 = mybir.dt.bfloat16
I32 = mybir.dt.int32
ALU = mybir.AluOpType
ACT = mybir.ActivationFunctionType

D = 1024
KC = 8
MEM_LEN = 256
DEPTH = 4
N_CORES = 8
BATCH = 16
SEQ = 2048
ROPE_THETA = 500000.0
ALPHA = (2 * DEPTH) ** 0.25
LN_EPS = 1e-5
C_CQ, C_CKV, C_KPE, C_DQ, C_DK, C_DV, C_MQ, C_GATE = 0, 384, 640, 672, 1696, 2720, 3744, 4256


class Res:
    __slots__ = ("name", "last_writer", "readers", "excl")

    def __init__(self, name="", excl=False):
        self.name = name
        self.last_writer = None
        self.readers = []
        self.excl = excl


class Op:
    __slots__ = ("eng", "emit", "deps", "signal", "seq", "is_dma", "dma_sem", "dma_val", "big")

    def __init__(self, eng, emit, is_dma=False, big=False):
        self.eng = eng
        self.emit = emit
        self.deps = []
        self.signal = False
        self.seq = 0
        self.is_dma = is_dma
        self.dma_sem = None
        self.dma_val = 0
        self.big = big


class FW:
    ENGS = ("tensor", "vector", "scalar", "gpsimd", "sync")

    def __init__(self, nc, same_engine_sync=True):
        self.nc = nc
        self.ops = []
        self.same_engine_sync = same_engine_sync
        self.dma_sems = {}
        self.all_res = []

    def res(self, name="", excl=False):
        r = Res(name, excl)
        self.all_res.append(r)
        return r

    def op(self, eng, emit, reads=(), writes=(), big=False):
        o = Op(eng, emit, big=big)
        self._track(o, reads, writes)
        return o

    def dma(self, eng, emit, semkey, reads=(), writes=()):
        o = Op(eng, emit, is_dma=True)
        if semkey not in self.dma_sems:
            self.dma_sems[semkey] = [self.nc.alloc_semaphore("dq%d" % len(self.dma_sems)), 0]
        ent = self.dma_sems[semkey]
        ent[1] += 16
        o.dma_sem = ent[0]
        o.dma_val = ent[1]
        self._track(o, reads, writes)
        return o

    def _track(self, o, reads, writes):
        ex = [r for r in reads if r.excl]
        if ex:
            reads = [r for r in reads if not r.excl]
            writes = list(writes) + [r for r in ex if r not in writes]
        deps = []
        for r in reads:
            if r.last_writer is not None:
                deps.append(r.last_writer)
        for w in writes:
            if w.last_writer is not None:
                deps.append(w.last_writer)
            deps.extend(w.readers)
        for r in reads:
            r.readers.append(o)
        for w in writes:
            w.last_writer = o
            w.readers = []
        seen = set()
        for d in deps:
            if id(d) not in seen and d is not o:
                seen.add(id(d))
                o.deps.append(d)
        self.ops.append(o)

    def barrier(self):
        for e in self.ENGS:
            self.op(e, lambda eng: eng.nop(), reads=(), writes=self.all_res)

    def finish(self):
        nc = self.nc
        engs = {e: getattr(nc, e) for e in self.ENGS}
        sems = {e: nc.alloc_semaphore("eng_" + e) for e in self.ENGS}
        for o in self.ops:
            kept = []
            for d in o.deps:
                if d.is_dma:
                    kept.append(d)
                    continue
                if d.eng == o.eng and not o.is_dma:
                    if d.eng == "tensor":
                        continue
                    if not self.same_engine_sync:
                        continue
                    if d.big and o.big:
                        continue
                kept.append(d)
                d.signal = True
            o.deps = kept
        cnt = {e: 0 for e in self.ENGS}
        for o in self.ops:
            if o.signal and not o.is_dma:
                cnt[o.eng] += 1
                o.seq = cnt[o.eng]
        waited = {e: {} for e in self.ENGS}
        for o in self.ops:
            eng = engs[o.eng]
            need = {}
            for d in o.deps:
                if d.is_dma:
                    s, v = d.dma_sem, d.dma_val
                else:
                    s, v = sems[d.eng], d.seq
                if need.get(s.num, (None, 0))[1] < v:
                    need[s.num] = (s, v)
            for num, (s, v) in need.items():
                if waited[o.eng].get(num, 0) >= v:
                    continue
                eng.wait_ge(s, v)
                waited[o.eng][num] = v
            ins = o.emit(eng)
            if o.is_dma:
                ins.then_inc(o.dma_sem, 16)
            elif o.signal:
                ins.then_inc(sems[o.eng], 1)
        return cnt


class Buf:
    __slots__ = ("ap", "res", "key")

    def __init__(self, ap, res, key=None):
        self.ap = ap
        self.res = res
        self.key = key


class Ring:
    def __init__(self, items):
        self.items = items
        self.i = 0

    def next(self):
        it = self.items[self.i % len(self.items)]
        self.i += 1
        return it


def _host_consts():
    c = {}
    c["ident"] = np.eye(128, dtype=np.float32)
    inv_m = (ROPE_THETA ** (-np.arange(0, 32, 2, dtype=np.float32) / np.float32(32))).astype(np.float32)
    inv_d = (ROPE_THETA ** (-np.arange(0, 16, 2, dtype=np.float32) / np.float32(16))).astype(np.float32)
    pv = np.zeros((128, 8), dtype=np.float32)
    rot_m = np.zeros((128, 128), dtype=np.float32)
    rot_d = np.zeros((128, 128), dtype=np.float32)
    for i in range(32):
        p = 64 + i
        pv[p, 0] = inv_m[i % 16]
        pv[p, 1] = -1.0 if i < 16 else 1.0
        partner = 64 + (i + 16 if i < 16 else i - 16)
        rot_m[partner, p] = 1.0
    for o in (0, 64):
        for i in range(16):
            p = o + i
            pv[p, 2] = inv_d[i % 8]
            pv[p, 3] = -1.0 if i < 8 else 1.0
            partner = o + (i + 8 if i < 8 else i - 8)
            rot_d[partner, p] = 1.0
    c["pvec"] = pv
    c["rot"] = np.stack([rot_m, rot_d], axis=0)
    return c


def build_program(NSEQ, S, NLAYER, layer_ids=None, upto=99):
    assert S % 512 == 0
    TB = S // 512
    TT = S // 128
    HB = min(1024, S // 2)
    NH = S // HB
    SBH = HB // 512
    nc = bass.Bass("TRN2", target_bir_lowering=False)
    fw = FW(nc)
    L = NLAYER

    def din(name, shape, dt=F32):
        return nc.dram_tensor(name, list(shape), dt, kind="ExternalInput").ap()

    x_d = din("x", [NSEQ, S, D])
    mem_d = din("mem", [NSEQ, MEM_LEN, D])
    pos_d = din("positions", [NSEQ, S], I32)
    w_in_d = din("w_in", [L, D, 7328])
    b_gate_d = din("b_gate", [L, 3, D])
    qn_d = din("mla_q_norm", [L, 384])
    kvn_d = din("mla_kv_norm", [L, 256])
    wuq_d = din("mla_w_uq", [L, 384, 768])
    wukv_d = din("mla_w_ukv", [L, 256, 1024])
    dlam_d = din("diff_lambda", [L, 4, 64])
    subln_d = din("diff_subln", [L, 128])
    wkv_d = din("mem_w_kv", [L, D, 1024])
    wbr_d = din("w_branch", [L, 2048, D])
    wout_d = din("w_out", [L, D, D])
    ln1g_d = din("ln1_g", [L, D])
    ln1b_d = din("ln1_b", [L, D])
    w1_d = din("mlp_w1", [L, D, 4096])
    w2_d = din("mlp_w2", [L, 4096, D])
    ln2g_d = din("ln2_g", [L, D])
    ln2b_d = din("ln2_b", [L, D])
    ident_d = din("c_ident", [128, 128])
    pvec_d = din("c_pvec", [128, 8])
    rot_d_ = din("c_rot", [2, 128, 128])
    y_d = nc.dram_tensor("y", [NSEQ, S, D], F32, kind="ExternalOutput").ap()

    def sb(name, shape, dt):
        return nc.alloc_sbuf_tensor(name, list(shape), dt)

    xhi = sb("xhi", [128, KC, S], BF16)
    xlo = sb("xlo", [128, KC, S], BF16)
    oT = sb("oT", [128, 16, S], BF16)
    memT = sb("memT", [128, KC, MEM_LEN], BF16)
    ident_f = sb("ident_f", [128, 128], F32)
    ones_b = sb("ones_b", [128, 128], BF16)
    onesD = sb("onesD", [128, 128], BF16)
    rot_f = sb("rot_f", [128, 2, 128], F32)
    rot_b = sb("rot_b", [128, 2, 128], BF16)
    pvec = sb("pvec", [128, 8], F32)
    neghalf = sb("neghalf", [128, 512], F32)
    halfpi = sb("halfpi", [128, 1], F32)
    par = sb("par", [128, L, 64], F32)
    bgh = par[:, :, 0:24]
    gq = par[:, :, 24:27]
    gkv = par[:, :, 27:29]
    gsub = par[:, :, 29]
    l1g = par[:, :, 30:38]
    l1b = par[:, :, 38:46]
    l2g = par[:, :, 46:54]
    l2b = par[:, :, 54:62]
    dls = sb("dls", [128, L, 2], F32)
    nlam = sb("nlam", [128, L], F32)
    R_const = fw.res("const")
    R_par = fw.res("par")
    R_memT = fw.res("memT")
    R_X = [[fw.res("x%d_%d" % (c, t)) for t in range(TB)] for c in range(KC)]
    R_O = [[fw.res("o%d_%d" % (c, t)) for t in range(TB)] for c in range(16)]

    AR = (nc.sbuf_bytes_remaining - 2048) // 2 // 16 * 16
    arena = sb("arena", [128, AR], BF16)

    class Carver:
        def __init__(self):
            self.off = 0

        def bf(self, n):
            v = arena[:, self.off:self.off + n]
            self.off += n
            assert self.off <= AR, ("arena overflow", self.off, AR)
            return v

        def f32(self, n):
            return self.bf(2 * n).bitcast(F32)

    psb = [nc.alloc_psum_tensor("ps%d" % i, [128, 512], F32) for i in range(8)]
    PS = [Buf(psb[i], fw.res("ps%d" % i, excl=True)) for i in range(8)]

    def V(fn, reads, writes):
        fw.op("vector", fn, reads, writes)

    def A(fn, reads, writes):
        fw.op("scalar", fn, reads, writes)

    def G(fn, reads, writes):
        fw.op("gpsimd", fn, reads, writes)

    def mm(out_ap, pairs, reads, wres):
        n = len(pairs)
        for i, (l_, r_) in enumerate(pairs):
            fw.op("tensor",
                  lambda e, l_=l_, r_=r_, i=i: e.matmul(out_ap, lhsT=l_, rhs=r_, start=(i == 0), stop=(i == n - 1)),
                  reads, [wres])

    def load_w(slot, dst_ap, src_ap):
        fw.dma("gpsimd", lambda e: e.dma_start(out=dst_ap, in_=src_ap), slot.key, writes=[slot.res])

    def wsrc(w_ap, r0, r1, c0, c1):
        return w_ap[r0:r1, c0:c1].rearrange("(c p) n -> p c n", p=128)

    def tok(tb):
        return slice(tb * 512, (tb + 1) * 512)

    fw.dma("sync", lambda e: e.dma_start(out=ident_f[:], in_=ident_d), "c0", writes=[R_const])
    fw.dma("sync", lambda e: e.dma_start(out=pvec[:], in_=pvec_d), "c1", writes=[R_const])
    fw.dma("sync", lambda e: e.dma_start(out=rot_f[:], in_=rot_d_.rearrange("r k m -> k r m")), "c2", writes=[R_const])
    V(lambda e: e.tensor_copy(out=rot_b[:], in_=rot_f[:]), [R_const], [R_const])
    V(lambda e: e.memset(ones_b[:], 1.0), [], [R_const])
    V(lambda e: e.memset(onesD[:], 1.0 / 1024.0), [], [R_const])
    V(lambda e: e.memset(neghalf[:], -0.5), [], [R_const])
    V(lambda e: e.memset(halfpi[:], math.pi / 2), [], [R_const])

    car0 = Carver()
    dl = car0.f32(L * 256).rearrange("p (l n) -> p l n", l=L)
    dlp = car0.f32(L * 128).rearrange("p (l n) -> p l n", l=L)

    stage = car0.f32(L * 128).rearrange("p (l n) -> p l n", l=L)
    R_stage = fw.res("stage")
    for l in range(L):
        rows = [(0, b_gate_d[l].rearrange("i (c p) -> (i c) p", p=128), 24),
                (24, qn_d[l].rearrange("(c p) -> c p", p=128), 3),
                (27, kvn_d[l].rearrange("(c p) -> c p", p=128), 2),
                (29, subln_d[l].rearrange("(c p) -> c p", p=128), 1),
                (30, ln1g_d[l].rearrange("(c p) -> c p", p=128), 8),
                (38, ln1b_d[l].rearrange("(c p) -> c p", p=128), 8),
                (46, ln2g_d[l].rearrange("(c p) -> c p", p=128), 8),
                (54, ln2b_d[l].rearrange("(c p) -> c p", p=128), 8)]
        for (r0, src, n) in rows:
            fw.dma("sync", lambda e, r0=r0, src=src, n=n, l=l: e.dma_start(out=stage[r0:r0 + n, l, :], in_=src), "pst", writes=[R_stage])
        fw.dma("sync", lambda e, l=l: e.dma_start(out=dl[:, l, :], in_=dlam_d[l].rearrange("a b -> (a b)").partition_broadcast(128)),
               "p8", writes=[R_par])
    for l in range(L):
        pp = PS[l % 8]
        fw.op("tensor", lambda e, l=l, pp=pp: e.transpose(out=pp.ap[:, 0:62], in_=stage[0:62, l, :], identity=ident_f[0:62, 0:62]),
              [R_stage, R_const], [pp.res])
        V(lambda e, l=l, pp=pp: e.tensor_copy(out=par[:, l, 0:62], in_=pp.ap[:, 0:62]), [pp.res], [R_par])
    for l in range(L):
        lam_init = 0.8 - 0.6 * math.exp(-0.3 * ((layer_ids[l]) if layer_ids is not None else l))
        V(lambda e, l=l: e.tensor_scalar(out=bgh[:, l, :], in0=bgh[:, l, :], scalar1=0.5, scalar2=None, op0=ALU.mult), [R_par], [R_par])
        V(lambda e, l=l: e.tensor_tensor(out=dlp[:, l, 0:64], in0=dl[:, l, 0:64], in1=dl[:, l, 64:128], op=ALU.mult), [R_par], [R_par])
        V(lambda e, l=l: e.tensor_tensor(out=dlp[:, l, 64:128], in0=dl[:, l, 128:192], in1=dl[:, l, 192:256], op=ALU.mult), [R_par], [R_par])
        V(lambda e, l=l: e.reduce_sum(out=dls[:, l, 0:1], in_=dlp[:, l, 0:64], axis=mybir.AxisListType.X), [R_par], [R_par])
        V(lambda e, l=l: e.reduce_sum(out=dls[:, l, 1:2], in_=dlp[:, l, 64:128], axis=mybir.AxisListType.X), [R_par], [R_par])
        A(lambda e, l=l: e.activation(out=dls[:, l, :], in_=dls[:, l, :], func=ACT.Exp), [R_par], [R_par])
        V(lambda e, l=l: e.tensor_tensor(out=nlam[:, l:l + 1], in0=dls[:, l, 1:2], in1=dls[:, l, 0:1], op=ALU.subtract), [R_par], [R_par])
        V(lambda e, l=l, li=lam_init: e.tensor_scalar(out=nlam[:, l:l + 1], in0=nlam[:, l:l + 1], scalar1=-li, scalar2=None, op0=ALU.add), [R_par], [R_par])

    def attn_chunk(car_state, qc, qT, kT, vfn, kb, Kdim, dv, ro, nk, scale, q_reads, k_reads, v_reads, post, merged_den=False):
        SC, ACC, PT, pending = car_state
        pv = ACC.next()
        den = None if merged_den else ACC.next()

        def scores(kt):
            sc = SC.next()
            mm(sc.ap[:, :], [(kT[kb:kb + Kdim, kt * 128:(kt + 1) * 128], qT[kb:kb + Kdim, tok(qc)])],
               q_reads + k_reads, sc.res)
            return sc

        LA = len(SC.items) - 1
        scq = [scores(i) for i in range(min(LA, nk))]
        for kt in range(nk):
            sc = scq.pop(0)
            pt = PT.next()
            A(lambda e, sc=sc, pt=pt: e.activation(out=pt.ap, in_=sc.ap[:, :], func=ACT.Exp, scale=scale),
              [sc.res], [pt.res])
            if kt + LA < nk:
                scq.append(scores(kt + LA))
            fw.op("tensor", lambda e, pt=pt, kt=kt: e.matmul(pv.ap[ro:ro + dv, :], lhsT=vfn(kt), rhs=pt.ap,
                                                             start=(kt == 0), stop=(kt == nk - 1)),
                  [pt.res] + v_reads, [pv.res])
            if not merged_den:
                fw.op("tensor", lambda e, pt=pt, kt=kt: e.matmul(den.ap[ro:ro + dv, :], lhsT=ones_b[:, 0:dv], rhs=pt.ap,
                                                                 start=(kt == 0), stop=(kt == nk - 1)),
                      [pt.res, R_const], [den.res])
            if kt == min(3, nk - 1) and pending:
                for f_ in pending:
                    f_()
                del pending[:]
        post(qc, pv, den)

    def attention(car_state, qT, kT, vfn, kb, Kdim, dv, ro, nk, scale, q_reads, k_reads, v_reads, post):
        for qc in range(TB):
            attn_chunk(car_state, qc, qT, kT, vfn, kb, Kdim, dv, ro, nk, scale, q_reads, k_reads, v_reads, post)

    _csts = {}

    def cst(val):
        key = float(val)
        if key not in _csts:
            t = sb("cst%d" % len(_csts), [128, 1], F32)
            r = fw.res("cst")
            V(lambda e, t=t, key=key: e.memset(t[:], key), [], [r])
            _csts[key] = (t, r)
        return _csts[key]

    def rsqrt_act(dst, src_ap, reads, scale=1.0, bias=None):
        if bias is None:
            A(lambda e: e.activation(out=dst.ap, in_=src_ap, func=ACT.Ln, scale=float(scale)), reads, [dst.res])
        else:
            bt, br = cst(bias)
            A(lambda e: e.activation(out=dst.ap, in_=src_ap, func=ACT.Ln, scale=float(scale), bias=bt[:, 0:1]),
              reads + [br], [dst.res])
        A(lambda e: e.activation(out=dst.ap, in_=dst.ap, func=ACT.Exp, scale=-0.5), [dst.res], [dst.res])

    def load_sequence(s):
        car = Carver()
        xin = [Buf(car.f32(1024), fw.res("xin%d" % i), "xin%d" % i) for i in range(2)]
        ring = Ring(xin)
        pr = Ring(PS)
        for tt in range(TT):
            xb = ring.next()
            fw.dma("sync", lambda e, xb=xb, tt=tt: e.dma_start(out=xb.ap, in_=x_d[s, tt * 128:(tt + 1) * 128, :]),
                   xb.key, writes=[xb.res])
            tb = tt // 4
            for g in range(2):
                p = pr.next()
                for cc in range(4):
                    c = g * 4 + cc
                    fw.op("tensor", lambda e, p=p, xb=xb, c=c, cc=cc: e.transpose(out=p.ap[:, cc * 128:(cc + 1) * 128],
                                                                                 in_=xb.ap[:, c * 128:(c + 1) * 128], identity=ident_f[:]),
                          [xb.res, R_const], [p.res])
                wr = [R_X[g * 4 + cc][tb] for cc in range(4)]
                hi_v = xhi[:, g * 4:g * 4 + 4, tt * 128:(tt + 1) * 128]
                lo_v = xlo[:, g * 4:g * 4 + 4, tt * 128:(tt + 1) * 128]
                pv3 = p.ap[:, :].rearrange("p (a b) -> p a b", a=4)
                A(lambda e, hi_v=hi_v, pv3=pv3: e.activation(out=hi_v, in_=pv3, func=ACT.Copy), [p.res], wr)
                V(lambda e, lo_v=lo_v, hi_v=hi_v, pv3=pv3: e.tensor_tensor(out=lo_v, in0=pv3, in1=hi_v, op=ALU.subtract),
                  [p.res] + wr, wr)
        for mt in range(MEM_LEN // 128):
            xb = ring.next()
            fw.dma("sync", lambda e, xb=xb, mt=mt: e.dma_start(out=xb.ap, in_=mem_d[s, mt * 128:(mt + 1) * 128, :]),
                   xb.key, writes=[xb.res])
            for g in range(2):
                p = pr.next()
                for cc in range(4):
                    c = g * 4 + cc
                    fw.op("tensor", lambda e, p=p, xb=xb, c=c, cc=cc: e.transpose(out=p.ap[:, cc * 128:(cc + 1) * 128],
                                                                                 in_=xb.ap[:, c * 128:(c + 1) * 128], identity=ident_f[:]),
                          [xb.res, R_const], [p.res])
                V(lambda e, p=p, g=g, mt=mt: e.tensor_copy(out=memT[:, g * 4:g * 4 + 4, mt * 128:(mt + 1) * 128],
                                                          in_=p.ap[:, :].rearrange("p (a b) -> p a b", a=4)), [p.res], [R_memT])

    tab_d = nc.dram_tensor("tab_scratch", [4, 128, S], BF16, kind="Internal").ap()
    R_tabd = fw.res("tabd")

    def build_tables(s):
        car = Carver()
        posi = car.bf(2 * S).bitcast(I32)
        posf = car.f32(S)
        ang = car.f32(S)
        t1 = car.f32(S)
        t2 = car.f32(S)
        ki = car.bf(2 * S).bitcast(I32)
        tb16 = car.bf(S)
        R_t = fw.res("tabtmp")
        fw.dma("sync", lambda e: e.dma_start(out=posi, in_=pos_d[s, :].partition_broadcast(128)), "posi", writes=[R_t])
        V(lambda e: e.tensor_copy(out=posf, in_=posi), [R_t], [R_t])
        inv2pi = float(1.0 / (2 * math.pi))
        for ti, (fcol, scol) in enumerate([(0, 1), (2, 3)]):
            V(lambda e, fcol=fcol: e.tensor_scalar(out=ang, in0=posf, scalar1=pvec[:, fcol:fcol + 1], scalar2=None, op0=ALU.mult),
              [R_t, R_const], [R_t])
            for which in range(2):
                if which == 0:
                    V(lambda e: e.tensor_scalar(out=t1, in0=ang, scalar1=halfpi[:, 0:1], scalar2=None, op0=ALU.add), [R_t, R_const], [R_t])
                    src = t1
                else:
                    src = ang
                V(lambda e, src=src: e.tensor_scalar(out=t2, in0=src, scalar1=inv2pi, scalar2=None, op0=ALU.mult), [R_t], [R_t])
                V(lambda e: e.tensor_copy(out=ki, in_=t2), [R_t], [R_t])
                V(lambda e: e.tensor_copy(out=t2, in_=ki), [R_t], [R_t])
                V(lambda e, src=src: e.scalar_tensor_tensor(out=t2, in0=t2, scalar=float(-2 * math.pi), in1=src, op0=ALU.mult, op1=ALU.add),
                  [R_t], [R_t])
                A(lambda e: e.activation(out=t2, in_=t2, func=ACT.Sin), [R_t], [R_t])
                if which == 0:
                    V(lambda e: e.tensor_copy(out=tb16, in_=t2), [R_t], [R_t])
                else:
                    V(lambda e, scol=scol: e.tensor_scalar(out=tb16, in0=t2, scalar1=pvec[:, scol:scol + 1], scalar2=None, op0=ALU.mult),
                      [R_t, R_const], [R_t])
                idx = ti * 2 + which
                fw.dma("sync", lambda e, idx=idx: e.dma_start(out=tab_d[idx], in_=tb16), "tabst", reads=[R_t], writes=[R_tabd])

    def layer(s, l, last):
        lid = layer_ids[l] if layer_ids is not None else l
        lam_init = 0.8 - 0.6 * math.exp(-0.3 * lid)
        fw.barrier()
        car = Carver()
        tabs = car.bf(4 * S).rearrange("p (a b) -> p a b", a=4)
        R_tab = fw.res("tab")
        Cm, Sm, Cd, Sd = tabs[:, 0, :], tabs[:, 1, :], tabs[:, 2, :], tabs[:, 3, :]
        fw.dma("sync", lambda e: e.dma_start(out=tabs, in_=tab_d.rearrange("a p s -> p a s")), "tabld", reads=[R_tabd], writes=[R_tab])
        WS = Ring([Buf(car.bf(4096), fw.res("w%d" % i), "w%d" % i) for i in range(2)])
        qTt = car.bf(S)
        kTt = car.bf(S)
        vt = car.bf(TT * 128).rearrange("p (a b) -> p a b", a=TT)
        R_q, R_k, R_v = fw.res("q"), fw.res("k"), fw.res("v")
        PT = Ring([Buf(car.bf(512), fw.res("pt%d" % i)) for i in range(3)])
        TF = Ring([Buf(car.f32(512), fw.res("tf%d" % i)) for i in range(5)])
        TBf = Ring([Buf(car.bf(512), fw.res("tb%d" % i)) for i in range(2)])
        qz1 = car.bf(S)
        R_q1 = fw.res("q1")
        kpe = car.bf(max(S, 2048))
        R_kpe = fw.res("kpe")
        SC = Ring(PS[0:3])
        ACC = Ring(PS[3:7])
        GEN = Ring([PS[7]])
        GP = Ring(PS[3:8])
        pending = []
        car_state = (SC, ACC, PT, pending)

        def flush():
            for f_ in pending:
                f_()
            del pending[:]
        USE["attn"] = car.off
        cqn = lambda j, tb: oT[:, 4 + j, tok(tb)]
        ckvn = lambda j, tb: oT[:, 7 + j, tok(tb)]
        R_cq = lambda j, tb: R_O[4 + j][tb]
        R_ckv = lambda j, tb: R_O[7 + j][tb]

        def rope_rows(dst_ap, a_ps, rows, tabC, tabS, rotsel, tsl, reads, wres, split=None):
            ab = TBf.next()
            A(lambda e: e.activation(out=ab.ap[rows, :], in_=a_ps.ap[rows, :], func=ACT.Copy), [a_ps.res], [ab.res])
            bp = GP.next()
            mm(bp.ap[:, :], [(rot_b[:, rotsel, :], ab.ap)], [ab.res, R_const], bp.res)
            t1 = TF.next()
            V(lambda e: e.tensor_tensor(out=t1.ap[rows, :], in0=a_ps.ap[rows, :], in1=tabC[rows, tsl], op=ALU.mult),
              [a_ps.res, R_tab], [t1.res])
            t2 = TF.next()
            V(lambda e: e.tensor_tensor(out=t2.ap[rows, :], in0=bp.ap[rows, :], in1=tabS[rows, tsl], op=ALU.mult),
              [bp.res, R_tab], [t2.res])
            if split is None:
                V(lambda e: e.tensor_tensor(out=dst_ap, in0=t1.ap[rows, :], in1=t2.ap[rows, :], op=ALU.add),
                  [t1.res, t2.res] + reads, [wres])
            else:
                for (d_ap, d_res, pr) in split:
                    V(lambda e, d_ap=d_ap, pr=pr: e.tensor_tensor(out=d_ap, in0=t1.ap[pr, :], in1=t2.ap[pr, :], op=ALU.add),
                      [t1.res, t2.res] + reads, [d_res])

        for b_ in TBf.items:
            V(lambda e, b_=b_: e.memset(b_.ap, 0.0), [], [b_.res])

        ws = WS.next()
        wcq = ws.ap[:, 0:KC * 384].rearrange("p (c n) -> p c n", c=KC)
        load_w(ws, wcq, wsrc(w_in_d[l], 0, D, C_CQ, C_CQ + 384))
        ws2 = WS.next()
        wck = ws2.ap[:, 0:KC * 288].rearrange("p (c n) -> p c n", c=KC)
        load_w(ws2, wck, wsrc(w_in_d[l], 0, D, C_CKV, C_CKV + 288))
        for tb in range(TB):
            xr = [R_X[k][tb] for k in range(KC)]
            for (wt, wsl, nj, cols, gvec, dstf, rdst, eps, nfe) in (
                    (wcq, ws, 3, 0, gq, cqn, R_cq, 1e-6, 384.0),
                    (wck, ws2, 2, 0, gkv, ckvn, R_ckv, 1e-6, 256.0)):
                pj = []
                sqs = []
                for j in range(nj):
                    p = GP.next()
                    pj.append(p)
                    mm(p.ap[:, :], [(wt[:, k, cols + j * 128:cols + (j + 1) * 128], xhi[:, k, tok(tb)]) for k in range(KC)],
                       xr + [wsl.res], p.res)
                    sq = PT.next()
                    A(lambda e, p=p, sq=sq: e.activation(out=sq.ap, in_=p.ap[:, :], func=ACT.Square), [p.res], [sq.res])
                    sqs.append(sq)
                pss = GP.next()
                mm(pss.ap[:, :], [(ones_b[:, :], sq.ap) for sq in sqs], [sq.res for sq in sqs] + [R_const], pss.res)
                rs = TF.next()
                rsqrt_act(rs, pss.ap[:, :], [pss.res], scale=1.0 / nfe, bias=eps)
                for j in range(nj):
                    V(lambda e, j=j, p=pj[j], rs=rs, gvec=gvec, dap=dstf(j, tb): e.scalar_tensor_tensor(
                        out=dap, in0=p.ap[:, :], scalar=gvec[:, l, j:j + 1], in1=rs.ap, op0=ALU.mult, op1=ALU.mult),
                      [pj[j].res, rs.res, R_par], [rdst(j, tb)])
            p = GP.next()
            mm(p.ap[64:96, :], [(wck[:, k, 256:288], xhi[:, k, tok(tb)]) for k in range(KC)], xr + [ws2.res], p.res)
            rope_rows(kpe[64:96, tok(tb)], p, slice(64, 96), Cm, Sm, 0, tok(tb), [], R_kpe)
        for h in range(8):
            ws = WS.next()
            wq = ws.ap[:, 0:3 * 96].rearrange("p (c n) -> p c n", c=3)
            wkv_ = ws.ap[:, 512:512 + 2 * 128].rearrange("p (c n) -> p c n", c=2)
            load_w(ws, wq, wsrc(wuq_d[l], 0, 384, h * 96, (h + 1) * 96))
            load_w(ws, wkv_, wsrc(wukv_d[l], 0, 256, h * 128, (h + 1) * 128))
            for tb in range(TB):
                cqr = [R_cq(j, tb) for j in range(3)]
                ckr = [R_ckv(j, tb) for j in range(2)]
                p = GP.next()
                mm(p.ap[0:96, :], [(wq[:, j, :], cqn(j, tb)) for j in range(3)], cqr + [ws.res], p.res)
                V(lambda e, p=p, tb=tb: e.tensor_copy(out=qTt[0:64, tok(tb)], in_=p.ap[0:64, :]), [p.res], [R_q])
                rope_rows(qTt[64:96, tok(tb)], p, slice(64, 96), Cm, Sm, 0, tok(tb), [], R_q)
                p2 = GP.next()
                mm(p2.ap[0:64, :], [(wkv_[:, j, 0:64], ckvn(j, tb)) for j in range(2)], ckr + [ws.res], p2.res)
                V(lambda e, p2=p2, tb=tb: e.tensor_copy(out=kTt[0:64, tok(tb)], in_=p2.ap[0:64, :]), [p2.res], [R_k])
                V(lambda e, tb=tb: e.tensor_copy(out=kTt[64:96, tok(tb)], in_=kpe[64:96, tok(tb)]), [R_kpe], [R_k])
                p3 = GP.next()
                for t4 in range(4):
                    tsl = slice(tb * 512 + t4 * 128, tb * 512 + (t4 + 1) * 128)
                    mm(p3.ap[:, t4 * 64:(t4 + 1) * 64], [(oT[:, 7 + j, tsl], wkv_[:, j, 64:128]) for j in range(2)], ckr + [ws.res], p3.res)
                V(lambda e, p3=p3, tb=tb, ro=(h % 2) * 64: e.tensor_copy(out=vt[:, tb * 4:(tb + 1) * 4, ro:ro + 64],
                                                        in_=p3.ap[:, 0:256].rearrange("p (a b) -> p a b", a=4)), [p3.res], [R_v])
            ro = (h % 2) * 64
            V(lambda e, ro=ro: e.memset(vt[:, :, 64 - ro:128 - ro], 1.0), [], [R_v])

            def post_mla(qc, acc, _den, h=h, ro=ro):
                prow = slice(ro, ro + 64)
                drow = slice(64 - ro, 128 - ro)
                T = TF.next()
                V(lambda e: e.reciprocal(out=T.ap[drow, :], in_=acc.ap[drow, :]), [acc.res], [T.res])

                def part_b():
                    g = GEN.next()
                    fw.op("tensor", lambda e: e.matmul(g.ap[prow, :], lhsT=ident_f[drow, drow], rhs=T.ap[drow, :], start=True, stop=True),
                          [T.res, R_const], [g.res])
                    rs = TF.next()
                    A(lambda e: e.activation(out=rs.ap[prow, :], in_=g.ap[prow, :], func=ACT.Copy), [g.res], [rs.res])
                    V(lambda e: e.tensor_tensor(out=oT[prow, h // 2, tok(qc)], in0=acc.ap[prow, :], in1=rs.ap[prow, :], op=ALU.mult),
                      [acc.res, rs.res], [R_O[h // 2][qc]])
                pending.append(part_b)

            for qc in range(TB):
                attn_chunk(car_state, qc, qTt, kTt, lambda kt: vt[:, kt, :], 0, 96, 128, 0, TT, 96.0 ** -0.5,
                           [R_q], [R_k], [R_v], post_mla, merged_den=True)
            flush()

        if upto < 4:
            return
        k1 = 1.0 / (128.0 * (1.0 - lam_init) ** 2)
        k2 = 1e-5 / ((1.0 - lam_init) ** 2)
        V(lambda e: e.memset(qTt[64:128, :], 0.0), [], [R_q])
        V(lambda e: e.memset(qz1[0:64, :], 0.0), [], [R_q1])
        for h in range(8):
            ws = WS.next()
            w3 = ws.ap[:, 0:KC * 384].rearrange("p (c n) -> p c n", c=KC)
            for i3, c0 in enumerate((C_DQ, C_DK, C_DV)):
                load_w(ws, w3[:, :, i3 * 128:(i3 + 1) * 128], wsrc(w_in_d[l], 0, D, c0 + h * 128, c0 + (h + 1) * 128))
            for tb in range(TB):
                xr = [R_X[k][tb] for k in range(KC)]
                for i3 in range(2):
                    p = GP.next()
                    mm(p.ap[:, :], [(w3[:, k, i3 * 128:(i3 + 1) * 128], xhi[:, k, tok(tb)]) for k in range(KC)], xr + [ws.res], p.res)
                    if i3 == 0:
                        rope_rows(None, p, slice(0, 128), Cd, Sd, 1, tok(tb), [], None,
                                  split=[(qTt[0:64, tok(tb)], R_q, slice(0, 64)), (qz1[64:128, tok(tb)], R_q1, slice(64, 128))])
                    else:
                        rope_rows(kTt[:, tok(tb)], p, slice(0, 128), Cd, Sd, 1, tok(tb), [], R_k)
                p3 = GP.next()
                for t4 in range(4):
                    tsl = slice(tb * 512 + t4 * 128, tb * 512 + (t4 + 1) * 128)
                    mm(p3.ap[:, t4 * 128:(t4 + 1) * 128], [(xhi[:, k, tsl], w3[:, k, 256:384]) for k in range(KC)], xr + [ws.res], p3.res)
                for t4 in range(4):
                    V(lambda e, p3=p3, tb=tb, t4=t4: e.tensor_copy(out=vt[:, tb * 4 + t4, :], in_=p3.ap[:, t4 * 128:(t4 + 1) * 128]), [p3.res], [R_v])
            keep = {}

            def post_d0(qc, pv, den):
                r0 = TF.next()
                V(lambda e: e.reciprocal(out=r0.ap, in_=den.ap[:, :]), [den.res], [r0.res])
                a_ = TF.next()
                V(lambda e: e.tensor_tensor(out=a_.ap, in0=pv.ap[:, :], in1=r0.ap, op=ALU.mult), [pv.res, r0.res], [a_.res])
                keep[qc] = a_

            def post_d1(qc, pv, den, h=h):
                a_ = keep[qc]
                r1 = TF.next()
                V(lambda e: e.reciprocal(out=r1.ap, in_=den.ap[:, :]), [den.res], [r1.res])
                b_ = TF.next()
                V(lambda e: e.scalar_tensor_tensor(out=b_.ap, in0=pv.ap[:, :], scalar=nlam[:, l:l + 1], in1=r1.ap,
                                                   op0=ALU.mult, op1=ALU.mult), [pv.res, r1.res, R_par], [b_.res])
                V(lambda e: e.tensor_tensor(out=a_.ap, in0=a_.ap, in1=b_.ap, op=ALU.add), [a_.res, b_.res], [a_.res])
                sq = TBf.next()
                A(lambda e: e.activation(out=sq.ap, in_=a_.ap, func=ACT.Square), [a_.res], [sq.res])

                def part_b():
                    pss = GEN.next()
                    mm(pss.ap[:, :], [(ones_b[:, :], sq.ap)], [sq.res, R_const], pss.res)
                    vv = TF.next()
                    rsqrt_act(vv, pss.ap[:, :], [pss.res], scale=k1, bias=k2)
                    V(lambda e: e.scalar_tensor_tensor(out=oT[:, 4 + h, tok(qc)], in0=a_.ap, scalar=par[:, l, 29:30],
                                                       in1=vv.ap, op0=ALU.mult, op1=ALU.mult),
                      [a_.res, vv.res, R_par], [R_O[4 + h][qc]])
                pending.append(part_b)

            for qc in range(TB):
                for qz, rq, pst in ((qTt, R_q, post_d0), (qz1, R_q1, post_d1)):
                    attn_chunk(car_state, qc, qz, kTt, lambda kt: vt[:, kt, :], 0, 128, 128, 0, TT, 64.0 ** -0.5,
                               [rq], [R_k], [R_v], pst)
            flush()

        if upto < 5:
            return
        Km = kpe[:, 0:4 * MEM_LEN].rearrange("p (a b) -> p a b", a=4)
        Vm = kpe[:, 4 * MEM_LEN:4 * MEM_LEN + 2 * 512].rearrange("p (a b) -> p a b", a=2)
        for half in range(2):
            ws = WS.next()
            wk_ = ws.ap[:, 0:KC * 512].rearrange("p (c n) -> p c n", c=KC)
            load_w(ws, wk_, wsrc(wkv_d[l], 0, D, half * 512, (half + 1) * 512))
            if half == 0:
                for hh in range(4):
                    p = GP.next()
                    mm(p.ap[:, 0:MEM_LEN], [(wk_[:, k, hh * 128:(hh + 1) * 128], memT[:, k, :]) for k in range(KC)], [R_memT, ws.res], p.res)
                    V(lambda e, p=p, hh=hh: e.tensor_copy(out=Km[:, hh, :], in_=p.ap[:, 0:MEM_LEN]), [p.res], [R_kpe])
            else:
                for mt in range(2):
                    p = GP.next()
                    mm(p.ap[:, :], [(memT[:, k, mt * 128:(mt + 1) * 128], wk_[:, k, :]) for k in range(KC)], [R_memT, ws.res], p.res)
                    V(lambda e, p=p, mt=mt: e.tensor_copy(out=Vm[:, mt, :], in_=p.ap[:, :]), [p.res], [R_kpe])
        ws = WS.next()
        wmq = ws.ap[:, 0:KC * 512].rearrange("p (c n) -> p c n", c=KC)
        load_w(ws, wmq, wsrc(w_in_d[l], 0, D, C_MQ, C_MQ + 512))
        for hh in range(4):
            for tb in range(TB):
                xr = [R_X[k][tb] for k in range(KC)]
                p = GP.next()
                mm(p.ap[:, :], [(wmq[:, k, hh * 128:(hh + 1) * 128], xhi[:, k, tok(tb)]) for k in range(KC)], xr + [ws.res], p.res)
                V(lambda e, p=p, tb=tb: e.tensor_copy(out=qTt[:, tok(tb)], in_=p.ap[:, :]), [p.res], [R_q])

            def post_mem(qc, pv, den, hh=hh):
                r = TF.next()
                V(lambda e: e.reciprocal(out=r.ap, in_=den.ap[:, :]), [den.res], [r.res])
                V(lambda e: e.tensor_tensor(out=oT[:, 12 + hh, tok(qc)], in0=pv.ap[:, :], in1=r.ap, op=ALU.mult),
                  [pv.res, r.res], [R_O[12 + hh][qc]])

            attention(car_state, qTt, Km[:, hh, :], lambda kt, hh=hh: Vm[:, kt, hh * 128:(hh + 1) * 128], 0, 128, 128, 0, 2,
                      128.0 ** -0.5, [R_q], [R_kpe], [R_kpe], post_mem)

        if upto < 6:
            return
        fw.barrier()
        car = Carver()
        WG = Ring([Buf(car.bf(3072), fw.res("wm%d" % i), "wm%d" % i) for i in range(2)])
        WB = Ring([Buf(car.bf(2048), fw.res("wn%d" % i), "wn%d" % i) for i in range(2)])
        merged = car.bf(KC * HB).rearrange("p (c n) -> p c n", c=KC)
        R_m = [[fw.res("m%d_%d" % (c, sbk)) for sbk in range(SBH)] for c in range(KC)]
        zt = [car.f32(KC * 512).rearrange("p (c n) -> p c n", c=KC) for _ in range(1)]
        R_z = [[fw.res("z%d_%d" % (i, c)) for c in range(KC)] for i in range(1)]
        TF = Ring([Buf(car.f32(512), fw.res("tf%d" % i)) for i in range(6)])
        ZB = Ring([Buf(car.bf(512), fw.res("zb%d" % i)) for i in range(2)])
        GEN = Ring(PS[0:6])
        STAT = PS[6:8]
        USE["merge"] = car.off
        br_ranges = ((0, 4), (4, 12), (12, 16))

        def layernorm(tb, zi, eps, gvec, bvec, final_out):
            z = zt[zi]
            s1, s2 = STAT
            for c in range(KC):
                zb = ZB.next()
                A(lambda e, zb=zb, c=c: e.activation(out=zb.ap, in_=z[:, c, :], func=ACT.Copy), [R_z[zi][c]], [zb.res])
                fw.op("tensor", lambda e, zb=zb, c=c: e.matmul(s1.ap[:, :], lhsT=onesD[:, :], rhs=zb.ap, start=(c == 0), stop=(c == KC - 1)),
                      [zb.res, R_const], [s1.res])
                zq = ZB.next()
                A(lambda e, zq=zq, c=c: e.activation(out=zq.ap, in_=z[:, c, :], func=ACT.Square), [R_z[zi][c]], [zq.res])
                fw.op("tensor", lambda e, zq=zq, c=c: e.matmul(s2.ap[:, :], lhsT=onesD[:, :], rhs=zq.ap, start=(c == 0), stop=(c == KC - 1)),
                      [zq.res, R_const], [s2.res])
            msq = TF.next()
            A(lambda e: e.activation(out=msq.ap, in_=s1.ap[:, :], func=ACT.Square), [s1.res], [msq.res])
            vv = TF.next()
            V(lambda e: e.scalar_tensor_tensor(out=vv.ap, in0=s2.ap[:, :], scalar=float(eps), in1=msq.ap, op0=ALU.add, op1=ALU.subtract),
              [s2.res, msq.res], [vv.res])
            rsqrt_act(vv, vv.ap, [vv.res])
            nmr = TF.next()
            V(lambda e: e.scalar_tensor_tensor(out=nmr.ap, in0=s1.ap[:, :], scalar=-1.0, in1=vv.ap, op0=ALU.mult, op1=ALU.mult),
              [s1.res, vv.res], [nmr.res])
            for c in range(KC):
                rz = R_z[zi][c]
                V(lambda e, c=c: e.tensor_tensor(out=z[:, c, :], in0=z[:, c, :], in1=vv.ap, op=ALU.mult), [rz, vv.res], [rz])
                V(lambda e, c=c: e.tensor_tensor(out=z[:, c, :], in0=z[:, c, :], in1=nmr.ap, op=ALU.add), [rz, nmr.res], [rz])
                A(lambda e, c=c: e.activation(out=z[:, c, :], in_=z[:, c, :], func=ACT.Identity, bias=bvec[:, l, c:c + 1],
                                              scale=gvec[:, l, c:c + 1]), [rz, R_par], [rz])
                rx = R_X[c][tb]
                A(lambda e, c=c: e.activation(out=xhi[:, c, tok(tb)], in_=z[:, c, :], func=ACT.Copy), [rz], [rx])
                V(lambda e, c=c: e.tensor_tensor(out=xlo[:, c, tok(tb)], in0=z[:, c, :], in1=xhi[:, c, tok(tb)], op=ALU.subtract),
                  [rz, rx], [rx])
            if final_out:
                for t4 in range(4):
                    ob = OUTB.next()
                    for g in range(2):
                        p = GEN.next()
                        for cc in range(4):
                            c = g * 4 + cc
                            fw.op("tensor", lambda e, p=p, c=c, cc=cc, t4=t4: e.transpose(out=p.ap[:, cc * 128:(cc + 1) * 128],
                                                                                         in_=z[:, c, t4 * 128:(t4 + 1) * 128], identity=ident_f[:]),
                                  [R_z[zi][c], R_const], [p.res])
                        A(lambda e, p=p, ob=ob, g=g: e.activation(out=ob.ap[:, g * 512:(g + 1) * 512], in_=p.ap[:, :], func=ACT.Copy), [p.res], [ob.res])
                    t0 = tb * 512 + t4 * 128
                    ry = fw.res("y")
                    R_y.append(ry)
                    fw.dma("sync", lambda e, ob=ob, t0=t0: e.dma_start(out=y_d[s, t0:t0 + 128, :], in_=ob.ap), ob.key,
                           reads=[ob.res], writes=[ry])

        for th in range(NH):
            for c in range(KC):
                wsg = WG.next()
                wg = wsg.ap[:, 0:KC * 384].rearrange("p (k n) -> p k n", k=KC)
                for i in range(3):
                    c0 = C_GATE + i * D + c * 128
                    load_w(wsg, wg[:, :, i * 128:(i + 1) * 128], wsrc(w_in_d[l], 0, D, c0, c0 + 128))
                wsb = WB.next()
                wb = wsb.ap[:, 0:16 * 128].rearrange("p (k n) -> p k n", k=16)
                load_w(wsb, wb, wsrc(wbr_d[l], 0, 2048, c * 128, (c + 1) * 128))
                for sbk in range(SBH):
                    tb = th * SBH + sbk
                    xr = [R_X[k][tb] for k in range(KC)]
                    us = []
                    for i in range(3):
                        pg = GEN.next()
                        mm(pg.ap[:, :], [(wg[:, k, i * 128:(i + 1) * 128], xhi[:, k, tok(tb)]) for k in range(KC)], xr + [wsg.res], pg.res)
                        t_ = TF.next()
                        A(lambda e, pg=pg, t_=t_, i=i, c=c: e.activation(out=t_.ap, in_=pg.ap[:, :], func=ACT.Tanh,
                                                                        bias=bgh[:, l, i * 8 + c:i * 8 + c + 1], scale=0.5),
                          [pg.res, R_par], [t_.res])
                        pb = GEN.next()
                        k0, k1_ = br_ranges[i]
                        mm(pb.ap[:, :], [(wb[:, kk, :], oT[:, kk, tok(tb)]) for kk in range(k0, k1_)],
                           [R_O[kk][tb] for kk in range(k0, k1_)] + [wsb.res], pb.res)
                        V(lambda e, t_=t_, pb=pb: e.scalar_tensor_tensor(out=t_.ap, in0=t_.ap, scalar=1.0, in1=pb.ap[:, :],
                                                                        op0=ALU.add, op1=ALU.mult), [t_.res, pb.res], [t_.res])
                        us.append(t_)
                    V(lambda e, us=us: e.tensor_tensor(out=us[0].ap, in0=us[0].ap, in1=us[1].ap, op=ALU.add),
                      [us[0].res, us[1].res], [us[0].res])
                    V(lambda e, us=us, c=c, sbk=sbk: e.tensor_tensor(out=merged[:, c, sbk * 512:(sbk + 1) * 512], in0=us[0].ap, in1=us[2].ap, op=ALU.add),
                      [us[0].res, us[2].res], [R_m[c][sbk]])
            for sbk in range(SBH):
                tb = th * SBH + sbk
                for c2 in range(KC):
                    wso = WB.next()
                    wo = wso.ap[:, 0:KC * 128].rearrange("p (k n) -> p k n", k=KC)
                    load_w(wso, wo, wsrc(wout_d[l], 0, D, c2 * 128, (c2 + 1) * 128))
                    py = GEN.next()
                    mm(py.ap[:, :], [(wo[:, k, :], merged[:, k, sbk * 512:(sbk + 1) * 512]) for k in range(KC)],
                       [R_m[k][sbk] for k in range(KC)] + [wso.res], py.res)
                    rz = R_z[0][c2]
                    V(lambda e, py=py, c2=c2, tb=tb, z0=zt[0]: e.scalar_tensor_tensor(out=z0[:, c2, :], in0=xhi[:, c2, tok(tb)], scalar=2.0 * ALPHA,
                                                                           in1=py.ap[:, :], op0=ALU.mult, op1=ALU.add),
                      [py.res, R_X[c2][tb]], [rz])
                    V(lambda e, c2=c2, tb=tb, z0=zt[0]: e.scalar_tensor_tensor(out=z0[:, c2, :], in0=xlo[:, c2, tok(tb)], scalar=2.0 * ALPHA,
                                                                    in1=z0[:, c2, :], op0=ALU.mult, op1=ALU.add),
                      [R_X[c2][tb], rz], [rz])
                layernorm(tb, 0, 4.0 * LN_EPS, l1g, l1b, False)

        if upto < 7:
            return
        fw.barrier()
        car = Carver()
        W1 = Ring([Buf(car.bf(1024), fw.res("wa%d" % i), "wa%d" % i) for i in range(2)])
        W2 = Ring([Buf(car.bf(2048), fw.res("wb%d" % i), "wb%d" % i) for i in range(4)])
        zt = [car.f32(KC * 512).rearrange("p (c n) -> p c n", c=KC) for _ in range(SBH)]
        R_z = [[fw.res("zf%d_%d" % (i, c)) for c in range(KC)] for i in range(SBH)]
        TF = Ring([Buf(car.f32(512), fw.res("tg%d" % i)) for i in range(4)])
        ZB = Ring([Buf(car.bf(512), fw.res("zc%d" % i)) for i in range(2)])
        OUTB = Ring([Buf(car.f32(1024), fw.res("ob%d" % i), "ob%d" % i) for i in range(1)]) if last else None
        GEN = Ring(PS[0:6])
        STAT = PS[6:8]
        USE["ffn"] = car.off
        h1 = oT[:, :, :].rearrange("p c s -> p (c s)")[:, 0:32 * HB].rearrange("p (j t) -> p j t", j=32)

        def R_h(j, sbk):
            flat = j * HB + sbk * 512
            return R_O[flat // S][(flat % S) // 512]

        if KSTOP <= 0:
            return
        for th in range(NH):
            for j in range(32):
                ws = W1.next()
                w1c = ws.ap[:, 0:KC * 128].rearrange("p (k n) -> p k n", k=KC)
                load_w(ws, w1c, wsrc(w1_d[l], 0, D, j * 128, (j + 1) * 128))
                for sbk in range(SBH):
                    tb = th * SBH + sbk
                    xr = [R_X[k][tb] for k in range(KC)]
                    ph = GEN.next()
                    mm(ph.ap[:, :], [(w1c[:, k, :], xhi[:, k, tok(tb)]) for k in range(KC)], xr + [ws.res], ph.res)
                    r_ = TF.next()
                    V(lambda e, ph=ph, r_=r_: e.tensor_scalar(out=r_.ap, in0=ph.ap[:, :], scalar1=0.0, scalar2=None, op0=ALU.max), [ph.res], [r_.res])
                    A(lambda e, r_=r_, j=j, sbk=sbk: e.activation(out=h1[:, j, sbk * 512:(sbk + 1) * 512], in_=r_.ap, func=ACT.Square),
                      [r_.res], [R_h(j, sbk)])
                    if KDBG == 78 and j == 0 and sbk == 0 and th == 0:
                        dr = nc.dram_tensor("dbg_r", [128, 512], F32, kind="ExternalOutput").ap()
                        dw = nc.dram_tensor("dbg_w", [128, KC, 128], BF16, kind="ExternalOutput").ap()
                        dxx = nc.dram_tensor("dbg_x", [128, KC, 512], BF16, kind="ExternalOutput").ap()
                        dp = nc.dram_tensor("dbg_p", [128, 512], F32, kind="ExternalOutput").ap()
                        for nm_, dst_, src_, rd_ in (("r", dr, r_.ap, [r_.res]), ("w", dw, w1c, [ws.res]), ("x", dxx, xhi[:, :, tok(tb)], xr)):
                            rr_ = fw.res("dbg" + nm_)
                            R_y.append(rr_)
                            fw.dma("sync", lambda e, dst_=dst_, src_=src_: e.dma_start(out=dst_, in_=src_), "dbgk" + nm_, reads=rd_, writes=[rr_])
                        ph2 = GEN.next()
                        mm(ph2.ap[:, :], [(w1c[:, k, :], xhi[:, k, tok(tb)]) for k in range(KC)], xr + [ws.res], ph2.res)
                        r2_ = TF.next()
                        A(lambda e, ph2=ph2, r2_=r2_: e.activation(out=r2_.ap, in_=ph2.ap[:, :], func=ACT.Copy), [ph2.res], [r2_.res])
                        rr_ = fw.res("dbgp")
                        R_y.append(rr_)
                        fw.dma("sync", lambda e, dp=dp, r2_=r2_: e.dma_start(out=dp, in_=r2_.ap), "dbgkp", reads=[r2_.res], writes=[rr_])
            if KSTOP <= 1:
                return
            for c2 in range(KC):
                wsa = W2.next()
                w2a = wsa.ap[:, 0:16 * 128].rearrange("p (k n) -> p k n", k=16)
                load_w(wsa, w2a, wsrc(w2_d[l], 0, 2048, c2 * 128, (c2 + 1) * 128))
                wsb2 = W2.next()
                w2b = wsb2.ap[:, 0:16 * 128].rearrange("p (k n) -> p k n", k=16)
                load_w(wsb2, w2b, wsrc(w2_d[l], 2048, 4096, c2 * 128, (c2 + 1) * 128))
                for sbk in range(SBH):
                    tb = th * SBH + sbk
                    pf = GEN.next()
                    mm(pf.ap[:, :], [((w2a if j < 16 else w2b)[:, j % 16, :], h1[:, j, sbk * 512:(sbk + 1) * 512]) for j in range(32)],
                       [R_h(j, sbk) for j in range(32)] + [wsa.res, wsb2.res], pf.res)
                    rz = R_z[sbk][c2]
                    V(lambda e, pf=pf, c2=c2, tb=tb, zs=zt[sbk]: e.scalar_tensor_tensor(out=zs[:, c2, :], in0=xhi[:, c2, tok(tb)], scalar=ALPHA,
                                                                                    in1=pf.ap[:, :], op0=ALU.mult, op1=ALU.add),
                      [pf.res, R_X[c2][tb]], [rz])
                    V(lambda e, c2=c2, tb=tb, zs=zt[sbk]: e.scalar_tensor_tensor(out=zs[:, c2, :], in0=xlo[:, c2, tok(tb)], scalar=ALPHA,
                                                                             in1=zs[:, c2, :], op0=ALU.mult, op1=ALU.add),
                      [R_X[c2][tb], rz], [rz])
            if KSTOP <= 2:
                return
            for sbk in range(SBH):
                layernorm(th * SBH + sbk, sbk, LN_EPS, l2g, l2b, last)

    R_y = []
    USE = {}
    for s in range(NSEQ):
        fw.barrier()
        if upto >= 1:
            load_sequence(s)
        fw.barrier()
        if upto >= 2:
            build_tables(s)
        if upto >= 3:
            for l in range(NLAYER):
                layer(s, l, last=(l == NLAYER - 1))
    if KDBG == 77:
        dx = nc.dram_tensor("dbg_xhi", [128, KC, S], BF16, kind="ExternalOutput").ap()
        do = nc.dram_tensor("dbg_oT", [128, 16, S], BF16, kind="ExternalOutput").ap()
        fw.barrier()
        r1_, r2_ = fw.res("d1"), fw.res("d2")
        fw.dma("sync", lambda e: e.dma_start(out=dx, in_=xhi[:]), "dbg1", reads=[R_X[c][t] for c in range(KC) for t in range(TB)], writes=[r1_])
        fw.dma("sync", lambda e: e.dma_start(out=do, in_=oT[:]), "dbg2", reads=[R_O[c][t] for c in range(16) for t in range(TB)], writes=[r2_])
        R_y = R_y + [r1_, r2_]
    fw.op("sync", lambda e: e.nop(), reads=R_y, writes=[])
    counts = fw.finish()
    counts["arena"] = AR
    counts.update(USE)
    counts["nops"] = len(fw.ops)
    return nc, counts


_CACHE = {}
WEIGHT_KEYS = ["w_in", "b_gate", "mla_q_norm", "mla_kv_norm", "mla_w_uq", "mla_w_ukv", "diff_lambda",
               "diff_subln", "mem_w_kv", "w_branch", "w_out", "ln1_g", "ln1_b", "mlp_w1", "mlp_w2", "ln2_g", "ln2_b"]


def kernel(**inputs):
    x = np.ascontiguousarray(np.asarray(inputs["x"], dtype=np.float32))
    mem = np.ascontiguousarray(np.asarray(inputs["mem"], dtype=np.float32))
    pos = np.ascontiguousarray(np.asarray(inputs["positions"], dtype=np.int32))
    B, S, _ = x.shape
    nseq = B // N_CORES
    key = (nseq, S, DEPTH)
    if key not in _CACHE:
        _CACHE[key] = build_program(nseq, S, DEPTH)[0]
    nc = _CACHE[key]
    consts = _host_consts()
    shared = {k: np.ascontiguousarray(np.asarray(inputs[k], dtype=np.float32)) for k in WEIGHT_KEYS}
    shared["c_ident"] = consts["ident"]
    shared["c_pvec"] = consts["pvec"]
    shared["c_rot"] = consts["rot"]
    in_maps = []
    for c in range(N_CORES):
        m = dict(shared)
        m["x"] = x[c * nseq:(c + 1) * nseq]
        m["mem"] = mem[c * nseq:(c + 1) * nseq]
        m["positions"] = pos[c * nseq:(c + 1) * nseq]
        in_maps.append(m)
    res = run_bass_kernel_spmd(nc, in_maps, core_ids=list(range(N_CORES)))
    out = np.concatenate([np.asarray(r["y"]) for r in res.results], axis=0)
    return out.astype(np.float32)
```

```python
import math
import os
import numpy as np
KDBG = float(os.environ.get('KDBG', '99'))
KSTOP = int(os.environ.get('KSTOP', '99'))
import concourse.bass as bass
import concourse.mybir as mybir
from concourse.bass_utils import run_bass_kernel_spmd

F32 = mybir.dt.float32
BF16 = mybir.dt.bfloat16
I32 = mybir.dt.int32
ALU = mybir.AluOpType
ACT = mybir.ActivationFunctionType

D = 1024
KC = 8
MEM_LEN = 256
DEPTH = 4
N_CORES = 8
BATCH = 16
SEQ = 2048
ROPE_THETA = 500000.0
ALPHA = (2 * DEPTH) ** 0.25
LN_EPS = 1e-5
C_CQ, C_CKV, C_KPE, C_DQ, C_DK, C_DV, C_MQ, C_GATE = 0, 384, 640, 672, 1696, 2720, 3744, 4256


class Res:
    __slots__ = ("name", "last_writer", "readers", "excl")

    def __init__(self, name="", excl=False):
        self.name = name
        self.last_writer = None
        self.readers = []
        self.excl = excl


class Op:
    __slots__ = ("eng", "emit", "deps", "signal", "seq", "is_dma", "dma_sem", "dma_val", "big")

    def __init__(self, eng, emit, is_dma=False, big=False):
        self.eng = eng
        self.emit = emit
        self.deps = []
        self.signal = False
        self.seq = 0
        self.is_dma = is_dma
        self.dma_sem = None
        self.dma_val = 0
        self.big = big


class FW:
    ENGS = ("tensor", "vector", "scalar", "gpsimd", "sync")

    def __init__(self, nc, same_engine_sync=True):
        self.nc = nc
        self.ops = []
        self.same_engine_sync = same_engine_sync
        self.dma_sems = {}
        self.all_res = []

    def res(self, name="", excl=False):
        r = Res(name, excl)
        self.all_res.append(r)
        return r

    def op(self, eng, emit, reads=(), writes=(), big=False):
        o = Op(eng, emit, big=big)
        self._track(o, reads, writes)
        return o

    def dma(self, eng, emit, semkey, reads=(), writes=()):
        o = Op(eng, emit, is_dma=True)
        if semkey not in self.dma_sems:
            self.dma_sems[semkey] = [self.nc.alloc_semaphore("dq%d" % len(self.dma_sems)), 0]
        ent = self.dma_sems[semkey]
        ent[1] += 16
        o.dma_sem = ent[0]
        o.dma_val = ent[1]
        self._track(o, reads, writes)
        return o

    def _track(self, o, reads, writes):
        ex = [r for r in reads if r.excl]
        if ex:
            reads = [r for r in reads if not r.excl]
            writes = list(writes) + [r for r in ex if r not in writes]
        deps = []
        for r in reads:
            if r.last_writer is not None:
                deps.append(r.last_writer)
        for w in writes:
            if w.last_writer is not None:
                deps.append(w.last_writer)
            deps.extend(w.readers)
        for r in reads:
            r.readers.append(o)
        for w in writes:
            w.last_writer = o
            w.readers = []
        seen = set()
        for d in deps:
            if id(d) not in seen and d is not o:
                seen.add(id(d))
                o.deps.append(d)
        self.ops.append(o)

    def barrier(self):
        for e in self.ENGS:
            self.op(e, lambda eng: eng.nop(), reads=(), writes=self.all_res)

    def finish(self):
        nc = self.nc
        engs = {e: getattr(nc, e) for e in self.ENGS}
        sems = {e: nc.alloc_semaphore("eng_" + e) for e in self.ENGS}
        for o in self.ops:
            kept = []
            for d in o.deps:
                if d.is_dma:
                    kept.append(d)
                    continue
                if d.eng == o.eng and not o.is_dma:
                    if d.eng == "tensor":
                        continue
                    if not self.same_engine_sync:
                        continue
                    if d.big and o.big:
                        continue
                kept.append(d)
                d.signal = True
            o.deps = kept
        cnt = {e: 0 for e in self.ENGS}
        for o in self.ops:
            if o.signal and not o.is_dma:
                cnt[o.eng] += 1
                o.seq = cnt[o.eng]
        waited = {e: {} for e in self.ENGS}
        for o in self.ops:
            eng = engs[o.eng]
            need = {}
            for d in o.deps:
                if d.is_dma:
                    s, v = d.dma_sem, d.dma_val
                else:
                    s, v = sems[d.eng], d.seq
                if need.get(s.num, (None, 0))[1] < v:
                    need[s.num] = (s, v)
            for num, (s, v) in need.items():
                if waited[o.eng].get(num, 0) >= v:
                    continue
                eng.wait_ge(s, v)
                waited[o.eng][num] = v
            ins = o.emit(eng)
            if o.is_dma:
                ins.then_inc(o.dma_sem, 16)
            elif o.signal:
                ins.then_inc(sems[o.eng], 1)
        return cnt


class Buf:
    __slots__ = ("ap", "res", "key")

    def __init__(self, ap, res, key=None):
        self.ap = ap
        self.res = res
        self.key = key


class Ring:
    def __init__(self, items):
        self.items = items
        self.i = 0

    def next(self):
        it = self.items[self.i % len(self.items)]
        self.i += 1
        return it


def _host_consts():
    c = {}
    c["ident"] = np.eye(128, dtype=np.float32)
    inv_m = (ROPE_THETA ** (-np.arange(0, 32, 2, dtype=np.float32) / np.float32(32))).astype(np.float32)
    inv_d = (ROPE_THETA ** (-np.arange(0, 16, 2, dtype=np.float32) / np.float32(16))).astype(np.float32)
    pv = np.zeros((128, 8), dtype=np.float32)
    rot_m = np.zeros((128, 128), dtype=np.float32)
    rot_d = np.zeros((128, 128), dtype=np.float32)
    for i in range(32):
        p = 64 + i
        pv[p, 0] = inv_m[i % 16]
        pv[p, 1] = -1.0 if i < 16 else 1.0
        partner = 64 + (i + 16 if i < 16 else i - 16)
        rot_m[partner, p] = 1.0
    for o in (0, 64):
        for i in range(16):
            p = o + i
            pv[p, 2] = inv_d[i % 8]
            pv[p, 3] = -1.0 if i < 8 else 1.0
            partner = o + (i + 8 if i < 8 else i - 8)
            rot_d[partner, p] = 1.0
    c["pvec"] = pv
    c["rot"] = np.stack([rot_m, rot_d], axis=0)
    return c


def build_program(NSEQ, S, NLAYER, layer_ids=None, upto=99):
    assert S % 512 == 0
    TB = S // 512
    TT = S // 128
    HB = min(1024, S // 2)
    NH = S // HB
    SBH = HB // 512
    nc = bass.Bass("TRN2", target_bir_lowering=False)
    fw = FW(nc)
    L = NLAYER

    def din(name, shape, dt=F32):
        return nc.dram_tensor(name, list(shape), dt, kind="ExternalInput").ap()

    x_d = din("x", [NSEQ, S, D])
    mem_d = din("mem", [NSEQ, MEM_LEN, D])
    pos_d = din("positions", [NSEQ, S], I32)
    w_in_d = din("w_in", [L, D, 7328])
    b_gate_d = din("b_gate", [L, 3, D])
    qn_d = din("mla_q_norm", [L, 384])
    kvn_d = din("mla_kv_norm", [L, 256])
    wuq_d = din("mla_w_uq", [L, 384, 768])
    wukv_d = din("mla_w_ukv", [L, 256, 1024])
    dlam_d = din("diff_lambda", [L, 4, 64])
    subln_d = din("diff_subln", [L, 128])
    wkv_d = din("mem_w_kv", [L, D, 1024])
    wbr_d = din("w_branch", [L, 2048, D])
    wout_d = din("w_out", [L, D, D])
    ln1g_d = din("ln1_g", [L, D])
    ln1b_d = din("ln1_b", [L, D])
    w1_d = din("mlp_w1", [L, D, 4096])
    w2_d = din("mlp_w2", [L, 4096, D])
    ln2g_d = din("ln2_g", [L, D])
    ln2b_d = din("ln2_b", [L, D])
    ident_d = din("c_ident", [128, 128])
    pvec_d = din("c_pvec", [128, 8])
    rot_d_ = din("c_rot", [2, 128, 128])
    y_d = nc.dram_tensor("y", [NSEQ, S, D], F32, kind="ExternalOutput").ap()

    def sb(name, shape, dt):
        return nc.alloc_sbuf_tensor(name, list(shape), dt)

    xhi = sb("xhi", [128, KC, S], BF16)
    xlo = sb("xlo", [128, KC, S], BF16)
    oT = sb("oT", [128, 16, S], BF16)
    memT = sb("memT", [128, KC, MEM_LEN], BF16)
    ident_f = sb("ident_f", [128, 128], F32)
    ones_b = sb("ones_b", [128, 128], BF16)
    onesD = sb("onesD", [128, 128], BF16)
    rot_f = sb("rot_f", [128, 2, 128], F32)
    rot_b = sb("rot_b", [128, 2, 128], BF16)
    pvec = sb("pvec", [128, 8], F32)
    neghalf = sb("neghalf", [128, 512], F32)
    halfpi = sb("halfpi", [128, 1], F32)
    par = sb("par", [128, L, 64], F32)
    bgh = par[:, :, 0:24]
    gq = par[:, :, 24:27]
    gkv = par[:, :, 27:29]
    gsub = par[:, :, 29]
    l1g = par[:, :, 30:38]
    l1b = par[:, :, 38:46]
    l2g = par[:, :, 46:54]
    l2b = par[:, :, 54:62]
    dls = sb("dls", [128, L, 2], F32)
    nlam = sb("nlam", [128, L], F32)
    R_const = fw.res("const")
    R_par = fw.res("par")
    R_memT = fw.res("memT")
    R_X = [[fw.res("x%d_%d" % (c, t)) for t in range(TB)] for c in range(KC)]
    R_O = [[fw.res("o%d_%d" % (c, t)) for t in range(TB)] for c in range(16)]

    AR = (nc.sbuf_bytes_remaining - 2048) // 2 // 16 * 16
    arena = sb("arena", [128, AR], BF16)

    class Carver:
        def __init__(self):
            self.off = 0

        def bf(self, n):
            v = arena[:, self.off:self.off + n]
            self.off += n
            assert self.off <= AR, ("arena overflow", self.off, AR)
            return v

        def f32(self, n):
            return self.bf(2 * n).bitcast(F32)

    psb = [nc.alloc_psum_tensor("ps%d" % i, [128, 512], F32) for i in range(8)]
    PS = [Buf(psb[i], fw.res("ps%d" % i, excl=True)) for i in range(8)]

    def V(fn, reads, writes):
        fw.op("vector", fn, reads, writes)

    def A(fn, reads, writes):
        fw.op("scalar", fn, reads, writes)

    def G(fn, reads, writes):
        fw.op("gpsimd", fn, reads, writes)

    def mm(out_ap, pairs, reads, wres):
        n = len(pairs)
        for i, (l_, r_) in enumerate(pairs):
            fw.op("tensor",
                  lambda e, l_=l_, r_=r_, i=i: e.matmul(out_ap, lhsT=l_, rhs=r_, start=(i == 0), stop=(i == n - 1)),
                  reads, [wres])

    def load_w(slot, dst_ap, src_ap):
        fw.dma("gpsimd", lambda e: e.dma_start(out=dst_ap, in_=src_ap), slot.key, writes=[slot.res])

    def wsrc(w_ap, r0, r1, c0, c1):
        return w_ap[r0:r1, c0:c1].rearrange("(c p) n -> p c n", p=128)

    def tok(tb):
        return slice(tb * 512, (tb + 1) * 512)

    fw.dma("sync", lambda e: e.dma_start(out=ident_f[:], in_=ident_d), "c0", writes=[R_const])
    fw.dma("sync", lambda e: e.dma_start(out=pvec[:], in_=pvec_d), "c1", writes=[R_const])
    fw.dma("sync", lambda e: e.dma_start(out=rot_f[:], in_=rot_d_.rearrange("r k m -> k r m")), "c2", writes=[R_const])
    V(lambda e: e.tensor_copy(out=rot_b[:], in_=rot_f[:]), [R_const], [R_const])
    V(lambda e: e.memset(ones_b[:], 1.0), [], [R_const])
    V(lambda e: e.memset(onesD[:], 1.0 / 1024.0), [], [R_const])
    V(lambda e: e.memset(neghalf[:], -0.5), [], [R_const])
    V(lambda e: e.memset(halfpi[:], math.pi / 2), [], [R_const])

    car0 = Carver()
    dl = car0.f32(L * 256).rearrange("p (l n) -> p l n", l=L)
    dlp = car0.f32(L * 128).rearrange("p (l n) -> p l n", l=L)

    stage = car0.f32(L * 128).rearrange("p (l n) -> p l n", l=L)
    R_stage = fw.res("stage")
    for l in range(L):
        rows = [(0, b_gate_d[l].rearrange("i (c p) -> (i c) p", p=128), 24),
                (24, qn_d[l].rearrange("(c p) -> c p", p=128), 3),
                (27, kvn_d[l].rearrange("(c p) -> c p", p=128), 2),
                (29, subln_d[l].rearrange("(c p) -> c p", p=128), 1),
                (30, ln1g_d[l].rearrange("(c p) -> c p", p=128), 8),
                (38, ln1b_d[l].rearrange("(c p) -> c p", p=128), 8),
                (46, ln2g_d[l].rearrange("(c p) -> c p", p=128), 8),
                (54, ln2b_d[l].rearrange("(c p) -> c p", p=128), 8)]
        for (r0, src, n) in rows:
            fw.dma("sync", lambda e, r0=r0, src=src, n=n, l=l: e.dma_start(out=stage[r0:r0 + n, l, :], in_=src), "pst", writes=[R_stage])
        fw.dma("sync", lambda e, l=l: e.dma_start(out=dl[:, l, :], in_=dlam_d[l].rearrange("a b -> (a b)").partition_broadcast(128)),
               "p8", writes=[R_par])
    for l in range(L):
        pp = PS[l % 8]
        fw.op("tensor", lambda e, l=l, pp=pp: e.transpose(out=pp.ap[:, 0:62], in_=stage[0:62, l, :], identity=ident_f[0:62, 0:62]),
              [R_stage, R_const], [pp.res])
        V(lambda e, l=l, pp=pp: e.tensor_copy(out=par[:, l, 0:62], in_=pp.ap[:, 0:62]), [pp.res], [R_par])
    for l in range(L):
        lam_init = 0.8 - 0.6 * math.exp(-0.3 * ((layer_ids[l]) if layer_ids is not None else l))
        V(lambda e, l=l: e.tensor_scalar(out=bgh[:, l, :], in0=bgh[:, l, :], scalar1=0.5, scalar2=None, op0=ALU.mult), [R_par], [R_par])
        V(lambda e, l=l: e.tensor_tensor(out=dlp[:, l, 0:64], in0=dl[:, l, 0:64], in1=dl[:, l, 64:128], op=ALU.mult), [R_par], [R_par])
        V(lambda e, l=l: e.tensor_tensor(out=dlp[:, l, 64:128], in0=dl[:, l, 128:192], in1=dl[:, l, 192:256], op=ALU.mult), [R_par], [R_par])
        V(lambda e, l=l: e.reduce_sum(out=dls[:, l, 0:1], in_=dlp[:, l, 0:64], axis=mybir.AxisListType.X), [R_par], [R_par])
        V(lambda e, l=l: e.reduce_sum(out=dls[:, l, 1:2], in_=dlp[:, l, 64:128], axis=mybir.AxisListType.X), [R_par], [R_par])
        A(lambda e, l=l: e.activation(out=dls[:, l, :], in_=dls[:, l, :], func=ACT.Exp), [R_par], [R_par])
        V(lambda e, l=l: e.tensor_tensor(out=nlam[:, l:l + 1], in0=dls[:, l, 1:2], in1=dls[:, l, 0:1], op=ALU.subtract), [R_par], [R_par])
        V(lambda e, l=l, li=lam_init: e.tensor_scalar(out=nlam[:, l:l + 1], in0=nlam[:, l:l + 1], scalar1=-li, scalar2=None, op0=ALU.add), [R_par], [R_par])

    def attn_chunk(car_state, qc, qT, kT, vfn, kb, Kdim, dv, ro, nk, scale, q_reads, k_reads, v_reads, post, merged_den=False):
        SC, ACC, PT, pending = car_state
        pv = ACC.next()
        den = None if merged_den else ACC.next()

        def scores(kt):
            sc = SC.next()
            mm(sc.ap[:, :], [(kT[kb:kb + Kdim, kt * 128:(kt + 1) * 128], qT[kb:kb + Kdim, tok(qc)])],
               q_reads + k_reads, sc.res)
            return sc

        LA = len(SC.items) - 1
        scq = [scores(i) for i in range(min(LA, nk))]
        for kt in range(nk):
            sc = scq.pop(0)
            pt = PT.next()
            A(lambda e, sc=sc, pt=pt: e.activation(out=pt.ap, in_=sc.ap[:, :], func=ACT.Exp, scale=scale),
              [sc.res], [pt.res])
            if kt + LA < nk:
                scq.append(scores(kt + LA))
            fw.op("tensor", lambda e, pt=pt, kt=kt: e.matmul(pv.ap[ro:ro + dv, :], lhsT=vfn(kt), rhs=pt.ap,
                                                             start=(kt == 0), stop=(kt == nk - 1)),
                  [pt.res] + v_reads, [pv.res])
            if not merged_den:
                fw.op("tensor", lambda e, pt=pt, kt=kt: e.matmul(den.ap[ro:ro + dv, :], lhsT=ones_b[:, 0:dv], rhs=pt.ap,
                                                                 start=(kt == 0), stop=(kt == nk - 1)),
                      [pt.res, R_const], [den.res])
            if kt == min(10, nk - 1) and pending:
                for f_ in pending:
                    f_()
                del pending[:]
        post(qc, pv, den)

    def attention(car_state, qT, kT, vfn, kb, Kdim, dv, ro, nk, scale, q_reads, k_reads, v_reads, post):
        for qc in range(TB):
            attn_chunk(car_state, qc, qT, kT, vfn, kb, Kdim, dv, ro, nk, scale, q_reads, k_reads, v_reads, post)

    _csts = {}

    def cst(val):
        key = float(val)
        if key not in _csts:
            t = sb("cst%d" % len(_csts), [128, 1], F32)
            r = fw.res("cst")
            V(lambda e, t=t, key=key: e.memset(t[:], key), [], [r])
            _csts[key] = (t, r)
        return _csts[key]

    def rsqrt_act(dst, src_ap, reads, scale=1.0, bias=None):
        if bias is None:
            A(lambda e: e.activation(out=dst.ap, in_=src_ap, func=ACT.Ln, scale=float(scale)), reads, [dst.res])
        else:
            bt, br = cst(bias)
            A(lambda e: e.activation(out=dst.ap, in_=src_ap, func=ACT.Ln, scale=float(scale), bias=bt[:, 0:1]),
              reads + [br], [dst.res])
        A(lambda e: e.activation(out=dst.ap, in_=dst.ap, func=ACT.Exp, scale=-0.5), [dst.res], [dst.res])

    def load_sequence(s):
        car = Carver()
        xin = [Buf(car.f32(1024), fw.res("xin%d" % i), "xin%d" % i) for i in range(2)]
        ring = Ring(xin)
        pr = Ring(PS)
        for tt in range(TT):
            xb = ring.next()
            fw.dma("sync", lambda e, xb=xb, tt=tt: e.dma_start(out=xb.ap, in_=x_d[s, tt * 128:(tt + 1) * 128, :]),
                   xb.key, writes=[xb.res])
            tb = tt // 4
            for g in range(2):
                p = pr.next()
                for cc in range(4):
                    c = g * 4 + cc
                    fw.op("tensor", lambda e, p=p, xb=xb, c=c, cc=cc: e.transpose(out=p.ap[:, cc * 128:(cc + 1) * 128],
                                                                                 in_=xb.ap[:, c * 128:(c + 1) * 128], identity=ident_f[:]),
                          [xb.res, R_const], [p.res])
                wr = [R_X[g * 4 + cc][tb] for cc in range(4)]
                hi_v = xhi[:, g * 4:g * 4 + 4, tt * 128:(tt + 1) * 128]
                lo_v = xlo[:, g * 4:g * 4 + 4, tt * 128:(tt + 1) * 128]
                pv3 = p.ap[:, :].rearrange("p (a b) -> p a b", a=4)
                A(lambda e, hi_v=hi_v, pv3=pv3: e.activation(out=hi_v, in_=pv3, func=ACT.Copy), [p.res], wr)
                V(lambda e, lo_v=lo_v, hi_v=hi_v, pv3=pv3: e.tensor_tensor(out=lo_v, in0=pv3, in1=hi_v, op=ALU.subtract),
                  [p.res] + wr, wr)
        for mt in range(MEM_LEN // 128):
            xb = ring.next()
            fw.dma("sync", lambda e, xb=xb, mt=mt: e.dma_start(out=xb.ap, in_=mem_d[s, mt * 128:(mt + 1) * 128, :]),
                   xb.key, writes=[xb.res])
            for g in range(2):
                p = pr.next()
                for cc in range(4):
                    c = g * 4 + cc
                    fw.op("tensor", lambda e, p=p, xb=xb, c=c, cc=cc: e.transpose(out=p.ap[:, cc * 128:(cc + 1) * 128],
                                                                                 in_=xb.ap[:, c * 128:(c + 1) * 128], identity=ident_f[:]),
                          [xb.res, R_const], [p.res])
                V(lambda e, p=p, g=g, mt=mt: e.tensor_copy(out=memT[:, g * 4:g * 4 + 4, mt * 128:(mt + 1) * 128],
                                                          in_=p.ap[:, :].rearrange("p (a b) -> p a b", a=4)), [p.res], [R_memT])

    tab_d = nc.dram_tensor("tab_scratch", [4, 128, S], BF16, kind="Internal").ap()
    R_tabd = fw.res("tabd")

    def build_tables(s):
        car = Carver()
        posi = car.bf(2 * S).bitcast(I32)
        posf = car.f32(S)
        ang = car.f32(S)
        t1 = car.f32(S)
        t2 = car.f32(S)
        ki = car.bf(2 * S).bitcast(I32)
        tb16 = car.bf(S)
        R_t = fw.res("tabtmp")
        fw.dma("sync", lambda e: e.dma_start(out=posi, in_=pos_d[s, :].partition_broadcast(128)), "posi", writes=[R_t])
        V(lambda e: e.tensor_copy(out=posf, in_=posi), [R_t], [R_t])
        inv2pi = float(1.0 / (2 * math.pi))
        for ti, (fcol, scol) in enumerate([(0, 1), (2, 3)]):
            V(lambda e, fcol=fcol: e.tensor_scalar(out=ang, in0=posf, scalar1=pvec[:, fcol:fcol + 1], scalar2=None, op0=ALU.mult),
              [R_t, R_const], [R_t])
            for which in range(2):
                if which == 0:
                    V(lambda e: e.tensor_scalar(out=t1, in0=ang, scalar1=halfpi[:, 0:1], scalar2=None, op0=ALU.add), [R_t, R_const], [R_t])
                    src = t1
                else:
                    src = ang
                V(lambda e, src=src: e.tensor_scalar(out=t2, in0=src, scalar1=inv2pi, scalar2=None, op0=ALU.mult), [R_t], [R_t])
                V(lambda e: e.tensor_copy(out=ki, in_=t2), [R_t], [R_t])
                V(lambda e: e.tensor_copy(out=t2, in_=ki), [R_t], [R_t])
                V(lambda e, src=src: e.scalar_tensor_tensor(out=t2, in0=t2, scalar=float(-2 * math.pi), in1=src, op0=ALU.mult, op1=ALU.add),
                  [R_t], [R_t])
                A(lambda e: e.activation(out=t2, in_=t2, func=ACT.Sin), [R_t], [R_t])
                if which == 0:
                    V(lambda e: e.tensor_copy(out=tb16, in_=t2), [R_t], [R_t])
                else:
                    V(lambda e, scol=scol: e.tensor_scalar(out=tb16, in0=t2, scalar1=pvec[:, scol:scol + 1], scalar2=None, op0=ALU.mult),
                      [R_t, R_const], [R_t])
                idx = ti * 2 + which
                fw.dma("sync", lambda e, idx=idx: e.dma_start(out=tab_d[idx], in_=tb16), "tabst", reads=[R_t], writes=[R_tabd])

    def layer(s, l, last):
        lid = layer_ids[l] if layer_ids is not None else l
        lam_init = 0.8 - 0.6 * math.exp(-0.3 * lid)
        fw.barrier()
        car = Carver()
        tabs = car.bf(4 * S).rearrange("p (a b) -> p a b", a=4)
        R_tab = fw.res("tab")
        Cm, Sm, Cd, Sd = tabs[:, 0, :], tabs[:, 1, :], tabs[:, 2, :], tabs[:, 3, :]
        fw.dma("sync", lambda e: e.dma_start(out=tabs, in_=tab_d.rearrange("a p s -> p a s")), "tabld", reads=[R_tabd], writes=[R_tab])
        WS = Ring([Buf(car.bf(4096), fw.res("w%d" % i), "w%d" % i) for i in range(2)])
        qTt = car.bf(S)
        kTt = car.bf(S)
        vt = car.bf(TT * 128).rearrange("p (a b) -> p a b", a=TT)
        R_q, R_k, R_v = fw.res("q"), fw.res("k"), fw.res("v")
        PT = Ring([Buf(car.bf(512), fw.res("pt%d" % i)) for i in range(3)])
        TF = Ring([Buf(car.f32(512), fw.res("tf%d" % i)) for i in range(5)])
        TBf = Ring([Buf(car.bf(512), fw.res("tb%d" % i)) for i in range(2)])
        qz1 = car.bf(S)
        R_q1 = fw.res("q1")
        kpe = car.bf(max(S, 2048))
        R_kpe = fw.res("kpe")
        SC = Ring(PS[0:3])
        ACC = Ring(PS[3:7])
        GEN = Ring([PS[7]])
        GP = Ring(PS[3:8])
        pending = []
        car_state = (SC, ACC, PT, pending)

        def flush():
            for f_ in pending:
                f_()
            del pending[:]
        USE["attn"] = car.off
        cqn = lambda j, tb: oT[:, 4 + j, tok(tb)]
        ckvn = lambda j, tb: oT[:, 7 + j, tok(tb)]
        R_cq = lambda j, tb: R_O[4 + j][tb]
        R_ckv = lambda j, tb: R_O[7 + j][tb]

        def rope_rows(dst_ap, a_ps, rows, tabC, tabS, rotsel, tsl, reads, wres, split=None):
            ab = TBf.next()
            A(lambda e: e.activation(out=ab.ap[rows, :], in_=a_ps.ap[rows, :], func=ACT.Copy), [a_ps.res], [ab.res])
            bp = GP.next()
            mm(bp.ap[:, :], [(rot_b[:, rotsel, :], ab.ap)], [ab.res, R_const], bp.res)
            t1 = TF.next()
            V(lambda e: e.tensor_tensor(out=t1.ap[rows, :], in0=a_ps.ap[rows, :], in1=tabC[rows, tsl], op=ALU.mult),
              [a_ps.res, R_tab], [t1.res])
            t2 = TF.next()
            V(lambda e: e.tensor_tensor(out=t2.ap[rows, :], in0=bp.ap[rows, :], in1=tabS[rows, tsl], op=ALU.mult),
              [bp.res, R_tab], [t2.res])
            if split is None:
                V(lambda e: e.tensor_tensor(out=dst_ap, in0=t1.ap[rows, :], in1=t2.ap[rows, :], op=ALU.add),
                  [t1.res, t2.res] + reads, [wres])
            else:
                for (d_ap, d_res, pr) in split:
                    V(lambda e, d_ap=d_ap, pr=pr: e.tensor_tensor(out=d_ap, in0=t1.ap[pr, :], in1=t2.ap[pr, :], op=ALU.add),
                      [t1.res, t2.res] + reads, [d_res])

        for b_ in TBf.items:
            V(lambda e, b_=b_: e.memset(b_.ap, 0.0), [], [b_.res])

        ws = WS.next()
        wcq = ws.ap[:, 0:KC * 384].rearrange("p (c n) -> p c n", c=KC)
        load_w(ws, wcq, wsrc(w_in_d[l], 0, D, C_CQ, C_CQ + 384))
        ws2 = WS.next()
        wck = ws2.ap[:, 0:KC * 288].rearrange("p (c n) -> p c n", c=KC)
        load_w(ws2, wck, wsrc(w_in_d[l], 0, D, C_CKV, C_CKV + 288))
        for tb in range(TB):
            xr = [R_X[k][tb] for k in range(KC)]
            for (wt, wsl, nj, cols, gvec, dstf, rdst, eps, nfe) in (
                    (wcq, ws, 3, 0, gq, cqn, R_cq, 1e-6, 384.0),
                    (wck, ws2, 2, 0, gkv, ckvn, R_ckv, 1e-6, 256.0)):
                pj = []
                sqs = []
                for j in range(nj):
                    p = GP.next()
                    pj.append(p)
                    mm(p.ap[:, :], [(wt[:, k, cols + j * 128:cols + (j + 1) * 128], xhi[:, k, tok(tb)]) for k in range(KC)],
                       xr + [wsl.res], p.res)
                    sq = PT.next()
                    A(lambda e, p=p, sq=sq: e.activation(out=sq.ap, in_=p.ap[:, :], func=ACT.Square), [p.res], [sq.res])
                    sqs.append(sq)
                pss = GP.next()
                mm(pss.ap[:, :], [(ones_b[:, :], sq.ap) for sq in sqs], [sq.res for sq in sqs] + [R_const], pss.res)
                rs = TF.next()
                rsqrt_act(rs, pss.ap[:, :], [pss.res], scale=1.0 / nfe, bias=eps)
                for j in range(nj):
                    V(lambda e, j=j, p=pj[j], rs=rs, gvec=gvec, dap=dstf(j, tb): e.scalar_tensor_tensor(
                        out=dap, in0=p.ap[:, :], scalar=gvec[:, l, j:j + 1], in1=rs.ap, op0=ALU.mult, op1=ALU.mult),
                      [pj[j].res, rs.res, R_par], [rdst(j, tb)])
            p = GP.next()
            mm(p.ap[64:96, :], [(wck[:, k, 256:288], xhi[:, k, tok(tb)]) for k in range(KC)], xr + [ws2.res], p.res)
            rope_rows(kpe[64:96, tok(tb)], p, slice(64, 96), Cm, Sm, 0, tok(tb), [], R_kpe)
        for h in range(8):
            ws = WS.next()
            wq = ws.ap[:, 0:3 * 96].rearrange("p (c n) -> p c n", c=3)
            wkv_ = ws.ap[:, 512:512 + 2 * 128].rearrange("p (c n) -> p c n", c=2)
            load_w(ws, wq, wsrc(wuq_d[l], 0, 384, h * 96, (h + 1) * 96))
            load_w(ws, wkv_, wsrc(wukv_d[l], 0, 256, h * 128, (h + 1) * 128))
            for tb in range(TB):
                cqr = [R_cq(j, tb) for j in range(3)]
                ckr = [R_ckv(j, tb) for j in range(2)]
                p = GP.next()
                mm(p.ap[0:96, :], [(wq[:, j, :], cqn(j, tb)) for j in range(3)], cqr + [ws.res], p.res)
                V(lambda e, p=p, tb=tb: e.tensor_copy(out=qTt[0:64, tok(tb)], in_=p.ap[0:64, :]), [p.res], [R_q])
                rope_rows(qTt[64:96, tok(tb)], p, slice(64, 96), Cm, Sm, 0, tok(tb), [], R_q)
                p2 = GP.next()
                mm(p2.ap[0:64, :], [(wkv_[:, j, 0:64], ckvn(j, tb)) for j in range(2)], ckr + [ws.res], p2.res)
                V(lambda e, p2=p2, tb=tb: e.tensor_copy(out=kTt[0:64, tok(tb)], in_=p2.ap[0:64, :]), [p2.res], [R_k])
                V(lambda e, tb=tb: e.tensor_copy(out=kTt[64:96, tok(tb)], in_=kpe[64:96, tok(tb)]), [R_kpe], [R_k])
                p3 = GP.next()
                for t4 in range(4):
                    tsl = slice(tb * 512 + t4 * 128, tb * 512 + (t4 + 1) * 128)
                    mm(p3.ap[:, t4 * 64:(t4 + 1) * 64], [(oT[:, 7 + j, tsl], wkv_[:, j, 64:128]) for j in range(2)], ckr + [ws.res], p3.res)
                V(lambda e, p3=p3, tb=tb, ro=(h % 2) * 64: e.tensor_copy(out=vt[:, tb * 4:(tb + 1) * 4, ro:ro + 64],
                                                        in_=p3.ap[:, 0:256].rearrange("p (a b) -> p a b", a=4)), [p3.res], [R_v])
            ro = (h % 2) * 64
            V(lambda e, ro=ro: e.memset(vt[:, :, 64 - ro:128 - ro], 1.0), [], [R_v])

            def post_mla(qc, acc, _den, h=h, ro=ro):
                prow = slice(ro, ro + 64)
                drow = slice(64 - ro, 128 - ro)
                T = TF.next()
                V(lambda e: e.reciprocal(out=T.ap[drow, :], in_=acc.ap[drow, :]), [acc.res], [T.res])

                def part_b():
                    g = GEN.next()
                    fw.op("tensor", lambda e: e.matmul(g.ap[prow, :], lhsT=ident_f[drow, drow], rhs=T.ap[drow, :], start=True, stop=True),
                          [T.res, R_const], [g.res])
                    rs = TF.next()
                    A(lambda e: e.activation(out=rs.ap[prow, :], in_=g.ap[prow, :], func=ACT.Copy), [g.res], [rs.res])
                    V(lambda e: e.tensor_tensor(out=oT[prow, h // 2, tok(qc)], in0=acc.ap[prow, :], in1=rs.ap[prow, :], op=ALU.mult),
                      [acc.res, rs.res], [R_O[h // 2][qc]])
                pending.append(part_b)

            for qc in range(TB):
                attn_chunk(car_state, qc, qTt, kTt, lambda kt: vt[:, kt, :], 0, 96, 128, 0, TT, 96.0 ** -0.5,
                           [R_q], [R_k], [R_v], post_mla, merged_den=True)
            flush()

        if upto < 4:
            return
        k1 = 1.0 / (128.0 * (1.0 - lam_init) ** 2)
        k2 = 1e-5 / ((1.0 - lam_init) ** 2)
        V(lambda e: e.memset(qTt[64:128, :], 0.0), [], [R_q])
        V(lambda e: e.memset(qz1[0:64, :], 0.0), [], [R_q1])
        for h in range(8):
            ws = WS.next()
            w3 = ws.ap[:, 0:KC * 384].rearrange("p (c n) -> p c n", c=KC)
            for i3, c0 in enumerate((C_DQ, C_DK, C_DV)):
                load_w(ws, w3[:, :, i3 * 128:(i3 + 1) * 128], wsrc(w_in_d[l], 0, D, c0 + h * 128, c0 + (h + 1) * 128))
            for tb in range(TB):
                xr = [R_X[k][tb] for k in range(KC)]
                for i3 in range(2):
                    p = GP.next()
                    mm(p.ap[:, :], [(w3[:, k, i3 * 128:(i3 + 1) * 128], xhi[:, k, tok(tb)]) for k in range(KC)], xr + [ws.res], p.res)
                    if i3 == 0:
                        rope_rows(None, p, slice(0, 128), Cd, Sd, 1, tok(tb), [], None,
                                  split=[(qTt[0:64, tok(tb)], R_q, slice(0, 64)), (qz1[64:128, tok(tb)], R_q1, slice(64, 128))])
                    else:
                        rope_rows(kTt[:, tok(tb)], p, slice(0, 128), Cd, Sd, 1, tok(tb), [], R_k)
                p3 = GP.next()
                for t4 in range(4):
                    tsl = slice(tb * 512 + t4 * 128, tb * 512 + (t4 + 1) * 128)
                    mm(p3.ap[:, t4 * 128:(t4 + 1) * 128], [(xhi[:, k, tsl], w3[:, k, 256:384]) for k in range(KC)], xr + [ws.res], p3.res)
                for t4 in range(4):
                    V(lambda e, p3=p3, tb=tb, t4=t4: e.tensor_copy(out=vt[:, tb * 4 + t4, :], in_=p3.ap[:, t4 * 128:(t4 + 1) * 128]), [p3.res], [R_v])
            keep = {}

            def post_d0(qc, pv, den):
                r0 = TF.next()
                V(lambda e: e.reciprocal(out=r0.ap, in_=den.ap[:, :]), [den.res], [r0.res])
                a_ = TF.next()
                V(lambda e: e.tensor_tensor(out=a_.ap, in0=pv.ap[:, :], in1=r0.ap, op=ALU.mult), [pv.res, r0.res], [a_.res])
                keep[qc] = a_

            def post_d1(qc, pv, den, h=h):
                a_ = keep[qc]
                r1 = TF.next()
                V(lambda e: e.reciprocal(out=r1.ap, in_=den.ap[:, :]), [den.res], [r1.res])
                b_ = TF.next()
                V(lambda e: e.scalar_tensor_tensor(out=b_.ap, in0=pv.ap[:, :], scalar=nlam[:, l:l + 1], in1=r1.ap,
                                                   op0=ALU.mult, op1=ALU.mult), [pv.res, r1.res, R_par], [b_.res])
                V(lambda e: e.tensor_tensor(out=a_.ap, in0=a_.ap, in1=b_.ap, op=ALU.add), [a_.res, b_.res], [a_.res])
                sq = TBf.next()
                V(lambda e: e.tensor_tensor(out=sq.ap, in0=a_.ap, in1=a_.ap, op=ALU.mult), [a_.res], [sq.res])

                def part_b():
                    pss = GEN.next()
                    mm(pss.ap[:, :], [(ones_b[:, :], sq.ap)], [sq.res, R_const], pss.res)
                    vv = TF.next()
                    rsqrt_act(vv, pss.ap[:, :], [pss.res], scale=k1, bias=k2)
                    V(lambda e: e.scalar_tensor_tensor(out=oT[:, 4 + h, tok(qc)], in0=a_.ap, scalar=par[:, l, 29:30],
                                                       in1=vv.ap, op0=ALU.mult, op1=ALU.mult),
                      [a_.res, vv.res, R_par], [R_O[4 + h][qc]])
                pending.append(part_b)

            for qc in range(TB):
                for qz, rq, pst in ((qTt, R_q, post_d0), (qz1, R_q1, post_d1)):
                    attn_chunk(car_state, qc, qz, kTt, lambda kt: vt[:, kt, :], 0, 128, 128, 0, TT, 64.0 ** -0.5,
                               [rq], [R_k], [R_v], pst)
            flush()

        if upto < 5:
            return
        Km = kpe[:, 0:4 * MEM_LEN].rearrange("p (a b) -> p a b", a=4)
        Vm = kpe[:, 4 * MEM_LEN:4 * MEM_LEN + 2 * 512].rearrange("p (a b) -> p a b", a=2)
        for half in range(2):
            ws = WS.next()
            wk_ = ws.ap[:, 0:KC * 512].rearrange("p (c n) -> p c n", c=KC)
            load_w(ws, wk_, wsrc(wkv_d[l], 0, D, half * 512, (half + 1) * 512))
            if half == 0:
                for hh in range(4):
                    p = GP.next()
                    mm(p.ap[:, 0:MEM_LEN], [(wk_[:, k, hh * 128:(hh + 1) * 128], memT[:, k, :]) for k in range(KC)], [R_memT, ws.res], p.res)
                    V(lambda e, p=p, hh=hh: e.tensor_copy(out=Km[:, hh, :], in_=p.ap[:, 0:MEM_LEN]), [p.res], [R_kpe])
            else:
                for mt in range(2):
                    p = GP.next()
                    mm(p.ap[:, :], [(memT[:, k, mt * 128:(mt + 1) * 128], wk_[:, k, :]) for k in range(KC)], [R_memT, ws.res], p.res)
                    V(lambda e, p=p, mt=mt: e.tensor_copy(out=Vm[:, mt, :], in_=p.ap[:, :]), [p.res], [R_kpe])
        ws = WS.next()
        wmq = ws.ap[:, 0:KC * 512].rearrange("p (c n) -> p c n", c=KC)
        load_w(ws, wmq, wsrc(w_in_d[l], 0, D, C_MQ, C_MQ + 512))
        for hh in range(4):
            for tb in range(TB):
                xr = [R_X[k][tb] for k in range(KC)]
                p = GP.next()
                mm(p.ap[:, :], [(wmq[:, k, hh * 128:(hh + 1) * 128], xhi[:, k, tok(tb)]) for k in range(KC)], xr + [ws.res], p.res)
                V(lambda e, p=p, tb=tb: e.tensor_copy(out=qTt[:, tok(tb)], in_=p.ap[:, :]), [p.res], [R_q])

            def post_mem(qc, pv, den, hh=hh):
                r = TF.next()
                V(lambda e: e.reciprocal(out=r.ap, in_=den.ap[:, :]), [den.res], [r.res])
                V(lambda e: e.tensor_tensor(out=oT[:, 12 + hh, tok(qc)], in0=pv.ap[:, :], in1=r.ap, op=ALU.mult),
                  [pv.res, r.res], [R_O[12 + hh][qc]])

            attention(car_state, qTt, Km[:, hh, :], lambda kt, hh=hh: Vm[:, kt, hh * 128:(hh + 1) * 128], 0, 128, 128, 0, 2,
                      128.0 ** -0.5, [R_q], [R_kpe], [R_kpe], post_mem)

        if upto < 6:
            return
        fw.barrier()
        car = Carver()
        WG = Ring([Buf(car.bf(3072), fw.res("wm%d" % i), "wm%d" % i) for i in range(2)])
        WB = Ring([Buf(car.bf(2048), fw.res("wn%d" % i), "wn%d" % i) for i in range(2)])
        merged = car.bf(KC * HB).rearrange("p (c n) -> p c n", c=KC)
        R_m = [[fw.res("m%d_%d" % (c, sbk)) for sbk in range(SBH)] for c in range(KC)]
        zt = [car.f32(KC * 512).rearrange("p (c n) -> p c n", c=KC) for _ in range(1)]
        R_z = [[fw.res("z%d_%d" % (i, c)) for c in range(KC)] for i in range(1)]
        TF = Ring([Buf(car.f32(512), fw.res("tf%d" % i)) for i in range(6)])
        ZB = Ring([Buf(car.bf(512), fw.res("zb%d" % i)) for i in range(2)])
        GEN = Ring(PS[0:6])
        STAT = PS[6:8]
        USE["merge"] = car.off
        br_ranges = ((0, 4), (4, 12), (12, 16))

        def layernorm(tb, zi, eps, gvec, bvec, final_out):
            z = zt[zi]
            s1, s2 = STAT
            for c in range(KC):
                zb = ZB.next()
                A(lambda e, zb=zb, c=c: e.activation(out=zb.ap, in_=z[:, c, :], func=ACT.Copy), [R_z[zi][c]], [zb.res])
                fw.op("tensor", lambda e, zb=zb, c=c: e.matmul(s1.ap[:, :], lhsT=onesD[:, :], rhs=zb.ap, start=(c == 0), stop=(c == KC - 1)),
                      [zb.res, R_const], [s1.res])
                zq = ZB.next()
                A(lambda e, zq=zq, c=c: e.activation(out=zq.ap, in_=z[:, c, :], func=ACT.Square), [R_z[zi][c]], [zq.res])
                fw.op("tensor", lambda e, zq=zq, c=c: e.matmul(s2.ap[:, :], lhsT=onesD[:, :], rhs=zq.ap, start=(c == 0), stop=(c == KC - 1)),
                      [zq.res, R_const], [s2.res])
            msq = TF.next()
            A(lambda e: e.activation(out=msq.ap, in_=s1.ap[:, :], func=ACT.Square), [s1.res], [msq.res])
            vv = TF.next()
            V(lambda e: e.scalar_tensor_tensor(out=vv.ap, in0=s2.ap[:, :], scalar=float(eps), in1=msq.ap, op0=ALU.add, op1=ALU.subtract),
              [s2.res, msq.res], [vv.res])
            rsqrt_act(vv, vv.ap, [vv.res])
            nmr = TF.next()
            V(lambda e: e.scalar_tensor_tensor(out=nmr.ap, in0=s1.ap[:, :], scalar=-1.0, in1=vv.ap, op0=ALU.mult, op1=ALU.mult),
              [s1.res, vv.res], [nmr.res])
            for c in range(KC):
                rz = R_z[zi][c]
                V(lambda e, c=c: e.tensor_tensor(out=z[:, c, :], in0=z[:, c, :], in1=vv.ap, op=ALU.mult), [rz, vv.res], [rz])
                V(lambda e, c=c: e.tensor_tensor(out=z[:, c, :], in0=z[:, c, :], in1=nmr.ap, op=ALU.add), [rz, nmr.res], [rz])
                A(lambda e, c=c: e.activation(out=z[:, c, :], in_=z[:, c, :], func=ACT.Identity, bias=bvec[:, l, c:c + 1],
                                              scale=gvec[:, l, c:c + 1]), [rz, R_par], [rz])
                rx = R_X[c][tb]
                A(lambda e, c=c: e.activation(out=xhi[:, c, tok(tb)], in_=z[:, c, :], func=ACT.Copy), [rz], [rx])
                V(lambda e, c=c: e.tensor_tensor(out=xlo[:, c, tok(tb)], in0=z[:, c, :], in1=xhi[:, c, tok(tb)], op=ALU.subtract),
                  [rz, rx], [rx])
            if final_out:
                for t4 in range(4):
                    ob = OUTB.next()
                    for g in range(2):
                        p = GEN.next()
                        for cc in range(4):
                            c = g * 4 + cc
                            fw.op("tensor", lambda e, p=p, c=c, cc=cc, t4=t4: e.transpose(out=p.ap[:, cc * 128:(cc + 1) * 128],
                                                                                         in_=z[:, c, t4 * 128:(t4 + 1) * 128], identity=ident_f[:]),
                                  [R_z[zi][c], R_const], [p.res])
                        A(lambda e, p=p, ob=ob, g=g: e.activation(out=ob.ap[:, g * 512:(g + 1) * 512], in_=p.ap[:, :], func=ACT.Copy), [p.res], [ob.res])
                    t0 = tb * 512 + t4 * 128
                    ry = fw.res("y")
                    R_y.append(ry)
                    fw.dma("sync", lambda e, ob=ob, t0=t0: e.dma_start(out=y_d[s, t0:t0 + 128, :], in_=ob.ap), ob.key,
                           reads=[ob.res], writes=[ry])

        for th in range(NH):
            for c in range(KC):
                wsg = WG.next()
                wg = wsg.ap[:, 0:KC * 384].rearrange("p (k n) -> p k n", k=KC)
                for i in range(3):
                    c0 = C_GATE + i * D + c * 128
                    load_w(wsg, wg[:, :, i * 128:(i + 1) * 128], wsrc(w_in_d[l], 0, D, c0, c0 + 128))
                wsb = WB.next()
                wb = wsb.ap[:, 0:16 * 128].rearrange("p (k n) -> p k n", k=16)
                load_w(wsb, wb, wsrc(wbr_d[l], 0, 2048, c * 128, (c + 1) * 128))
                for sbk in range(SBH):
                    tb = th * SBH + sbk
                    xr = [R_X[k][tb] for k in range(KC)]
                    us = []
                    for i in range(3):
                        pg = GEN.next()
                        mm(pg.ap[:, :], [(wg[:, k, i * 128:(i + 1) * 128], xhi[:, k, tok(tb)]) for k in range(KC)], xr + [wsg.res], pg.res)
                        t_ = TF.next()
                        A(lambda e, pg=pg, t_=t_, i=i, c=c: e.activation(out=t_.ap, in_=pg.ap[:, :], func=ACT.Tanh,
                                                                        bias=bgh[:, l, i * 8 + c:i * 8 + c + 1], scale=0.5),
                          [pg.res, R_par], [t_.res])
                        pb = GEN.next()
                        k0, k1_ = br_ranges[i]
                        mm(pb.ap[:, :], [(wb[:, kk, :], oT[:, kk, tok(tb)]) for kk in range(k0, k1_)],
                           [R_O[kk][tb] for kk in range(k0, k1_)] + [wsb.res], pb.res)
                        V(lambda e, t_=t_, pb=pb: e.scalar_tensor_tensor(out=t_.ap, in0=t_.ap, scalar=1.0, in1=pb.ap[:, :],
                                                                        op0=ALU.add, op1=ALU.mult), [t_.res, pb.res], [t_.res])
                        us.append(t_)
                    V(lambda e, us=us: e.tensor_tensor(out=us[0].ap, in0=us[0].ap, in1=us[1].ap, op=ALU.add),
                      [us[0].res, us[1].res], [us[0].res])
                    V(lambda e, us=us, c=c, sbk=sbk: e.tensor_tensor(out=merged[:, c, sbk * 512:(sbk + 1) * 512], in0=us[0].ap, in1=us[2].ap, op=ALU.add),
                      [us[0].res, us[2].res], [R_m[c][sbk]])
            for sbk in range(SBH):
                tb = th * SBH + sbk
                for c2 in range(KC):
                    if c2 % 2 == 0:
                        wso = WB.next()
                        wo2 = wso.ap[:, 0:KC * 256].rearrange("p (k n) -> p k n", k=KC)
                        load_w(wso, wo2, wsrc(wout_d[l], 0, D, c2 * 128, (c2 + 2) * 128))
                    wo = wo2[:, :, (c2 % 2) * 128:(c2 % 2 + 1) * 128]
                    py = GEN.next()
                    mm(py.ap[:, :], [(wo[:, k, :], merged[:, k, sbk * 512:(sbk + 1) * 512]) for k in range(KC)],
                       [R_m[k][sbk] for k in range(KC)] + [wso.res], py.res)
                    rz = R_z[0][c2]
                    V(lambda e, py=py, c2=c2, tb=tb, z0=zt[0]: e.scalar_tensor_tensor(out=z0[:, c2, :], in0=xhi[:, c2, tok(tb)], scalar=2.0 * ALPHA,
                                                                           in1=py.ap[:, :], op0=ALU.mult, op1=ALU.add),
                      [py.res, R_X[c2][tb]], [rz])
                    V(lambda e, c2=c2, tb=tb, z0=zt[0]: e.scalar_tensor_tensor(out=z0[:, c2, :], in0=xlo[:, c2, tok(tb)], scalar=2.0 * ALPHA,
                                                                    in1=z0[:, c2, :], op0=ALU.mult, op1=ALU.add),
                      [R_X[c2][tb], rz], [rz])
                layernorm(tb, 0, 4.0 * LN_EPS, l1g, l1b, False)

        if upto < 7:
            return
        fw.barrier()
        car = Carver()
        W2 = Ring([Buf(car.bf(2048), fw.res("wb%d" % i), "wb%d" % i) for i in range(4)])
        zt = [car.f32(KC * 512).rearrange("p (c n) -> p c n", c=KC) for _ in range(SBH)]
        R_z = [[fw.res("zf%d_%d" % (i, c)) for c in range(KC)] for i in range(SBH)]
        TF = Ring([Buf(car.f32(512), fw.res("tg%d" % i)) for i in range(4)])
        ZB = Ring([Buf(car.bf(512), fw.res("zc%d" % i)) for i in range(2)])
        OUTB = Ring([Buf(car.f32(1024), fw.res("ob%d" % i), "ob%d" % i) for i in range(1)]) if last else None
        GEN = Ring(PS[0:6])
        STAT = PS[6:8]
        USE["ffn"] = car.off
        h1 = oT[:, :, :].rearrange("p c s -> p (c s)")[:, 0:32 * HB].rearrange("p (j t) -> p j t", j=32)

        def R_h(j, sbk):
            flat = j * HB + sbk * 512
            return R_O[flat // S][(flat % S) // 512]

        def w1_phase(th):
            for jp in range(16):
                ws = W2.next()
                w1p = ws.ap[:, 0:KC * 256].rearrange("p (k n) -> p k n", k=KC)
                load_w(ws, w1p, wsrc(w1_d[l], 0, D, jp * 256, (jp + 1) * 256))
                for jj in range(2):
                    j = jp * 2 + jj
                    for sbk in range(SBH):
                        tb = th * SBH + sbk
                        xr = [R_X[k][tb] for k in range(KC)]
                        ph = GEN.next()
                        mm(ph.ap[:, :], [(w1p[:, k, jj * 128:(jj + 1) * 128], xhi[:, k, tok(tb)]) for k in range(KC)], xr + [ws.res], ph.res)
                        r_ = TF.next()
                        V(lambda e, ph=ph, r_=r_: e.tensor_scalar(out=r_.ap, in0=ph.ap[:, :], scalar1=0.0, scalar2=None, op0=ALU.max), [ph.res], [r_.res])
                        A(lambda e, r_=r_, j=j, sbk=sbk: e.activation(out=h1[:, j, sbk * 512:(sbk + 1) * 512], in_=r_.ap, func=ACT.Square),
                          [r_.res], [R_h(j, sbk)])

        def w2_phase(th):
            for c2 in range(KC):
                wsa = W2.next()
                w2a = wsa.ap[:, 0:16 * 128].rearrange("p (k n) -> p k n", k=16)
                load_w(wsa, w2a, wsrc(w2_d[l], 0, 2048, c2 * 128, (c2 + 1) * 128))
                wsb2 = W2.next()
                w2b = wsb2.ap[:, 0:16 * 128].rearrange("p (k n) -> p k n", k=16)
                load_w(wsb2, w2b, wsrc(w2_d[l], 2048, 4096, c2 * 128, (c2 + 1) * 128))
                for sbk in range(SBH):
                    tb = th * SBH + sbk
                    pf = GEN.next()
                    mm(pf.ap[:, :], [((w2a if j < 16 else w2b)[:, j % 16, :], h1[:, j, sbk * 512:(sbk + 1) * 512]) for j in range(32)],
                       [R_h(j, sbk) for j in range(32)] + [wsa.res, wsb2.res], pf.res)
                    rz = R_z[sbk][c2]
                    V(lambda e, pf=pf, c2=c2, tb=tb, zs=zt[sbk]: e.scalar_tensor_tensor(out=zs[:, c2, :], in0=xhi[:, c2, tok(tb)], scalar=ALPHA,
                                                                                    in1=pf.ap[:, :], op0=ALU.mult, op1=ALU.add),
                      [pf.res, R_X[c2][tb]], [rz])
                    V(lambda e, c2=c2, tb=tb, zs=zt[sbk]: e.scalar_tensor_tensor(out=zs[:, c2, :], in0=xlo[:, c2, tok(tb)], scalar=ALPHA,
                                                                             in1=zs[:, c2, :], op0=ALU.mult, op1=ALU.add),
                      [R_X[c2][tb], rz], [rz])

        def ln2_phase(th):
            for sbk in range(SBH):
                layernorm(th * SBH + sbk, sbk, LN_EPS, l2g, l2b, last)

        w1_phase(0)
        for th in range(NH):
            w2_phase(th)
            if th + 1 < NH:
                w1_phase(th + 1)
            ln2_phase(th)

    R_y = []
    USE = {}
    for s in range(NSEQ):
        fw.barrier()
        if upto >= 1:
            load_sequence(s)
        fw.barrier()
        if upto >= 2:
            build_tables(s)
        if upto >= 3:
            for l in range(NLAYER):
                layer(s, l, last=(l == NLAYER - 1))
    if KDBG == 77:
        dx = nc.dram_tensor("dbg_xhi", [128, KC, S], BF16, kind="ExternalOutput").ap()
        do = nc.dram_tensor("dbg_oT", [128, 16, S], BF16, kind="ExternalOutput").ap()
        fw.barrier()
        r1_, r2_ = fw.res("d1"), fw.res("d2")
        fw.dma("sync", lambda e: e.dma_start(out=dx, in_=xhi[:]), "dbg1", reads=[R_X[c][t] for c in range(KC) for t in range(TB)], writes=[r1_])
        fw.dma("sync", lambda e: e.dma_start(out=do, in_=oT[:]), "dbg2", reads=[R_O[c][t] for c in range(16) for t in range(TB)], writes=[r2_])
        R_y = R_y + [r1_, r2_]
    fw.op("sync", lambda e: e.nop(), reads=R_y, writes=[])
    counts = fw.finish()
    counts["arena"] = AR
    counts.update(USE)
    counts["nops"] = len(fw.ops)
    return nc, counts


_CACHE = {}
WEIGHT_KEYS = ["w_in", "b_gate", "mla_q_norm", "mla_kv_norm", "mla_w_uq", "mla_w_ukv", "diff_lambda",
               "diff_subln", "mem_w_kv", "w_branch", "w_out", "ln1_g", "ln1_b", "mlp_w1", "mlp_w2", "ln2_g", "ln2_b"]


def kernel(**inputs):
    x = np.ascontiguousarray(np.asarray(inputs["x"], dtype=np.float32))
    mem = np.ascontiguousarray(np.asarray(inputs["mem"], dtype=np.float32))
    pos = np.ascontiguousarray(np.asarray(inputs["positions"], dtype=np.int32))
    B, S, _ = x.shape
    nseq = B // N_CORES
    key = (nseq, S, DEPTH)
    if key not in _CACHE:
        _CACHE[key] = build_program(nseq, S, DEPTH)[0]
    nc = _CACHE[key]
    consts = _host_consts()
    shared = {k: np.ascontiguousarray(np.asarray(inputs[k], dtype=np.float32)) for k in WEIGHT_KEYS}
    shared["c_ident"] = consts["ident"]
    shared["c_pvec"] = consts["pvec"]
    shared["c_rot"] = consts["rot"]
    in_maps = []
    for c in range(N_CORES):
        m = dict(shared)
        m["x"] = x[c * nseq:(c + 1) * nseq]
        m["mem"] = mem[c * nseq:(c + 1) * nseq]
        m["positions"] = pos[c * nseq:(c + 1) * nseq]
        in_maps.append(m)
    res = run_bass_kernel_spmd(nc, in_maps, core_ids=list(range(N_CORES)))
    out = np.concatenate([np.asarray(r["y"]) for r in res.results], axis=0)
    return out.astype(np.float32)
```
